# Optimizing a Trainium2 kernel written in Bass

```python
import jax, jax.numpy as jnp
from jax import lax
import numpy as np

D_MODEL = 2048
BATCH = 4
SEQ = 2048
DEPTH = 1
DEC_BATCH = 128
DEC_SEQ = 8
PAST_LEN = 16384
PAGE_SIZE = 128

D_MIX = D_MODEL
GDN_DK = 128
GDN_DV = 128
GDN_HEADS = (D_MIX // 2) // GDN_DV
GDN_QK = GDN_HEADS * GDN_DK
GDN_V = GDN_HEADS * GDN_DV
GDN_CONV_CH = 2 * GDN_QK + GDN_V
SSM_P = 64
SSM_N = 128
SSM_GROUPS = 2
SSM_DI = D_MIX - GDN_V
SSM_HEADS = SSM_DI // SSM_P
SSM_BC = SSM_GROUPS * SSM_N
SSM_CONV_CH = SSM_DI + 2 * SSM_BC
CONV_W = 4
CHUNK = 64
D_FF = -(-8 * D_MODEL // (3 * 256)) * 256
EPS = 1e-6
SPLITS = [GDN_CONV_CH, GDN_V, GDN_HEADS, GDN_HEADS, SSM_DI, SSM_CONV_CH, SSM_HEADS]
D_IN_PROJ = sum(SPLITS)

kernel_name = 'hybrid_gdn_ssd_parallel_heads_step'


def rmsnorm(x, w):
    x = x.astype(jnp.float32)
    return x * lax.rsqrt(jnp.mean(x * x, axis=-1, keepdims=True) + EPS) * w.astype(jnp.float32)


def l2norm(x):
    return x * lax.rsqrt(jnp.sum(x * x, axis=-1, keepdims=True) + EPS)


def causal_conv(x, buf, w, b):
    L = x.shape[1]
    xp = jnp.concatenate([buf.astype(jnp.float32), x], axis=1)
    out = sum(xp[:, i:i + L] * w[i].astype(jnp.float32) for i in range(CONV_W))
    if b is not None:
        out = out + b.astype(jnp.float32)
    return jax.nn.silu(out), xp[:, xp.shape[1] - (CONV_W - 1):]


def _chunk_len(L):
    return CHUNK if L >= CHUNK else L


def _chunks(t, c):
    B, L = t.shape[:2]
    n = -(-L // c)
    t = jnp.pad(t, [(0, 0), (0, n * c - L)] + [(0, 0)] * (t.ndim - 2))
    t = t.reshape((B, n, c) + t.shape[2:])
    return jnp.moveaxis(jnp.moveaxis(t, 3, 2), 1, 0)


def _unchunks(o, L):
    n, B, H, c, E = o.shape
    o = jnp.swapaxes(jnp.moveaxis(o, 0, 1), 2, 3).reshape(B, n * c, H, E)
    return o[:, :L]


def _decay(G, c):
    tril = jnp.tril(jnp.ones((c, c), bool))
    diff = G[..., :, None] - G[..., None, :]
    return jnp.exp(jnp.where(tril, diff, -jnp.inf))


def gated_delta_chunked(q, k, v, g, beta, S0):
    L = q.shape[1]
    dv = v.shape[-1]
    c = _chunk_len(L)
    qc, kc, vc, gc, bc = (_chunks(t, c) for t in (q, k, v, g, beta))
    G = jnp.cumsum(gc, axis=-1)
    decay = _decay(G, c)
    strict = jnp.tril(jnp.ones((c, c), bool), -1)
    kb = kc * bc[..., None]
    low = jnp.where(strict, jnp.einsum('nbhid,nbhjd->nbhij', kb, kc) * decay, 0.0)
    amat = low + jnp.eye(c, dtype=low.dtype)
    rhs = jnp.concatenate([vc * bc[..., None], kb * jnp.exp(G)[..., None]], axis=-1)
    sol = lax.linalg.triangular_solve(amat, rhs, left_side=True, lower=True, unit_diagonal=True)
    u, w = sol[..., :dv], sol[..., dv:]
    qk = jnp.einsum('nbhid,nbhjd->nbhij', qc, kc) * decay
    q_dec = qc * jnp.exp(G)[..., None]
    k_dec = kc * jnp.exp(G[..., -1:] - G)[..., None]
    g_last = jnp.exp(G[..., -1])

    def step(S, inp):
        u_i, w_i, qk_i, qd_i, kd_i, gl_i = inp
        v_new = u_i - jnp.einsum('bhcd,bhde->bhce', w_i, S)
        o = jnp.einsum('bhcd,bhde->bhce', qd_i, S) + jnp.einsum('bhij,bhje->bhie', qk_i, v_new)
        S = S * gl_i[..., None, None] + jnp.einsum('bhcd,bhce->bhde', kd_i, v_new)
        return S, o

    S, o = lax.scan(step, S0, (u, w, qk, q_dec, k_dec, g_last))
    return _unchunks(o, L), S


def ssd_chunked(x, dt, A, Bh, Ch, h0):
    L = x.shape[1]
    c = _chunk_len(L)
    xc, dtc, Bc, Cc = (_chunks(t, c) for t in (x, dt, Bh, Ch))
    Acum = jnp.cumsum(dtc * A[:, None], axis=-1)
    decay = _decay(Acum, c)
    xdt = xc * dtc[..., None]
    scores = jnp.einsum('nbhis,nbhjs->nbhij', Cc, Bc) * decay
    y_diag = jnp.einsum('nbhij,nbhjp->nbhip', scores, xdt)
    a_last = Acum[..., -1:]
    chunk_states = jnp.einsum('nbhjs,nbhjp->nbhps', Bc * jnp.exp(a_last - Acum)[..., None], xdt)
    c_dec = Cc * jnp.exp(Acum)[..., None]

    def step(h, inp):
        cd, st, al = inp
        y_off = jnp.einsum('bhis,bhps->bhip', cd, h)
        return h * al[..., None, None] + st, y_off

    h, y_off = lax.scan(step, h0, (c_dec, chunk_states, jnp.exp(a_last[..., 0])))
    return _unchunks(y_diag + y_off, L), h


def hybrid_layer(x, gdn_conv, gdn_S, ssm_conv, ssm_h, p):
    B, L, _ = x.shape
    h = rmsnorm(x, p['attn_norm_w'])
    proj = h @ p['w_in'].astype(jnp.float32)
    offs = np.cumsum(SPLITS)[:-1].tolist()
    qkv_raw, z_g, b_raw, a_raw, z_s, xbc_raw, dt_raw = jnp.split(proj, offs, axis=-1)

    qkv, gdn_conv_new = causal_conv(qkv_raw, gdn_conv, p['gdn_conv_w'], None)
    q, k, v = jnp.split(qkv, [GDN_QK, 2 * GDN_QK], axis=-1)
    q = l2norm(q.reshape(B, L, GDN_HEADS, GDN_DK)) * (GDN_DK ** -0.5)
    k = l2norm(k.reshape(B, L, GDN_HEADS, GDN_DK))
    v = v.reshape(B, L, GDN_HEADS, GDN_DV)
    beta = jax.nn.sigmoid(b_raw)
    g = -jnp.exp(p['gdn_A_log'].astype(jnp.float32)) * jax.nn.softplus(a_raw + p['gdn_dt_bias'].astype(jnp.float32))
    o, gdn_S_new = gated_delta_chunked(q, k, v, g, beta, gdn_S.astype(jnp.float32))
    o = rmsnorm(o, p['gdn_norm_w']) * jax.nn.silu(z_g.reshape(B, L, GDN_HEADS, GDN_DV))
    o = o.reshape(B, L, GDN_V)

    xbc, ssm_conv_new = causal_conv(xbc_raw, ssm_conv, p['ssm_conv_w'], p['ssm_conv_b'])
    xs, Bm, Cm = jnp.split(xbc, [SSM_DI, SSM_DI + SSM_BC], axis=-1)
    xs = xs.reshape(B, L, SSM_HEADS, SSM_P)
    rep = SSM_HEADS // SSM_GROUPS
    Bh = jnp.repeat(Bm.reshape(B, L, SSM_GROUPS, SSM_N), rep, axis=2)
    Ch = jnp.repeat(Cm.reshape(B, L, SSM_GROUPS, SSM_N), rep, axis=2)
    dt = jax.nn.softplus(dt_raw + p['ssm_dt_bias'].astype(jnp.float32))
    A = -jnp.exp(p['ssm_A_log'].astype(jnp.float32))
    y, ssm_h_new = ssd_chunked(xs, dt, A, Bh, Ch, ssm_h.astype(jnp.float32))
    y = y + p['ssm_D'].astype(jnp.float32)[:, None] * xs
    y = y.reshape(B, L, SSM_DI) * jax.nn.silu(z_s)
    gs = SSM_DI // SSM_GROUPS
    y = rmsnorm(y.reshape(B, L, SSM_GROUPS, gs), p['ssm_norm_w'].reshape(SSM_GROUPS, gs)).reshape(B, L, SSM_DI)

    mix = jnp.concatenate([o, y], axis=-1)
    x = x + mix @ p['w_out'].astype(jnp.float32)

    h2 = rmsnorm(x, p['ffn_norm_w'])
    ff = jax.nn.silu(h2 @ p['w_gate'].astype(jnp.float32)) * (h2 @ p['w_up'].astype(jnp.float32))
    x = x + ff @ p['w_down'].astype(jnp.float32)
    return x, (gdn_conv_new, gdn_S_new, ssm_conv_new, ssm_h_new)


def trunk(x, gdn_conv, gdn_S, ssm_conv, ssm_h, p, final_norm_w):
    h = x.astype(jnp.float32)
    outs = ([], [], [], [])
    for l in range(DEPTH):
        pl = {name: arr[l] for name, arr in p.items()}
        h, st = hybrid_layer(h, gdn_conv[l], gdn_S[l], ssm_conv[l], ssm_h[l], pl)
        for lst, s in zip(outs, st):
            lst.append(s.astype(x.dtype))
    y = rmsnorm(h, final_norm_w).astype(x.dtype)
    return y, [jnp.stack(lst, axis=0) for lst in outs]


def setup_inputs(seed: int = 0) -> dict:
    key = jax.random.key(seed)
    ks = jax.random.split(key, 32)
    f32 = jnp.float32
    nrm = lambda k, shape, s: jax.random.normal(k, shape, f32) * s

    def inv_softplus_dt(k, n):
        dt = jnp.exp(jax.random.uniform(k, (DEPTH, n), f32, float(np.log(1e-3)), float(np.log(1e-1))))
        return dt + jnp.log(-jnp.expm1(-dt))

    return {
        'x_prompt': nrm(ks[0], (BATCH, SEQ, D_MODEL), 1.0),
        'x_sample': nrm(ks[1], (DEC_BATCH, DEC_SEQ, D_MODEL), 1.0),
        'state_gdn_conv': nrm(ks[2], (DEPTH, DEC_BATCH, CONV_W - 1, GDN_CONV_CH), 1.0),
        'state_gdn': nrm(ks[3], (DEPTH, DEC_BATCH, GDN_HEADS, GDN_DK, GDN_DV), 0.05),
        'state_ssm_conv': nrm(ks[4], (DEPTH, DEC_BATCH, CONV_W - 1, SSM_CONV_CH), 1.0),
        'state_ssm': nrm(ks[5], (DEPTH, DEC_BATCH, SSM_HEADS, SSM_P, SSM_N), 0.1),
        'attn_norm_w': 1.0 + nrm(ks[6], (DEPTH, D_MODEL), 0.02),
        'w_in': nrm(ks[7], (DEPTH, D_MODEL, D_IN_PROJ), D_MODEL ** -0.5),
        'gdn_conv_w': nrm(ks[8], (DEPTH, CONV_W, GDN_CONV_CH), 0.5),
        'gdn_A_log': jnp.log(jax.random.uniform(ks[9], (DEPTH, GDN_HEADS), f32, 1.0, 16.0)),
        'gdn_dt_bias': inv_softplus_dt(ks[10], GDN_HEADS),
        'gdn_norm_w': 1.0 + nrm(ks[11], (DEPTH, GDN_DV), 0.02),
        'ssm_conv_w': nrm(ks[12], (DEPTH, CONV_W, SSM_CONV_CH), 0.5),
        'ssm_conv_b': nrm(ks[13], (DEPTH, SSM_CONV_CH), 0.02),
        'ssm_A_log': jnp.log(jax.random.uniform(ks[14], (DEPTH, SSM_HEADS), f32, 1.0, 16.0)),
        'ssm_dt_bias': inv_softplus_dt(ks[15], SSM_HEADS),
        'ssm_D': 1.0 + nrm(ks[16], (DEPTH, SSM_HEADS), 0.1),
        'ssm_norm_w': 1.0 + nrm(ks[17], (DEPTH, SSM_DI), 0.02),
        'w_out': nrm(ks[18], (DEPTH, D_MIX, D_MODEL), D_MIX ** -0.5),
        'ffn_norm_w': 1.0 + nrm(ks[19], (DEPTH, D_MODEL), 0.02),
        'w_gate': nrm(ks[20], (DEPTH, D_MODEL, D_FF), D_MODEL ** -0.5),
        'w_up': nrm(ks[21], (DEPTH, D_MODEL, D_FF), D_MODEL ** -0.5),
        'w_down': nrm(ks[22], (DEPTH, D_FF, D_MODEL), D_FF ** -0.5),
        'final_norm_w': 1.0 + nrm(ks[23], (D_MODEL,), 0.02),
    }


def reference(x_prompt, x_sample, state_gdn_conv, state_gdn, state_ssm_conv, state_ssm,
              attn_norm_w, w_in, gdn_conv_w, gdn_A_log, gdn_dt_bias, gdn_norm_w,
              ssm_conv_w, ssm_conv_b, ssm_A_log, ssm_dt_bias, ssm_D, ssm_norm_w,
              w_out, ffn_norm_w, w_gate, w_up, w_down, final_norm_w):
    p = dict(attn_norm_w=attn_norm_w, w_in=w_in, gdn_conv_w=gdn_conv_w, gdn_A_log=gdn_A_log,
             gdn_dt_bias=gdn_dt_bias, gdn_norm_w=gdn_norm_w, ssm_conv_w=ssm_conv_w,
             ssm_conv_b=ssm_conv_b, ssm_A_log=ssm_A_log, ssm_dt_bias=ssm_dt_bias, ssm_D=ssm_D,
             ssm_norm_w=ssm_norm_w, w_out=w_out, ffn_norm_w=ffn_norm_w, w_gate=w_gate,
             w_up=w_up, w_down=w_down)
    Bp = x_prompt.shape[0]
    dt_ = x_prompt.dtype
    z_gconv = jnp.zeros((DEPTH, Bp, CONV_W - 1, GDN_CONV_CH), dt_)
    z_gS = jnp.zeros((DEPTH, Bp, GDN_HEADS, GDN_DK, GDN_DV), dt_)
    z_sconv = jnp.zeros((DEPTH, Bp, CONV_W - 1, SSM_CONV_CH), dt_)
    z_sh = jnp.zeros((DEPTH, Bp, SSM_HEADS, SSM_P, SSM_N), dt_)
    y_prompt, st_p = trunk(x_prompt, z_gconv, z_gS, z_sconv, z_sh, p, final_norm_w)
    y_sample, st_s = trunk(x_sample, state_gdn_conv, state_gdn, state_ssm_conv, state_ssm, p, final_norm_w)
    return (y_prompt, y_sample, st_p[0], st_p[1], st_p[2], st_p[3], st_s[0], st_s[1], st_s[2], st_s[3])
```

```python
import numpy as np
import concourse.bass as bass
import concourse.mybir as mybir

F32 = mybir.dt.float32
BF16 = mybir.dt.bfloat16
ALU = mybir.AluOpType
AF = mybir.ActivationFunctionType
AX = mybir.AxisListType

_ES = {F32: 4, BF16: 2}


def _esize(dt):
    if dt in _ES:
        return _ES[dt]
    s = str(dt)
    if '32' in s:
        return 4
    if '16' in s:
        return 2
    if '64' in s:
        return 8
    return 1


def footprint(ap):
    name = ap.tensor.name
    dims = ap.ap
    es = _esize(ap.dtype)
    off = ap.offset
    space = str(ap.space)
    if 'DRAM' in space.upper() or 'HBM' in space.upper() or 'Dram' in space:
        ext = 1
        for st, cnt in dims:
            ext += (cnt - 1) * abs(st)
        return (name, True, 0, 1, off * es, (off + ext) * es)
    pst, pcnt = dims[0]
    if pst == 0:
        p0 = 0
        lo = off
        pcnt_eff = 1
    else:
        p0 = off // pst
        lo = off % pst
        pcnt_eff = pcnt
    ext = 1
    for st, cnt in dims[1:]:
        ext += (cnt - 1) * abs(st)
    lo_b, hi_b = lo * es, (lo + ext) * es
    if name.startswith('pw'):
        lo_b = (lo_b // 2048) * 2048
        hi_b = ((hi_b + 2047) // 2048) * 2048
        return (name, False, 0, 128, lo_b, hi_b)
    return (name, False, p0, p0 + pcnt_eff, lo_b, hi_b)


class Op:
    __slots__ = ('eng', 'fn', 'deps', 'dma', 'token', 'prewait', 'needed', 'idx')


class Prog:
    ENGS = ('pe', 'act', 'dve', 'pool', 'sp')
    NS = 8

    def __init__(self, nc):
        self.nc = nc
        self.ops = []
        self.acc = {}
        self.last_on_eng = {}

    def _track(self, idx, aps_r, aps_w):
        deps = set()
        for is_w, aps in ((False, aps_r), (True, aps_w)):
            for ap in aps:
                name, isd, p0, p1, lo, hi = footprint(ap)
                lst = self.acc.setdefault(name, [])
                keep = []
                for e in lst:
                    ov = not (e[2] <= p0 or e[1] >= p1 or e[4] <= lo or e[3] >= hi)
                    if ov and (is_w or e[5]):
                        if e[0] != idx:
                            deps.add(e[0])
                        if is_w and e[1] >= p0 and e[2] <= p1 and e[3] >= lo and e[4] <= hi:
                            continue
                    keep.append(e)
                keep.append([idx, p0, p1, lo, hi, is_w])
                self.acc[name] = keep
        return deps

    def add(self, eng, fn, r=(), w=(), dma=False, extra_deps=()):
        op = Op()
        op.eng = eng
        op.fn = fn
        op.dma = dma
        op.idx = len(self.ops)
        op.deps = self._track(op.idx, r, w)
        op.deps.update(extra_deps)
        op.token = None
        op.prewait = None
        op.needed = False
        self.ops.append(op)
        self.last_on_eng[eng] = op.idx
        return op.idx

    def fence(self):
        last = []
        for e in self.ENGS:
            pass
        idxs = set()
        seen_eng = set()
        for op in reversed(self.ops):
            if op.dma:
                idxs.add(op.idx)
            elif op.eng not in seen_eng:
                seen_eng.add(op.eng)
                idxs.add(op.idx)
        lf = getattr(self, '_last_fence', 0)
        idxs = {i for i in idxs if i >= lf or not self.ops[i].dma}
        for e in self.ENGS:
            self.add(e, None, extra_deps=set(idxs))
        self._last_fence = len(self.ops)

    def mm(self, out, lhsT, rhs, start=True, stop=True):
        return self.add('pe', lambda e: e.matmul(out, lhsT, rhs, start=start, stop=stop),
                        r=[lhsT, rhs], w=[out])

    def tr(self, out, in_, ident):
        return self.add('pe', lambda e: e.transpose(out, in_, ident), r=[in_, ident], w=[out])

    def actf(self, out, in_, func, bias=None, scale=None, accum=None, eng='act'):
        kw = {}
        r = [in_]
        w = [out]
        if bias is not None:
            kw['bias'] = bias
            if not isinstance(bias, (int, float)):
                r.append(bias)
        if scale is not None:
            kw['scale'] = scale
            if not isinstance(scale, (int, float)):
                r.append(scale)
        if accum is not None:
            kw['accum_out'] = accum
            w.append(accum)
        return self.add(eng, lambda e: e.activation(out, in_, func, **kw), r=r, w=w)

    def tt(self, out, a, b, op, eng='dve'):
        return self.add(eng, lambda e: e.tensor_tensor(out, a, b, op), r=[a, b], w=[out])

    def ts(self, out, a, s1, s2, op0, op1=None, eng='dve', accum=None):
        r = [a]
        if not isinstance(s1, (int, float)):
            r.append(s1)
        if s2 is not None and not isinstance(s2, (int, float)):
            r.append(s2)
        w = [out]
        kw = {}
        if accum is not None:
            kw['accum_out'] = accum
            w.append(accum)
        if op1 is None:
            if isinstance(s1, (int, float)):
                return self.add(eng, lambda e: e.tensor_scalar(out, a, s1, None, op0, **kw), r=r, w=w)
            return self.add(eng, lambda e: e.tensor_scalar(out, a, s1, 0.0, op0, ALU.add, **kw), r=r, w=w)
        return self.add(eng, lambda e: e.tensor_scalar(out, a, s1, s2, op0, op1, **kw), r=r, w=w)

    def stt(self, out, a, s, b, op0, op1, eng='dve'):
        r = [a, b]
        if not isinstance(s, (int, float)):
            r.append(s)
        return self.add(eng, lambda e: e.scalar_tensor_tensor(out, a, s, b, op0, op1), r=r, w=[out])

    def cp(self, out, in_, eng='dve'):
        if eng == 'act':
            return self.add('act', lambda e: e.copy(out, in_), r=[in_], w=[out])
        return self.add(eng, lambda e: e.tensor_copy(out, in_), r=[in_], w=[out])

    def red(self, out, in_, op=None, eng='dve'):
        op = op or ALU.add
        return self.add(eng, lambda e: e.tensor_reduce(out, in_, AX.X, op), r=[in_], w=[out])

    def memset(self, ap, val, eng='pool'):
        return self.add(eng, lambda e: e.memset(ap, val), w=[ap])

    def dma(self, out, in_, q='sp'):
        return self.add(q, lambda e: e.dma_start(out, in_), r=[in_], w=[out], dma=True)

    def emit(self, final_wait_ops=()):
        nc = self.nc
        ops = self.ops
        if final_wait_ops:
            self.add('sp', None, extra_deps=set(final_wait_ops))
        for op in ops:
            for d in op.deps:
                dop = ops[d]
                if dop.dma:
                    continue
                if dop.eng == 'pe' and op.eng == 'pe' and not op.dma:
                    continue
                dop.needed = True
        from contextlib import ExitStack
        es = ExitStack()
        sem = {e: es.enter_context(nc.semaphore('s_' + e)) for e in self.ENGS}
        dsem = {e: [es.enter_context(nc.semaphore('d_%s%d' % (e, i))) for i in range(self.NS)]
                for e in ('sp', 'act', 'pool')}
        cnt = {e: 0 for e in self.ENGS}
        dcnt = {e: 0 for e in dsem}
        for op in ops:
            if op.fn is None:
                continue
            if op.dma:
                m = dcnt[op.eng]
                s = dsem[op.eng][m % self.NS]
                op.token = (s, 16 * (m // self.NS + 1), 16)
                if m >= self.NS:
                    op.prewait = (s, 16 * (m // self.NS))
                dcnt[op.eng] = m + 1
            elif op.needed:
                cnt[op.eng] += 1
                op.token = (sem[op.eng], cnt[op.eng], 1)
        per_eng = {e: [op for op in ops if op.eng == e] for e in self.ENGS}
        stats = {e: [0, 0] for e in self.ENGS}

        def run(ename, eobj):
            known = {}
            for op in per_eng[ename]:
                waits = {}
                if op.prewait is not None:
                    s, v = op.prewait
                    waits[id(s)] = (s, v)
                for d in op.deps:
                    dop = ops[d]
                    if dop.token is None:
                        continue
                    if (not dop.dma) and dop.eng == 'pe' and ename == 'pe' and not op.dma:
                        continue
                    s, v, _ = dop.token
                    if id(s) not in waits or waits[id(s)][1] < v:
                        waits[id(s)] = (s, v)
                for k, (s, v) in waits.items():
                    if known.get(k, 0) >= v:
                        continue
                    eobj.wait_ge(s, v)
                    stats[ename][1] += 1
                    known[k] = v
                if op.fn is None:
                    continue
                ins = op.fn(eobj)
                stats[ename][0] += 1
                if op.token is not None:
                    ins.then_inc(op.token[0], op.token[2])

        with nc.Block() as block:
            @block.tensor
            def _(e):
                run('pe', e)

            @block.scalar
            def _(e):
                run('act', e)

            @block.vector
            def _(e):
                run('dve', e)

            @block.gpsimd
            def _(e):
                run('pool', e)

            @block.sync
            def _(e):
                run('sp', e)
        es.close()
        return stats

from contextlib import ExitStack
from concourse.bass_utils import run_bass_kernel_spmd

D = 2048
DIN = 6688
DFF = 5632
NT = 2176
NTILE = 17
NOUT = 1152
EPSV = 1e-6
PFW = 3 + NT
TMW = 2080


class Arena:
    def __init__(self, ar, total):
        self.ar = ar
        self.top = 0
        self.total = total

    def f32(self, n):
        o = self.top
        self.top += n
        assert self.top <= self.total, (self.top, self.total)
        return self.ar[:, o:o + n]

    def bf16(self, n):
        assert n % 2 == 0
        return self.f32(n // 2).bitcast(BF16)

    def at(self, o, n):
        return self.ar[:, o:o + n]


def v3(ap, a):
    return ap.rearrange("p (a b) -> p a b", a=a)


def build_program(debug=False):
    nc = bass.Bass("TRN2", target_bir_lowering=False)

    def din(name, shape):
        return nc.dram_tensor(name, list(shape), F32, kind="ExternalInput").ap()

    def dout(name, shape):
        return nc.dram_tensor(name, list(shape), F32, kind="ExternalOutput").ap()

    xin = din("xin", [NT, D])
    flag_d = din("flag", [128, 1])
    st_gconv = din("st_gconv", [48, 3072])
    st_g = din("st_g", [16, 8, 128, 128])
    st_sconv = din("st_sconv", [48, 1536])
    st_s = din("st_s", [16, 16, 64, 128])
    attn_norm_w = din("attn_norm_w", [16, 128])
    w_in = din("w_in", [D, DIN])
    gdn_conv_w = din("gdn_conv_w", [4, 3072])
    gdn_A_log = din("gdn_A_log", [1, 8])
    gdn_dt_bias = din("gdn_dt_bias", [1, 8])
    gdn_norm_w = din("gdn_norm_w", [1, 128])
    ssm_conv_w = din("ssm_conv_w", [4, 1536])
    ssm_conv_b = din("ssm_conv_b", [1, 1536])
    ssm_A_log = din("ssm_A_log", [1, 16])
    ssm_dt_bias = din("ssm_dt_bias", [1, 16])
    ssm_D = din("ssm_D", [1, 16])
    ssm_norm_w = din("ssm_norm_w", [1, 1024])
    w_out = din("w_out", [D, D])
    ffn_norm_w = din("ffn_norm_w", [16, 128])
    w_gate = din("w_gate", [D, DFF])
    w_up = din("w_up", [D, DFF])
    w_down = din("w_down", [DFF, D])
    final_norm_w = din("final_norm_w", [1, D])

    y_o = dout("y", [NOUT, D])
    gconv_p = dout("gconv_p", [3, 3072])
    gst_p = dout("gst_p", [8, 128, 128])
    sconv_p = dout("sconv_p", [3, 1536])
    sst_p = dout("sst_p", [16, 64, 128])
    gconv_s = dout("gconv_s", [48, 3072])
    gst_s = dout("gst_s", [16, 8, 128, 128])
    sconv_s = dout("sconv_s", [48, 1536])
    sst_s = dout("sst_s", [16, 16, 64, 128])

    P_fm = nc.dram_tensor("P_fm", [36, 128, PFW], F32).ap()
    P_tm = nc.dram_tensor("P_tm", [NT, TMW], F32).ap()
    P_cv = nc.dram_tensor("P_cv", [36, 128, 2048], F32).ap()

    es = ExitStack()
    TOT = 53000
    ar_t = es.enter_context(nc.sbuf_tensor("arena", [128, TOT], F32))
    PW = [es.enter_context(nc.psum_tensor("pw%d" % i, [128, 1024], F32)) for i in range(4)]
    A = Arena(ar_t, TOT)
    P = Prog(nc)
    pwc = [0]

    def pw():
        pwc[0] += 1
        return PW[pwc[0] % 3]
    pw_global = pw

    hb = [0]

    def phalf():
        hb[0] += 1
        k = hb[0] % 6
        return PW[k // 2][:, (k % 2) * 512:(k % 2) * 512 + 512]

    ID = A.f32(128)
    ONES = A.f32(128)
    MI = {}
    MS = {}
    UI = {}
    for ty in ('p', 's'):
        MI[ty] = A.f32(128)
        MS[ty] = A.f32(128)
        UI[ty] = A.f32(128)
    BD = A.f32(128)
    RM = A.f32(16)
    EPS = A.f32(1)
    FLAG = A.f32(1)
    ANW = A.f32(16)
    FNW = A.f32(16)
    FINW = A.f32(2048)
    GNW = A.f32(128)
    SNW = A.f32(1024)
    SD = A.f32(16)
    NAG = A.f32(8)
    GDB = A.f32(8)
    NAS = A.f32(16)
    SDB = A.f32(16)
    CW = A.f32(36 * 5)
    CW3 = v3(CW, 36)
    ZERO = A.f32(128)
    LNDK = A.f32(1)

    def asel(ap, pattern, op, fill, base, cm):
        P.add('pool', lambda e: e.affine_select(ap, ap, pattern, op, fill, base=base, channel_multiplier=cm),
              r=[ap], w=[ap])

    P.memset(ID, 0.0)
    asel(ID, [[-1, 128]], ALU.not_equal, 1.0, 0, 1)
    P.memset(ONES, 1.0)
    P.memset(ZERO, 0.0)
    P.memset(EPS, EPSV)
    P.memset(LNDK, float(np.log(128 ** -0.5)))
    P.memset(BD, 1.0)
    asel(v3(BD, 16), [[-8, 16], [0, 8]], ALU.is_ge, 0.0, 0, 1)
    asel(v3(BD, 16), [[8, 16], [0, 8]], ALU.is_ge, 0.0, 7, -1)
    P.memset(RM, 1.0)
    asel(RM, [[-8, 16]], ALU.is_ge, 0.0, 0, 1)
    asel(RM, [[8, 16]], ALU.is_ge, 0.0, 7, -1)
    for ty in ('p', 's'):
        P.memset(MI[ty], 1.0)
        asel(MI[ty], [[-1, 128]], ALU.is_ge, 0.0, 0, 1)
        P.memset(MS[ty], 1.0)
        asel(MS[ty], [[-1, 128]], ALU.is_gt, 0.0, 0, 1)
        P.memset(UI[ty], 1.0)
        asel(UI[ty], [[1, 128]], ALU.is_ge, 0.0, 0, -1)
        if ty == 's':
            for m in (MI, MS, UI):
                P.tt(m[ty], m[ty], BD, ALU.mult, eng='pool')
    P.dma(FLAG, flag_d)
    P.dma(FINW, final_norm_w.broadcast_to([128, D]))
    P.dma(GNW, gdn_norm_w.broadcast_to([128, 128]))
    P.dma(SNW, ssm_norm_w.broadcast_to([128, 1024]))
    P.dma(SD, ssm_D.broadcast_to([128, 16]))
    P.dma(NAG, gdn_A_log.broadcast_to([128, 8]))
    P.dma(GDB, gdn_dt_bias.broadcast_to([128, 8]))
    P.dma(NAS, ssm_A_log.broadcast_to([128, 16]))
    P.dma(SDB, ssm_dt_bias.broadcast_to([128, 16]))
    P.actf(NAG, NAG, AF.Exp)
    P.ts(NAG, NAG, -1.0, None, ALU.mult)
    P.actf(NAS, NAS, AF.Exp)
    P.ts(NAS, NAS, -1.0, None, ALU.mult)

    mark0 = A.top
    TMPA = A.f32(4608 + 256)
    cwst = TMPA[0:5, 0:4608]
    P.memset(TMPA[0:5, 0:4608], 0.0)
    P.dma(TMPA[0:4, 0:3072], gdn_conv_w)
    P.dma(TMPA[0:4, 3072:4608], ssm_conv_w)
    P.dma(TMPA[4:5, 3072:4608], ssm_conv_b)
    P.dma(TMPA[0:16, 4608:4736], attn_norm_w)
    P.dma(TMPA[0:16, 4736:4864], ffn_norm_w)
    ps = pw()
    for b in range(36):
        P.tr(ps[:, b * 5:b * 5 + 5], TMPA[0:5, b * 128:(b + 1) * 128], ID[0:5, 0:5])
    P.cp(CW, ps[:, 0:180], eng='act')
    ps = pw()
    P.tr(ps[:, 0:16], TMPA[0:16, 4608:4736], ID[0:16, 0:16])
    P.tr(ps[:, 16:32], TMPA[0:16, 4736:4864], ID[0:16, 0:16])
    P.cp(ANW, ps[:, 0:16], eng='act')
    P.cp(FNW, ps[:, 16:32], eng='act')
    P.fence()
    A.top = mark0
    persist_top = A.top

    def rms_to_hT(xt, nw, hT_dst, tmp_sq, xn, ss, rs):
        P.actf(tmp_sq, xt, AF.Square, accum=ss)
        P.actf(rs, ss, AF.Sqrt, scale=1.0 / D, bias=EPS)
        P.add('dve', lambda e: e.reciprocal(rs, rs), r=[rs], w=[rs])
        P.ts(xn, xt, rs, None, ALU.mult)
        for kq in range(4):
            ph = phalf()
            for j in range(4):
                kc = kq * 4 + j
                P.tr(ph[:, j * 128:(j + 1) * 128], xn[:, kc * 128:(kc + 1) * 128], ID)
            P.tt(hT_dst[:, kq * 4:kq * 4 + 4, :], v3(ph, 4),
                 nw[:, kq * 4:kq * 4 + 4].unsqueeze(2).to_broadcast([128, 4, 128]), ALU.mult)

    hT = A.bf16(16 * NT)
    hT3 = v3(hT, 16)
    XS = [A.f32(2048) for _ in range(2)]
    XN = A.f32(2048)
    SQJ = A.bf16(2048)
    SS = A.f32(1)
    RS = A.f32(1)
    for t in range(NTILE):
        xt = XS[t % 2]
        P.dma(xt, xin[t * 128:(t + 1) * 128, :])
        rms_to_hT(xt, ANW, hT3[:, :, t * 128:(t + 1) * 128], SQJ, XN, SS, RS)

    Z3 = A.f32(36 * 3)
    P.memset(Z3, 0.0)
    P.dma(P_fm.rearrange("b p t -> p b t")[:, :, 0:3], v3(Z3, 36))

    WB = [A.bf16(16 * 128) for _ in range(3)]
    STG = [A.f32(3 + NT) for _ in range(2)]
    CVS = [A.f32(2048) for _ in range(2)]
    for st_ in STG:
        P.memset(st_[:, 0:3], 0.0)
    w_in_v = w_in.rearrange("(kc p) c -> p kc c", p=128)
    chunks = [(0, 512), (512, 512), (1024, 512), (1536, 512), (2048, 128)]
    ev = [0]

    def evac(dst, src):
        ev[0] += 1
        P.cp(dst, src, eng='act' if ev[0] % 2 else 'dve')

    for cb in range(36):
        c0 = cb * 128 if cb < 24 else 5136 + (cb - 24) * 128
        W = WB[cb % 3]
        W3 = v3(W, 16)
        P.dma(W3, w_in_v[:, :, c0:c0 + 128], q='pool')
        stg = STG[cb % 2]
        cks = chunks if cb >= 8 else [(896, 128), (1024, 512), (1536, 512), (2048, 128)]
        for (t0, n) in cks:
            ph = phalf()
            for kc in range(16):
                P.mm(ph[:, 0:n], W3[:, kc, :], hT3[:, kc, t0:t0 + n], start=(kc == 0), stop=(kc == 15))
            evac(stg[:, 3 + t0:3 + t0 + n], ph[:, 0:n])
        tb = cks[0][0]
        P.dma(P_fm[cb, :, 3 + tb:3 + NT], stg[:, 3 + tb:3 + NT])
        tlo = 0 if cb >= 8 else 1024
        cvs = CVS[cb % 2]
        P.actf(cvs[:, tlo:2048], stg[:, tlo:2048], AF.Identity, scale=CW3[:, cb, 0:1], bias=CW3[:, cb, 4:5])
        for i in range(1, 4):
            P.stt(cvs[:, tlo:2048], stg[:, tlo + i:2048 + i], CW3[:, cb, i:i + 1], cvs[:, tlo:2048], ALU.mult, ALU.add)
        P.actf(cvs[:, tlo:2048], cvs[:, tlo:2048], AF.Silu)
        P.dma(P_cv[cb, :, tlo:2048], cvs[:, tlo:2048])

    WT = [A.bf16(16 * 512) for _ in range(2)]
    TS_ = [A.f32(512) for _ in range(2)]
    tmchunks = [(3072, 512, 0, 8), (3584, 512, 512, 8), (4112, 512, 1024, 8), (4624, 512, 1536, 8),
                (4096, 16, 2048, 0), (6672, 16, 2064, 0)]
    k = 0
    for i, (c0, ncol, pc0, t_first) in enumerate(tmchunks):
        W = WT[i % 2]
        W3 = v3(W, 16)
        P.dma(W3[:, :, 0:ncol], w_in_v[:, :, c0:c0 + ncol], q='pool')
        for t in range(t_first, NTILE):
            ph = phalf()
            for kc in range(16):
                P.mm(ph[:, 0:ncol], hT3[:, kc, t * 128:(t + 1) * 128], W3[:, kc, 0:ncol],
                     start=(kc == 0), stop=(kc == 15))
            ts_ = TS_[k % 2]
            k += 1
            evac(ts_[:, 0:ncol], ph[:, 0:ncol])
            P.dma(P_tm[t * 128:(t + 1) * 128, pc0:pc0 + ncol], ts_[:, 0:ncol])
    P.fence()

    A.top = persist_top
    MIXT = A.bf16(16 * NOUT)
    MIXT3 = v3(MIXT, 16)
    SG = A.f32(1024)
    HT = A.f32(1024)
    P.memset(SG, 0.0)
    P.memset(HT, 0.0)
    RAW = A.f32(36 * 176)
    CV = A.f32(36 * 128)
    RAWC = CV
    CV3 = v3(CV, 36)
    TM = A.f32(TMW)
    NB = 22
    bf0 = A.top
    Bf = [A.f32(1024) for _ in range(NB)]
    HST = A.at(bf0 + 17 * 1024, 4608)
    CST = HST
    MIX = A.at(bf0 + 13 * 1024, 2048)
    SMALL = A.f32(1024)
    SGS = [Bf[4], Bf[6]]
    SGO = [Bf[12], Bf[10]]
    out_dmas = []

    def sm(o, n):
        return SMALL[:, o:o + n]

    BETA = sm(0, 8)
    BETAN = sm(8, 8)
    G8 = sm(16, 8)
    TMP8 = sm(24, 8)
    DT16 = sm(32, 16)
    A16 = sm(48, 16)
    TMP16 = sm(64, 16)
    EGK = sm(80, 16)
    EG = sm(80, 8)
    EKD = sm(88, 8)
    BEG = sm(96, 8)
    RQ = sm(104, 8)
    RK = sm(112, 8)
    SSQ = sm(120, 8)
    EAA = sm(128, 32)
    EA = sm(128, 16)
    EAL = sm(144, 16)
    SSO = sm(160, 8)
    SS2 = sm(168, 2)
    EGL = sm(176, 128)
    EALB = sm(304, 256)
    RHE = sm(560, 256)
    SSK = sm(816, 8)

    def bh(ap, n=8, w=128):
        return ap.unsqueeze(2).to_broadcast([128, n, w])

    def bm(ap, n=8, w=128):
        return ap.unsqueeze(1).to_broadcast([128, n, w])

    DK_SCALE = 128 ** -0.5

    for t in range(NTILE):
        ty = 's' if t == 16 else 'p'
        nseq = 16 if ty == 's' else 1
        L = 8 if ty == 's' else 128
        emit_out = t >= 8
        ot = t - 8
        CVt = CV if t % 2 == 0 else RAW[:, 0:36 * 128]
        CV3 = v3(CVt, 36)
        if ty == 'p':
            b_lo = 0 if emit_out else 8
            P.dma(CV3[:, b_lo:36, :], P_cv[b_lo:36, :, t * 128:(t + 1) * 128].rearrange("b p t -> p b t"))
            if t == 15:
                RAW3 = v3(sm(840, 108), 36)
                P.dma(RAW3, P_fm[:, :, 3 + 2045:3 + 2048].rearrange("b p t -> p b t"))
        else:
            RAW4 = RAW.rearrange("p (b s t) -> p b s t", b=36, s=16)
            P.dma(v3(RAWC, 36), P_fm[:, :, 3 + 2048:3 + NT].rearrange("b p t -> p b t"))
            P.dma(HST[0:48, 0:3072], st_gconv)
            P.dma(HST[0:48, 3072:4608], st_sconv)
            for b0 in range(0, 36, 8):
                nb = min(8, 36 - b0)
                ps = pw()
                for b in range(nb):
                    P.tr(ps[:, b * 48:(b + 1) * 48], HST[0:48, (b0 + b) * 128:(b0 + b + 1) * 128], ID[0:48, 0:48])
                P.cp(RAW4[:, b0:b0 + nb, :, 0:3],
                     ps[:, 0:nb * 48].rearrange("p (b s t) -> p b s t", b=nb, s=16), eng='act')
            P.cp(RAW4[:, :, :, 3:11], RAWC.rearrange("p (b s t) -> p b s t", b=36, s=16), eng='pool')
        if emit_out:
            P.dma(TM, P_tm[t * 128:(t + 1) * 128, :])
        else:
            P.dma(TM[:, 2048:TMW], P_tm[t * 128:(t + 1) * 128, 2048:TMW])
        cblocks = list(range(36)) if ty == 's' else []
        NPOOL = 8
        pool_blocks = cblocks[-NPOOL:]
        dve_blocks = cblocks[:-NPOOL]
        CTMP = v3(Bf[21], 8)
        if ty == 's':
            NPOOL = 8

        def cio(b):
            if ty == 'p':
                return CV3[:, b, :], [RAW3[:, b, i:i + 128] for i in range(4)], (lambda a: a)
            return (CV3[:, b, :].rearrange("p (s t) -> p s t", s=16), [RAW4[:, b, :, i:i + 8] for i in range(4)],
                    (lambda a: a.rearrange("p (s t) -> p s t", s=16)))
        for b in cblocks:
            o_, ins, _ = cio(b)
            P.actf(o_, ins[0], AF.Identity, scale=CW3[:, b, 0:1], bias=CW3[:, b, 4:5])
        for i in range(1, 4):
            for b in dve_blocks:
                o_, ins, _ = cio(b)
                P.stt(o_, ins[i], CW3[:, b, i:i + 1], o_, ALU.mult, ALU.add)
            for j, b in enumerate(pool_blocks):
                o_, ins, vw = cio(b)
                P.ts(vw(CTMP[:, j, :]), ins[i], CW3[:, b, i:i + 1], None, ALU.mult, eng='pool')
            for j, b in enumerate(pool_blocks):
                o_, ins, vw = cio(b)
                P.tt(o_, o_, vw(CTMP[:, j, :]), ALU.add, eng='pool')
        if ty == 's':
            P.actf(CV, CV, AF.Silu)
        if emit_out:
            P.actf(TM[:, 0:2048], TM[:, 0:2048], AF.Silu)
        if t == 15 or ty == 's':
            nr = 3 if ty == 'p' else 48
            if ty == 's':
                LST = A.at(bf0 + 15 * 1024, 36 * 48)
                P.cp(LST.rearrange("p (b s t) -> p b s t", b=36, s=16), RAW4[:, :, :, 8:11], eng='pool')
                LST3 = v3(LST, 36)
            for b0 in range(0, 36, 4):
                ph = phalf()
                for b in range(4):
                    src = RAW3[:, b0 + b, 0:3] if ty == 'p' else LST3[:, b0 + b, :]
                    P.tr(ph[0:nr, b * 128:(b + 1) * 128], src, ID)
                P.cp(CST[0:nr, b0 * 128:(b0 + 4) * 128], ph[0:nr, 0:512], eng='act')
            if ty == 'p':
                out_dmas.append(P.dma(gconv_p, CST[0:3, 0:3072]))
                out_dmas.append(P.dma(sconv_p, CST[0:3, 3072:4608]))
            else:
                out_dmas.append(P.dma(gconv_s, CST[0:48, 0:3072]))
                out_dmas.append(P.dma(sconv_s, CST[0:48, 3072:4608]))
        P.actf(BETA, TM[:, 2048:2056], AF.Exp, scale=-1.0)
        P.ts(BETA, BETA, 1.0, None, ALU.add)
        P.add('dve', lambda e: e.reciprocal(BETA, BETA), r=[BETA], w=[BETA])
        P.ts(BETAN, BETA, -1.0, None, ALU.mult)
        P.tt(TMP8, TM[:, 2056:2064], GDB, ALU.add)
        P.actf(TMP8, TMP8, AF.Exp)
        P.actf(TMP8, TMP8, AF.Ln, bias=1.0)
        P.tt(G8, TMP8, NAG, ALU.mult)
        P.tt(TMP16, TM[:, 2064:2080], SDB, ALU.add)
        P.actf(TMP16, TMP16, AF.Exp)
        P.actf(DT16, TMP16, AF.Ln, bias=1.0)
        P.tt(A16, DT16, NAS, ALU.mult)
        Q, SQ, KN, V, RH, DX, DM, QNT, KNT, QDT, NT0, QKM, N0, QKMT, R, NA, NB_, UTb, WTb, KDEC, VNb, OTb = Bf
        QN = Q

        def cgs(g):
            return slice(g * 512, (g + 1) * 512)

        def G(buf, g):
            return buf[:, g * 512:(g + 1) * 512]

        def G4(buf, g):
            return v3(buf[:, g * 512:(g + 1) * 512], 4)

        def h4(ap8, g):
            return bh(ap8[:, 4 * g:4 * g + 4], 4)

        def m4(ap):
            return bm(ap, 4)

        def lock(stages):
            for st in stages:
                for g in (0, 1):
                    st(g)

        pss = phalf()
        P.mm(pss[:, 0:8], UI[ty], G8)
        P.mm(pss[:, 8:16], MS[ty], G8)
        P.actf(EGK, pss[:, 0:16], AF.Exp)
        if ty == 'p':
            P.mm(pss[:, 16:24], ONES, G8)
        else:
            P.tt(RHE[:, 0:128].rearrange("p (s h) -> p s h", s=16),
                 G8.unsqueeze(1).to_broadcast([128, 16, 8]), RM.unsqueeze(2).to_broadcast([128, 16, 8]), ALU.mult)
            P.mm(pss[:, 16:16 + 128], ONES, RHE[:, 0:128])
        P.actf(EGL[:, 0:nseq * 8], pss[:, 16:16 + nseq * 8], AF.Exp)
        EGL3 = EGL[:, 0:nseq * 8].rearrange("p (s h) -> p s h", s=nseq)
        P.tt(BEG, BETA, EG, ALU.mult)

        def tr4(dst, srcs):
            ph = phalf()
            for hh in range(4):
                P.tr(ph[:, hh * 128:(hh + 1) * 128], srcs[hh], ID)
            P.cp(dst, ph, eng='act')

        def s_qT(g):
            if emit_out:
                tr4(G(Q, g), [CV3[:, 4 * g + hh, :] for hh in range(4)])

        def s_kT(g):
            tr4(G(KN, g), [CV3[:, 8 + 4 * g + hh, :] for hh in range(4)])

        def s_vT(g):
            tr4(G(V, g), [CV3[:, 16 + 4 * g + hh, :] for hh in range(4)])

        def norm_(X_, R_, S_, sc, g):
            P.tt(G(SQ, g), G(X_, g), G(X_, g), ALU.mult)
            P.red(S_[:, 4 * g:4 * g + 4], G4(SQ, g))
            r_ = R_[:, 4 * g:4 * g + 4]
            P.actf(r_, S_[:, 4 * g:4 * g + 4], AF.Ln, bias=EPS)
            P.actf(r_, r_, AF.Exp, scale=-0.5, bias=(LNDK if sc != 1.0 else None))
            P.tt(G4(X_, g), G4(X_, g), h4(R_, g), ALU.mult)

        def s_qn(g):
            if emit_out:
                norm_(Q, RQ, SSQ, DK_SCALE, g)

        def s_kn(g):
            norm_(KN, RK, SSK, 1.0, g)

        def s_decay(g):
            P.tt(G4(RH, g), m4(MS[ty]), h4(G8, g), ALU.mult)
            ph = phalf()
            P.mm(ph, UI[ty], G(RH, g))
            P.actf(G(DX, g), ph, AF.Exp)
            if emit_out:
                P.tt(G4(DM, g), G4(DX, g), m4(MI[ty]), ALU.mult, eng='pool')
            P.tt(G4(DX, g), G4(DX, g), m4(MS[ty]), ALU.mult)
            P.tt(G4(DX, g), G4(DX, g), h4(BETAN, g), ALU.mult)

        def s_qnT(g):
            if emit_out:
                tr4(G(QNT, g), [QN[:, (4 * g + hh) * 128:(4 * g + hh + 1) * 128] for hh in range(4)])

        def s_knT(g):
            tr4(G(KNT, g), [KN[:, (4 * g + hh) * 128:(4 * g + hh + 1) * 128] for hh in range(4)])

        def s_qdT(g):
            if emit_out:
                P.tt(G4(SQ, g), G4(QN, g), h4(EG, g), ALU.mult)
                tr4(G(QDT, g), [SQ[:, (4 * g + hh) * 128:(4 * g + hh + 1) * 128] for hh in range(4)])

        def mm4(lh, rh, g):
            ph = phalf()
            for hh in range(4):
                sl = slice((4 * g + hh) * 128, (4 * g + hh + 1) * 128)
                P.mm(ph[:, hh * 128:(hh + 1) * 128], lh[:, sl], rh[:, sl])
            return ph

        def s_gram(g):
            ph = mm4(KNT, KNT, g)
            P.tt(G(NT0, g), ph, G(DX, g), ALU.mult)

        def s_qk(g):
            if emit_out:
                ph = mm4(QNT, KNT, g)
                P.tt(G(QKM, g), ph, G(DM, g), ALU.mult)

        def s_n0(g):
            tr4(G(N0, g), [NT0[:, (4 * g + hh) * 128:(4 * g + hh + 1) * 128] for hh in range(4)])
            P.tt(G4(R, g), G4(N0, g), m4(ID), ALU.add)

        def s_qkT(g):
            if emit_out:
                tr4(G(QKMT, g), [QKM[:, (4 * g + hh) * 128:(4 * g + hh + 1) * 128] for hh in range(4)])

        lock([s_kT, s_vT, s_qT, s_kn, s_decay, s_qn, s_knT, s_gram, s_qnT, s_n0, s_qdT, s_qk, s_qkT])

        nsteps = 6 if ty == 'p' else 2
        Nc, NTc = N0, NT0
        pp = [(NA, NB_), (RH, DM)]
        for s_ in range(1, nsteps + 1):
            Nn, NTn = pp[s_ % 2]

            def d_sq(g, Nc=Nc, NTc=NTc, Nn=Nn, NTn=NTn, s_=s_):
                psb = mm4(Nc, NTc, g)
                P.cp(G(NTn, g), psb, eng='act')
                if s_ < nsteps:
                    psa = mm4(NTc, Nc, g)
                    P.cp(G(Nn, g), psa, eng='dve')

            def d_r(g, NTn=NTn):
                psc = mm4(NTn, R, g)
                P.tt(G(R, g), G(R, g), psc, ALU.add)

            lock([d_sq, d_r])
            Nc, NTc = Nn, NTn

        VBb = NA
        KBG = NB_
        VZ = UTb

        def s_prep(g):
            P.tt(G4(VBb, g), G4(V, g), h4(BETA, g), ALU.mult, eng='pool')
            P.tt(G4(KBG, g), G4(KN, g), h4(BEG, g), ALU.mult, eng='pool')
            P.tt(G4(KDEC, g), G4(KN, g), h4(EKD, g), ALU.mult, eng='pool')

        def s_u(g):
            ph = mm4(VBb, R, g)
            P.cp(G(UTb, g), ph, eng='act')

        def s_w(g):
            ph = mm4(KBG, R, g)
            P.cp(G(WTb, g), ph, eng='act')

        VNT = SQ

        def load_S(g, s, bufs):
            S_s = bufs[s % len(bufs)]
            P.dma(G4(S_s, g), st_g[s, 4 * g:4 * g + 4].rearrange("h k v -> k h v"), q='pool')
            return S_s

        def s_scanA(g):
            psA = phalf()
            psB = phalf() if emit_out else None
            for s in range(nseq):
                S_s = SG if ty == 'p' else load_S(g, s, [RH, DM, N0, NT0])
                for hh in range(4):
                    h = 4 * g + hh
                    P.mm(psA[:, hh * 128 + s * L:hh * 128 + (s + 1) * L], S_s[:, h * 128:(h + 1) * 128],
                         WTb[:, h * 128 + s * L:h * 128 + (s + 1) * L])
                if emit_out:
                    for hh in range(4):
                        h = 4 * g + hh
                        P.mm(psB[:, hh * 128 + s * L:hh * 128 + (s + 1) * L], S_s[:, h * 128:(h + 1) * 128],
                             QDT[:, h * 128 + s * L:h * 128 + (s + 1) * L])
            P.tt(G(VNT, g), G(UTb, g), psA, ALU.subtract)
            if emit_out:
                P.cp(G(OTb, g), psB, eng='act')

        def s_vn(g):
            tr4(G(VNb, g), [VNT[:, (4 * g + hh) * 128:(4 * g + hh + 1) * 128] for hh in range(4)])

        def s_c(g):
            if emit_out:
                psC = mm4(VNb, QKMT, g)
                P.tt(G(OTb, g), G(OTb, g), psC, ALU.add)

        def s_upd(g):
            for s in range(nseq):
                if ty == 'p':
                    S_s, VZs, Sn = SG, VNb, SG
                else:
                    S_s = load_S(g, s, [RH, DM, QNT, KNT])
                    VZs = [UTb, WTb][s % 2]
                    P.ts(G(VZs, g), G(VNb, g), RM[:, s:s + 1], None, ALU.mult)
                    Sn = [N0, NT0, QDT, QKM][s % 4]
                psn = mm4(KDEC, VZs, g)
                P.tt(G4(Sn, g), G4(S_s, g), bh(EGL3[:, s, 4 * g:4 * g + 4], 4), ALU.mult)
                P.tt(G(Sn, g), G(Sn, g), psn, ALU.add)
                if ty == 's':
                    out_dmas.append(P.dma(gst_s[s, 4 * g:4 * g + 4].rearrange("h k v -> k h v"), G4(Sn, g)))
            if t == 7:
                P.ts(G(SG, g), G(SG, g), FLAG, None, ALU.mult)
            if t == 15:
                out_dmas.append(P.dma(gst_p[4 * g:4 * g + 4].rearrange("h k v -> k h v"), G4(SG, g)))

        O = Q

        def s_out(g):
            if not emit_out:
                return
            ph = phalf()
            for hh in range(4):
                h = 4 * g + hh
                P.tr(ph[:, hh * 128:(hh + 1) * 128], OTb[:, h * 128:(h + 1) * 128], ID)
            P.cp(G(O, g), ph, eng='act')
            P.tt(G(SQ, g), G(O, g), G(O, g), ALU.mult)
            so = SSO[:, 4 * g:4 * g + 4]
            P.red(so, G4(SQ, g))
            P.actf(so, so, AF.Ln, scale=1.0 / 128, bias=EPS)
            P.actf(so, so, AF.Exp, scale=-0.5)
            P.tt(G4(O, g), G4(O, g), h4(SSO, g), ALU.mult)
            P.tt(G4(O, g), G4(O, g), m4(GNW), ALU.mult, eng='pool')
            P.tt(MIX[:, g * 512:(g + 1) * 512], G(O, g), TM[:, g * 512:(g + 1) * 512], ALU.mult, eng='pool')

        lock([s_prep, s_u, s_w, s_scanA, s_vn, s_c, s_upd, s_out])

        XSb, XDT, XW, DXT0, DXT1, Yb, T1a, YOT, HTS, HTN, XZ = Bf[0:11]
        BTM = Bf[11][:, 0:256]
        CBM = Bf[11][:, 256:512]
        HNAT = Bf[12]
        T1b = Bf[15]
        T1 = [T1a, T1b]
        DXT = [DXT0, DXT1]
        PSO = [PW[3][:, 0:512], PW[3][:, 512:1024]]

        def G8h(buf, g):
            return v3(buf[:, g * 512:(g + 1) * 512], 8)

        def h8(ap16, g):
            return bh(ap16[:, 8 * g:8 * g + 8], 8, 64)

        psb_ = phalf()
        for g in range(2):
            P.tr(psb_[:, g * 128:(g + 1) * 128], CV3[:, 32 + g, :], ID)
        P.cp(BTM, psb_[:, 0:256], eng='act')
        pss = phalf()
        P.mm(pss[:, 0:16], UI[ty], A16)
        P.mm(pss[:, 16:32], MS[ty], A16)
        P.actf(EAA, pss[:, 0:32], AF.Exp)
        if ty == 'p':
            P.mm(pss[:, 32:48], ONES, A16)
        else:
            P.tt(RHE.rearrange("p (s h) -> p s h", s=16),
                 A16.unsqueeze(1).to_broadcast([128, 16, 16]), RM.unsqueeze(2).to_broadcast([128, 16, 16]), ALU.mult)
            P.mm(pss[:, 32:32 + 256], ONES, RHE)
        P.actf(EALB[:, 0:nseq * 16], pss[:, 32:32 + nseq * 16], AF.Exp)
        EALB3 = EALB[:, 0:nseq * 16].rearrange("p (s h) -> p s h", s=nseq)

        def t_xs(g):
            tr4(G(XSb, g), [CV3[:, 24 + 4 * g + bb, :] for bb in range(4)])

        def t_decay(g):
            if not emit_out:
                return
            P.tt(v3(T1[g], 8), bm(UI[ty]), bh(A16[:, g * 8:(g + 1) * 8]), ALU.mult)
            for hf in range(2):
                ph = phalf()
                P.mm(ph, MS[ty], T1[g][:, hf * 512:(hf + 1) * 512])
                P.actf(DXT[g][:, hf * 512:(hf + 1) * 512], ph, AF.Exp)

        def t_cb(g):
            if not emit_out:
                return
            ph = phalf()
            P.mm(ph[:, 0:128], CV3[:, 32 + g, :], CV3[:, 34 + g, :])
            cbm = CBM[:, g * 128:(g + 1) * 128]
            P.tt(cbm, ph[:, 0:128], UI[ty], ALU.mult)
            for hf in range(2):
                d_ = v3(DXT[g][:, hf * 512:(hf + 1) * 512], 4)
                P.tt(d_, d_, bm(cbm, 4), ALU.mult)

        def t_xdt(g):
            P.tt(G8h(XDT, g), G8h(XSb, g), h8(DT16, g), ALU.mult, eng='pool')
            P.tt(G8h(XW, g), G8h(XDT, g), h8(EAL, g), ALU.mult, eng='pool')

        def t_ydiag(g):
            if not emit_out:
                return
            ph = phalf()
            for hh in range(8):
                h = 8 * g + hh
                P.mm(ph[:, hh * 64:(hh + 1) * 64], DXT[g][:, hh * 128:(hh + 1) * 128], XDT[:, h * 64:(h + 1) * 64])
            P.cp(G(Yb, g), ph, eng='act')

        def t_state(g):
            pso = PSO[g]
            for s in range(nseq):
                if ty == 'p':
                    H_s = HT
                else:
                    hin = [Bf[12], Bf[16]][s % 2]
                    P.dma(v3(G(hin, g), 4),
                          st_s[s, 8 * g:8 * g + 8].rearrange("(b h2) p n -> (h2 p) b n", h2=2), q='pool')
                    H_s = [Bf[8], Bf[19]][s % 2]
                    tr4(G(H_s, g), [hin[:, (4 * g + bb) * 128:(4 * g + bb + 1) * 128] for bb in range(4)])
                if emit_out:
                    for bb in range(4):
                        b = 4 * g + bb
                        P.mm(pso[:, bb * 128 + s * L:bb * 128 + (s + 1) * L], H_s[:, b * 128:(b + 1) * 128],
                             CV3[:, 34 + g, s * L:(s + 1) * L])
                if ty == 'p':
                    XZs, Hn = XW, HT
                else:
                    XZs, Hn = [Bf[10], Bf[21]][s % 2], [Bf[9], Bf[20]][s % 2]
                    P.ts(G(XZs, g), G(XW, g), RM[:, s:s + 1], None, ALU.mult)
                psn = phalf()
                P.mm(psn, BTM[:, g * 128:(g + 1) * 128], G(XZs, g))
                P.tt(G8h(Hn, g), G8h(H_s, g), bh(EALB3[:, s, 8 * g:8 * g + 8], 8, 64), ALU.mult)
                P.tt(G(Hn, g), G(Hn, g), psn, ALU.add)
                if ty == 's':
                    hout = [Bf[17], Bf[18]][s % 2]
                    tr4(G(hout, g), [Hn[:, (4 * g + bb) * 128:(4 * g + bb + 1) * 128] for bb in range(4)])
                    out_dmas.append(P.dma(sst_s[s, 8 * g:8 * g + 8].rearrange("(b h2) p n -> (h2 p) b n", h2=2),
                                          v3(G(hout, g), 4)))
            if emit_out:
                P.cp(G(YOT, g), pso, eng='act')
            if t == 7:
                P.ts(G(HT, g), G(HT, g), FLAG, None, ALU.mult)
            if t == 15:
                tr4(G(HNAT, g), [HT[:, (4 * g + bb) * 128:(4 * g + bb + 1) * 128] for bb in range(4)])
                out_dmas.append(P.dma(sst_p[8 * g:8 * g + 8].rearrange("(b h2) p n -> (h2 p) b n", h2=2),
                                      v3(G(HNAT, g), 4)))

        def t_y(g):
            if not emit_out:
                return
            ph = phalf()
            for bb in range(4):
                b = 4 * g + bb
                P.tr(ph[:, bb * 128:(bb + 1) * 128], YOT[:, b * 128:(b + 1) * 128], ID)
            Tg = T1[g][:, 0:512]
            T8 = v3(Tg, 8)
            P.tt(T8, v3(ph, 8), h8(EA, g), ALU.mult)
            P.tt(G(Yb, g), G(Yb, g), Tg, ALU.add)
            P.tt(T8, G8h(XSb, g), h8(SD, g), ALU.mult, eng='pool')
            P.tt(G(Yb, g), G(Yb, g), Tg, ALU.add)
            P.tt(G(Yb, g), G(Yb, g), TM[:, 1024 + g * 512:1024 + (g + 1) * 512], ALU.mult)
            P.tt(Tg, G(Yb, g), G(Yb, g), ALU.mult)
            s2 = SS2[:, g:g + 1]
            P.red(s2, Tg)
            P.actf(s2, s2, AF.Ln, scale=1.0 / 512, bias=EPS)
            P.actf(s2, s2, AF.Exp, scale=-0.5)
            P.ts(G(Yb, g), G(Yb, g), s2, None, ALU.mult)
            P.tt(MIX[:, 1024 + g * 512:1024 + (g + 1) * 512], G(Yb, g), SNW[:, g * 512:(g + 1) * 512], ALU.mult,
                 eng='pool')

        lock([t_xs, t_decay, t_xdt, t_cb, t_ydiag, t_state, t_y])
        if emit_out:
            for kq in range(4):
                ph = phalf()
                for j in range(4):
                    kc = kq * 4 + j
                    P.tr(ph[:, j * 128:(j + 1) * 128], MIX[:, kc * 128:(kc + 1) * 128], ID)
                evac(MIXT3[:, kq * 4:kq * 4 + 4, ot * 128:(ot + 1) * 128], v3(ph, 4))
    P.fence()

    mixt_top = persist_top + 8 * NOUT
    A.top = mixt_top
    X1 = A.f32(9 * 2048)
    X13 = v3(X1, 9)
    WO = A.bf16(16 * 2048)
    WO3 = v3(WO, 16)
    XS2 = [A.f32(2048) for _ in range(2)]
    w_out_v = w_out.rearrange("(kc p) c -> p kc c", p=128)
    for q4 in range(4):
        P.dma(WO3[:, :, q4 * 512:(q4 + 1) * 512], w_out_v[:, :, q4 * 512:(q4 + 1) * 512], q='pool')
    for ot in range(9):
        xt = XS2[ot % 2]
        P.dma(xt, xin[1024 + ot * 128:1024 + (ot + 1) * 128, :])
        for dq in range(4):
            ph = phalf()
            for kc in range(16):
                P.mm(ph, MIXT3[:, kc, ot * 128:(ot + 1) * 128], WO3[:, kc, dq * 512:(dq + 1) * 512],
                     start=(kc == 0), stop=(kc == 15))
            P.tt(X13[:, ot, dq * 512:(dq + 1) * 512], xt[:, dq * 512:(dq + 1) * 512], ph, ALU.add)
    P.fence()
    A.top = persist_top
    H2T = A.bf16(16 * NOUT)
    H2T3 = v3(H2T, 16)
    assert A.top <= mixt_top
    A.top = mixt_top + 9 * 2048
    XN2 = A.f32(2048)
    SQJ2 = A.bf16(2048)
    SSc = A.f32(1)
    RSc = A.f32(1)
    for ot in range(9):
        rms_to_hT(X13[:, ot, :], FNW, H2T3[:, :, ot * 128:(ot + 1) * 128], SQJ2, XN2, SSc, RSc)
    NFB = 4
    FFT = A.bf16(NFB * NOUT)
    FFT3 = v3(FFT, NFB)
    WG = [A.bf16(16 * 128) for _ in range(2)]
    WU = [A.bf16(16 * 128) for _ in range(2)]
    WD = [A.bf16(NFB * 2048) for _ in range(2)]
    SIL = [A.f32(512) for _ in range(2)]
    w_gate_v = w_gate.rearrange("(kc p) c -> p kc c", p=128)
    w_up_v = w_up.rearrange("(kc p) c -> p kc c", p=128)
    tchunks = [(0, 512), (512, 512), (1024, 128)]
    groups = []
    b0 = 0
    while b0 < 44:
        nb = min(NFB, 44 - b0)
        groups.append((b0, nb))
        b0 += nb
    cnt = 0
    for gi, (b0, nb) in enumerate(groups):
        wd = WD[gi % 2]
        wd3 = v3(wd, NFB)
        P.dma(wd3[:, 0:nb, :], w_down[b0 * 128:(b0 + nb) * 128, :].rearrange("(b p) d -> p b d", p=128), q='pool')
        for j in range(nb):
            fb = b0 + j
            wg = v3(WG[fb % 2], 16)
            wu = v3(WU[fb % 2], 16)
            P.dma(wg, w_gate_v[:, :, fb * 128:(fb + 1) * 128], q='pool')
            P.dma(wu, w_up_v[:, :, fb * 128:(fb + 1) * 128], q='pool')
            for (t0, n) in tchunks:
                pg = phalf()
                for kc in range(16):
                    P.mm(pg[:, 0:n], wg[:, kc, :], H2T3[:, kc, t0:t0 + n], start=(kc == 0), stop=(kc == 15))
                pu = phalf()
                for kc in range(16):
                    P.mm(pu[:, 0:n], wu[:, kc, :], H2T3[:, kc, t0:t0 + n], start=(kc == 0), stop=(kc == 15))
                sl_ = SIL[cnt % 2]
                cnt += 1
                P.actf(sl_[:, 0:n], pg[:, 0:n], AF.Silu)
                P.tt(FFT3[:, j, t0:t0 + n], sl_[:, 0:n], pu[:, 0:n], ALU.mult)
        for ot in range(9):
            for dq in range(4):
                ph = phalf()
                for j in range(nb):
                    P.mm(ph, FFT3[:, j, ot * 128:(ot + 1) * 128], wd3[:, j, dq * 512:(dq + 1) * 512],
                         start=(j == 0), stop=(j == nb - 1))
                xs_ = X13[:, ot, dq * 512:(dq + 1) * 512]
                P.tt(xs_, xs_, ph, ALU.add)
    YB = [WD[0].bitcast(F32)[:, 0:2048], WD[1].bitcast(F32)[:, 0:2048]]
    for ot in range(9):
        xt = X13[:, ot, :]
        yb = YB[ot % 2]
        P.actf(XN2, xt, AF.Square, accum=SSc)
        P.actf(RSc, SSc, AF.Sqrt, scale=1.0 / D, bias=EPS)
        P.add('dve', lambda e: e.reciprocal(RSc, RSc), r=[RSc], w=[RSc])
        P.stt(yb, xt, RSc, FINW, ALU.mult, ALU.mult)
        out_dmas.append(P.dma(y_o[ot * 128:(ot + 1) * 128, :], yb))
    stats = P.emit(final_wait_ops=out_dmas)
    es.close()
    return nc, stats


_CACHE = {}


def kernel(x_prompt, x_sample, state_gdn_conv, state_gdn, state_ssm_conv, state_ssm,
           attn_norm_w, w_in, gdn_conv_w, gdn_A_log, gdn_dt_bias, gdn_norm_w,
           ssm_conv_w, ssm_conv_b, ssm_A_log, ssm_dt_bias, ssm_D, ssm_norm_w,
           w_out, ffn_norm_w, w_gate, w_up, w_down, final_norm_w):
    f = lambda a: np.ascontiguousarray(np.asarray(a, dtype=np.float32))
    x_prompt = f(x_prompt)
    x_sample = f(x_sample)
    if 'nc' not in _CACHE:
        _CACHE['nc'] = build_program()[0]
    nc = _CACHE['nc']
    shared = dict(
        attn_norm_w=f(attn_norm_w).reshape(16, 128), w_in=f(w_in)[0], gdn_conv_w=f(gdn_conv_w)[0],
        gdn_A_log=f(gdn_A_log).reshape(1, 8), gdn_dt_bias=f(gdn_dt_bias).reshape(1, 8),
        gdn_norm_w=f(gdn_norm_w).reshape(1, 128), ssm_conv_w=f(ssm_conv_w)[0],
        ssm_conv_b=f(ssm_conv_b).reshape(1, 1536), ssm_A_log=f(ssm_A_log).reshape(1, 16),
        ssm_dt_bias=f(ssm_dt_bias).reshape(1, 16), ssm_D=f(ssm_D).reshape(1, 16),
        ssm_norm_w=f(ssm_norm_w).reshape(1, 1024), w_out=f(w_out)[0],
        ffn_norm_w=f(ffn_norm_w).reshape(16, 128), w_gate=f(w_gate)[0], w_up=f(w_up)[0],
        w_down=f(w_down)[0], final_norm_w=f(final_norm_w).reshape(1, D))
    sgc = f(state_gdn_conv)[0]
    sg = f(state_gdn)[0]
    ssc = f(state_ssm_conv)[0]
    ssm = f(state_ssm)[0]
    in_maps = []
    for c in range(8):
        b, r = c // 2, c % 2
        xin = np.zeros((NT, D), np.float32)
        if r == 1:
            xin[0:1024] = x_prompt[b, 0:1024]
        xin[1024:2048] = x_prompt[b, r * 1024:(r + 1) * 1024]
        xin[2048:] = x_sample[16 * c:16 * c + 16].reshape(128, D)
        m = dict(shared)
        m.update(xin=xin, flag=np.full((128, 1), float(r), np.float32),
                 st_gconv=np.ascontiguousarray(sgc[16 * c:16 * c + 16].reshape(48, 3072)),
                 st_g=np.ascontiguousarray(sg[16 * c:16 * c + 16]),
                 st_sconv=np.ascontiguousarray(ssc[16 * c:16 * c + 16].reshape(48, 1536)),
                 st_s=np.ascontiguousarray(ssm[16 * c:16 * c + 16]))
        in_maps.append(m)
    res = run_bass_kernel_spmd(nc, in_maps, core_ids=list(range(8)))
    R = res.results
    y_prompt = np.zeros((4, 2048, D), np.float32)
    y_sample = np.zeros((128, 8, D), np.float32)
    gconv_p = np.zeros((1, 4, 3, 3072), np.float32)
    gst_p = np.zeros((1, 4, 8, 128, 128), np.float32)
    sconv_p = np.zeros((1, 4, 3, 1536), np.float32)
    sst_p = np.zeros((1, 4, 16, 64, 128), np.float32)
    gconv_s = np.zeros((1, 128, 3, 3072), np.float32)
    gst_s = np.zeros((1, 128, 8, 128, 128), np.float32)
    sconv_s = np.zeros((1, 128, 3, 1536), np.float32)
    sst_s = np.zeros((1, 128, 16, 64, 128), np.float32)
    for c in range(8):
        b, r = c // 2, c % 2
        o = R[c]
        y_prompt[b, r * 1024:(r + 1) * 1024] = o['y'][0:1024]
        y_sample[16 * c:16 * c + 16] = o['y'][1024:].reshape(16, 8, D)
        if r == 1:
            gconv_p[0, b] = o['gconv_p']
            gst_p[0, b] = o['gst_p']
            sconv_p[0, b] = o['sconv_p']
            sst_p[0, b] = o['sst_p']
        gconv_s[0, 16 * c:16 * c + 16] = o['gconv_s'].reshape(16, 3, 3072)
        gst_s[0, 16 * c:16 * c + 16] = o['gst_s']
        sconv_s[0, 16 * c:16 * c + 16] = o['sconv_s'].reshape(16, 3, 1536)
        sst_s[0, 16 * c:16 * c + 16] = o['sst_s']
    return (y_prompt, y_sample, gconv_p, gst_p, sconv_p, sst_p, gconv_s, gst_s, sconv_s, sst_s)
```

```python
import numpy as np
import concourse.bass as bass
import concourse.mybir as mybir

F32 = mybir.dt.float32
BF16 = mybir.dt.bfloat16
ALU = mybir.AluOpType
AF = mybir.ActivationFunctionType
AX = mybir.AxisListType

_ES = {F32: 4, BF16: 2}


def _esize(dt):
    if dt in _ES:
        return _ES[dt]
    s = str(dt)
    if '32' in s:
        return 4
    if '16' in s:
        return 2
    if '64' in s:
        return 8
    return 1


def footprint(ap):
    name = ap.tensor.name
    dims = ap.ap
    es = _esize(ap.dtype)
    off = ap.offset
    space = str(ap.space)
    if 'DRAM' in space.upper() or 'HBM' in space.upper() or 'Dram' in space:
        ext = 1
        for st, cnt in dims:
            ext += (cnt - 1) * abs(st)
        return (name, True, 0, 1, off * es, (off + ext) * es)
    pst, pcnt = dims[0]
    if pst == 0:
        p0 = 0
        lo = off
        pcnt_eff = 1
    else:
        p0 = off // pst
        lo = off % pst
        pcnt_eff = pcnt
    ext = 1
    for st, cnt in dims[1:]:
        ext += (cnt - 1) * abs(st)
    lo_b, hi_b = lo * es, (lo + ext) * es
    if name.startswith('pw'):
        lo_b = (lo_b // 2048) * 2048
        hi_b = ((hi_b + 2047) // 2048) * 2048
        return (name, False, 0, 128, lo_b, hi_b)
    return (name, False, p0, p0 + pcnt_eff, lo_b, hi_b)


class Op:
    __slots__ = ('eng', 'fn', 'deps', 'dma', 'token', 'prewait', 'needed', 'idx')


class Prog:
    ENGS = ('pe', 'act', 'dve', 'pool', 'sp')
    NS = 8

    def __init__(self, nc):
        self.nc = nc
        self.ops = []
        self.acc = {}
        self.last_on_eng = {}

    def _track(self, idx, aps_r, aps_w):
        deps = set()
        for is_w, aps in ((False, aps_r), (True, aps_w)):
            for ap in aps:
                name, isd, p0, p1, lo, hi = footprint(ap)
                lst = self.acc.setdefault(name, [])
                keep = []
                for e in lst:
                    ov = not (e[2] <= p0 or e[1] >= p1 or e[4] <= lo or e[3] >= hi)
                    if ov and (is_w or e[5]):
                        if e[0] != idx:
                            deps.add(e[0])
                        if is_w and e[1] >= p0 and e[2] <= p1 and e[3] >= lo and e[4] <= hi:
                            continue
                    keep.append(e)
                keep.append([idx, p0, p1, lo, hi, is_w])
                self.acc[name] = keep
        return deps

    def add(self, eng, fn, r=(), w=(), dma=False, extra_deps=()):
        op = Op()
        op.eng = eng
        op.fn = fn
        op.dma = dma
        op.idx = len(self.ops)
        op.deps = self._track(op.idx, r, w)
        op.deps.update(extra_deps)
        op.token = None
        op.prewait = None
        op.needed = False
        self.ops.append(op)
        self.last_on_eng[eng] = op.idx
        return op.idx

    def fence(self):
        last = []
        for e in self.ENGS:
            pass
        idxs = set()
        seen_eng = set()
        for op in reversed(self.ops):
            if op.dma:
                idxs.add(op.idx)
            elif op.eng not in seen_eng:
                seen_eng.add(op.eng)
                idxs.add(op.idx)
        lf = getattr(self, '_last_fence', 0)
        idxs = {i for i in idxs if i >= lf or not self.ops[i].dma}
        for e in self.ENGS:
            self.add(e, None, extra_deps=set(idxs))
        self._last_fence = len(self.ops)

    def mm(self, out, lhsT, rhs, start=True, stop=True):
        return self.add('pe', lambda e: e.matmul(out, lhsT, rhs, start=start, stop=stop),
                        r=[lhsT, rhs], w=[out])

    def tr(self, out, in_, ident):
        return self.add('pe', lambda e: e.transpose(out, in_, ident), r=[in_, ident], w=[out])

    def actf(self, out, in_, func, bias=None, scale=None, accum=None, eng='act'):
        kw = {}
        r = [in_]
        w = [out]
        if bias is not None:
            kw['bias'] = bias
            if not isinstance(bias, (int, float)):
                r.append(bias)
        if scale is not None:
            kw['scale'] = scale
            if not isinstance(scale, (int, float)):
                r.append(scale)
        if accum is not None:
            kw['accum_out'] = accum
            w.append(accum)
        return self.add(eng, lambda e: e.activation(out, in_, func, **kw), r=r, w=w)

    def tt(self, out, a, b, op, eng='dve'):
        return self.add(eng, lambda e: e.tensor_tensor(out, a, b, op), r=[a, b], w=[out])

    def ts(self, out, a, s1, s2, op0, op1=None, eng='dve', accum=None):
        r = [a]
        if not isinstance(s1, (int, float)):
            r.append(s1)
        if s2 is not None and not isinstance(s2, (int, float)):
            r.append(s2)
        w = [out]
        kw = {}
        if accum is not None:
            kw['accum_out'] = accum
            w.append(accum)
        if op1 is None:
            if isinstance(s1, (int, float)):
                return self.add(eng, lambda e: e.tensor_scalar(out, a, s1, None, op0, **kw), r=r, w=w)
            return self.add(eng, lambda e: e.tensor_scalar(out, a, s1, 0.0, op0, ALU.add, **kw), r=r, w=w)
        return self.add(eng, lambda e: e.tensor_scalar(out, a, s1, s2, op0, op1, **kw), r=r, w=w)

    def stt(self, out, a, s, b, op0, op1, eng='dve'):
        r = [a, b]
        if not isinstance(s, (int, float)):
            r.append(s)
        return self.add(eng, lambda e: e.scalar_tensor_tensor(out, a, s, b, op0, op1), r=r, w=[out])

    def cp(self, out, in_, eng='dve'):
        if eng == 'act':
            return self.add('act', lambda e: e.copy(out, in_), r=[in_], w=[out])
        return self.add(eng, lambda e: e.tensor_copy(out, in_), r=[in_], w=[out])

    def red(self, out, in_, op=None, eng='dve'):
        op = op or ALU.add
        return self.add(eng, lambda e: e.tensor_reduce(out, in_, AX.X, op), r=[in_], w=[out])

    def memset(self, ap, val, eng='pool'):
        return self.add(eng, lambda e: e.memset(ap, val), w=[ap])

    def dma(self, out, in_, q='sp'):
        return self.add(q, lambda e: e.dma_start(out, in_), r=[in_], w=[out], dma=True)

    def emit(self, final_wait_ops=()):
        nc = self.nc
        ops = self.ops
        if final_wait_ops:
            self.add('sp', None, extra_deps=set(final_wait_ops))
        for op in ops:
            for d in op.deps:
                dop = ops[d]
                if dop.dma:
                    continue
                if dop.eng == 'pe' and op.eng == 'pe' and not op.dma:
                    continue
                dop.needed = True
        from contextlib import ExitStack
        es = ExitStack()
        sem = {e: es.enter_context(nc.semaphore('s_' + e)) for e in self.ENGS}
        dsem = {e: [es.enter_context(nc.semaphore('d_%s%d' % (e, i))) for i in range(self.NS)]
                for e in ('sp', 'act', 'pool')}
        cnt = {e: 0 for e in self.ENGS}
        dcnt = {e: 0 for e in dsem}
        for op in ops:
            if op.fn is None:
                continue
            if op.dma:
                m = dcnt[op.eng]
                s = dsem[op.eng][m % self.NS]
                op.token = (s, 16 * (m // self.NS + 1), 16)
                if m >= self.NS:
                    op.prewait = (s, 16 * (m // self.NS))
                dcnt[op.eng] = m + 1
            elif op.needed:
                cnt[op.eng] += 1
                op.token = (sem[op.eng], cnt[op.eng], 1)
        per_eng = {e: [op for op in ops if op.eng == e] for e in self.ENGS}
        stats = {e: [0, 0] for e in self.ENGS}

        def run(ename, eobj):
            known = {}
            for op in per_eng[ename]:
                waits = {}
                if op.prewait is not None:
                    s, v = op.prewait
                    waits[id(s)] = (s, v)
                for d in op.deps:
                    dop = ops[d]
                    if dop.token is None:
                        continue
                    if (not dop.dma) and dop.eng == 'pe' and ename == 'pe' and not op.dma:
                        continue
                    s, v, _ = dop.token
                    if id(s) not in waits or waits[id(s)][1] < v:
                        waits[id(s)] = (s, v)
                for k, (s, v) in waits.items():
                    if known.get(k, 0) >= v:
                        continue
                    eobj.wait_ge(s, v)
                    stats[ename][1] += 1
                    known[k] = v
                if op.fn is None:
                    continue
                ins = op.fn(eobj)
                stats[ename][0] += 1
                if op.token is not None:
                    ins.then_inc(op.token[0], op.token[2])

        with nc.Block() as block:
            @block.tensor
            def _(e):
                run('pe', e)

            @block.scalar
            def _(e):
                run('act', e)

            @block.vector
            def _(e):
                run('dve', e)

            @block.gpsimd
            def _(e):
                run('pool', e)

            @block.sync
            def _(e):
                run('sp', e)
        es.close()
        return stats

from contextlib import ExitStack
from concourse.bass_utils import run_bass_kernel_spmd

D = 2048
DIN = 6688
DFF = 5632
NT = 2176
NTILE = 17
NOUT = 1152
EPSV = 1e-6
PFW = 3 + NT
TMW = 2080


class Arena:
    def __init__(self, ar, total):
        self.ar = ar
        self.top = 0
        self.total = total

    def f32(self, n):
        o = self.top
        self.top += n
        assert self.top <= self.total, (self.top, self.total)
        return self.ar[:, o:o + n]

    def bf16(self, n):
        assert n % 2 == 0
        return self.f32(n // 2).bitcast(BF16)

    def at(self, o, n):
        return self.ar[:, o:o + n]


def v3(ap, a):
    return ap.rearrange("p (a b) -> p a b", a=a)


def build_program(debug=False):
    nc = bass.Bass("TRN2", target_bir_lowering=False)

    def din(name, shape):
        return nc.dram_tensor(name, list(shape), F32, kind="ExternalInput").ap()

    def dout(name, shape):
        return nc.dram_tensor(name, list(shape), F32, kind="ExternalOutput").ap()

    xin = din("xin", [NT, D])
    flag_d = din("flag", [128, 1])
    st_gconv = din("st_gconv", [48, 3072])
    st_g = din("st_g", [16, 8, 128, 128])
    st_sconv = din("st_sconv", [48, 1536])
    st_s = din("st_s", [16, 16, 64, 128])
    attn_norm_w = din("attn_norm_w", [16, 128])
    w_in = din("w_in", [D, DIN])
    gdn_conv_w = din("gdn_conv_w", [4, 3072])
    gdn_A_log = din("gdn_A_log", [1, 8])
    gdn_dt_bias = din("gdn_dt_bias", [1, 8])
    gdn_norm_w = din("gdn_norm_w", [1, 128])
    ssm_conv_w = din("ssm_conv_w", [4, 1536])
    ssm_conv_b = din("ssm_conv_b", [1, 1536])
    ssm_A_log = din("ssm_A_log", [1, 16])
    ssm_dt_bias = din("ssm_dt_bias", [1, 16])
    ssm_D = din("ssm_D", [1, 16])
    ssm_norm_w = din("ssm_norm_w", [1, 1024])
    w_out = din("w_out", [D, D])
    ffn_norm_w = din("ffn_norm_w", [16, 128])
    w_gate = din("w_gate", [D, DFF])
    w_up = din("w_up", [D, DFF])
    w_down = din("w_down", [DFF, D])
    final_norm_w = din("final_norm_w", [1, D])

    y_o = dout("y", [NOUT, D])
    gconv_p = dout("gconv_p", [3, 3072])
    gst_p = dout("gst_p", [8, 128, 128])
    sconv_p = dout("sconv_p", [3, 1536])
    sst_p = dout("sst_p", [16, 64, 128])
    gconv_s = dout("gconv_s", [48, 3072])
    gst_s = dout("gst_s", [16, 8, 128, 128])
    sconv_s = dout("sconv_s", [48, 1536])
    sst_s = dout("sst_s", [16, 16, 64, 128])

    P_fm = nc.dram_tensor("P_fm", [36, 128, PFW], F32).ap()
    P_tm = nc.dram_tensor("P_tm", [NT, TMW], F32).ap()
    P_cv = nc.dram_tensor("P_cv", [36, 128, 2048], F32).ap()

    es = ExitStack()
    TOT = 53000
    ar_t = es.enter_context(nc.sbuf_tensor("arena", [128, TOT], F32))
    PW = [es.enter_context(nc.psum_tensor("pw%d" % i, [128, 1024], F32)) for i in range(4)]
    A = Arena(ar_t, TOT)
    P = Prog(nc)
    pwc = [0]

    def pw():
        pwc[0] += 1
        return PW[pwc[0] % 3]
    pw_global = pw

    hb = [0]

    def phalf():
        hb[0] += 1
        k = hb[0] % 6
        return PW[k // 2][:, (k % 2) * 512:(k % 2) * 512 + 512]

    ID = A.f32(128)
    ONES = A.f32(128)
    MI = {}
    MS = {}
    UI = {}
    for ty in ('p', 's'):
        MI[ty] = A.f32(128)
        MS[ty] = A.f32(128)
        UI[ty] = A.f32(128)
    BD = A.f32(128)
    RM = A.f32(16)
    EPS = A.f32(1)
    FLAG = A.f32(1)
    ANW = A.f32(16)
    FNW = A.f32(16)
    FINW = A.f32(2048)
    GNW = A.f32(128)
    SNW = A.f32(1024)
    SD = A.f32(16)
    NAG = A.f32(8)
    GDB = A.f32(8)
    NAS = A.f32(16)
    SDB = A.f32(16)
    CW = A.f32(36 * 5)
    CW3 = v3(CW, 36)
    ZERO = A.f32(128)
    LNDK = A.f32(1)

    def asel(ap, pattern, op, fill, base, cm):
        P.add('pool', lambda e: e.affine_select(ap, ap, pattern, op, fill, base=base, channel_multiplier=cm),
              r=[ap], w=[ap])

    P.memset(ID, 0.0)
    asel(ID, [[-1, 128]], ALU.not_equal, 1.0, 0, 1)
    P.memset(ONES, 1.0)
    P.memset(ZERO, 0.0)
    P.memset(EPS, EPSV)
    P.memset(LNDK, float(np.log(128 ** -0.5)))
    P.memset(BD, 1.0)
    asel(v3(BD, 16), [[-8, 16], [0, 8]], ALU.is_ge, 0.0, 0, 1)
    asel(v3(BD, 16), [[8, 16], [0, 8]], ALU.is_ge, 0.0, 7, -1)
    P.memset(RM, 1.0)
    asel(RM, [[-8, 16]], ALU.is_ge, 0.0, 0, 1)
    asel(RM, [[8, 16]], ALU.is_ge, 0.0, 7, -1)
    for ty in ('p', 's'):
        P.memset(MI[ty], 1.0)
        asel(MI[ty], [[-1, 128]], ALU.is_ge, 0.0, 0, 1)
        P.memset(MS[ty], 1.0)
        asel(MS[ty], [[-1, 128]], ALU.is_gt, 0.0, 0, 1)
        P.memset(UI[ty], 1.0)
        asel(UI[ty], [[1, 128]], ALU.is_ge, 0.0, 0, -1)
        if ty == 's':
            for m in (MI, MS, UI):
                P.tt(m[ty], m[ty], BD, ALU.mult, eng='pool')
    P.dma(FLAG, flag_d)
    P.dma(FINW, final_norm_w.broadcast_to([128, D]))
    P.dma(GNW, gdn_norm_w.broadcast_to([128, 128]))
    P.dma(SNW, ssm_norm_w.broadcast_to([128, 1024]))
    P.dma(SD, ssm_D.broadcast_to([128, 16]))
    P.dma(NAG, gdn_A_log.broadcast_to([128, 8]))
    P.dma(GDB, gdn_dt_bias.broadcast_to([128, 8]))
    P.dma(NAS, ssm_A_log.broadcast_to([128, 16]))
    P.dma(SDB, ssm_dt_bias.broadcast_to([128, 16]))
    P.actf(NAG, NAG, AF.Exp)
    P.ts(NAG, NAG, -1.0, None, ALU.mult)
    P.actf(NAS, NAS, AF.Exp)
    P.ts(NAS, NAS, -1.0, None, ALU.mult)

    mark0 = A.top
    TMPA = A.f32(4608 + 256)
    cwst = TMPA[0:5, 0:4608]
    P.memset(TMPA[0:5, 0:4608], 0.0)
    P.dma(TMPA[0:4, 0:3072], gdn_conv_w)
    P.dma(TMPA[0:4, 3072:4608], ssm_conv_w)
    P.dma(TMPA[4:5, 3072:4608], ssm_conv_b)
    P.dma(TMPA[0:16, 4608:4736], attn_norm_w)
    P.dma(TMPA[0:16, 4736:4864], ffn_norm_w)
    ps = pw()
    for b in range(36):
        P.tr(ps[:, b * 5:b * 5 + 5], TMPA[0:5, b * 128:(b + 1) * 128], ID[0:5, 0:5])
    P.cp(CW, ps[:, 0:180], eng='act')
    ps = pw()
    P.tr(ps[:, 0:16], TMPA[0:16, 4608:4736], ID[0:16, 0:16])
    P.tr(ps[:, 16:32], TMPA[0:16, 4736:4864], ID[0:16, 0:16])
    P.cp(ANW, ps[:, 0:16], eng='act')
    P.cp(FNW, ps[:, 16:32], eng='act')
    P.fence()
    A.top = mark0
    persist_top = A.top

    def rms_to_hT(xt, nw, hT_dst, tmp_sq, xn, ss, rs):
        P.actf(tmp_sq, xt, AF.Square, accum=ss)
        P.actf(rs, ss, AF.Sqrt, scale=1.0 / D, bias=EPS)
        P.add('dve', lambda e: e.reciprocal(rs, rs), r=[rs], w=[rs])
        P.ts(xn, xt, rs, None, ALU.mult)
        for kq in range(4):
            ph = phalf()
            for j in range(4):
                kc = kq * 4 + j
                P.tr(ph[:, j * 128:(j + 1) * 128], xn[:, kc * 128:(kc + 1) * 128], ID)
            P.tt(hT_dst[:, kq * 4:kq * 4 + 4, :], v3(ph, 4),
                 nw[:, kq * 4:kq * 4 + 4].unsqueeze(2).to_broadcast([128, 4, 128]), ALU.mult)

    hT = A.bf16(16 * NT)
    hT3 = v3(hT, 16)
    XS = [A.f32(2048) for _ in range(2)]
    XN = A.f32(2048)
    SQJ = A.bf16(2048)
    SS = A.f32(1)
    RS = A.f32(1)
    for t in range(NTILE):
        xt = XS[t % 2]
        P.dma(xt, xin[t * 128:(t + 1) * 128, :])
        rms_to_hT(xt, ANW, hT3[:, :, t * 128:(t + 1) * 128], SQJ, XN, SS, RS)

    Z3 = A.f32(36 * 3)
    P.memset(Z3, 0.0)
    P.dma(P_fm.rearrange("b p t -> p b t")[:, :, 0:3], v3(Z3, 36))

    WB = [A.bf16(16 * 128) for _ in range(3)]
    STG = [A.f32(3 + NT) for _ in range(2)]
    CVS = [A.f32(2048) for _ in range(2)]
    for st_ in STG:
        P.memset(st_[:, 0:3], 0.0)
    w_in_v = w_in.rearrange("(kc p) c -> p kc c", p=128)
    chunks = [(0, 512), (512, 512), (1024, 512), (1536, 512), (2048, 128)]
    ev = [0]

    def evac(dst, src):
        ev[0] += 1
        P.cp(dst, src, eng='act' if ev[0] % 2 else 'dve')

    for cb in range(36):
        c0 = cb * 128 if cb < 24 else 5136 + (cb - 24) * 128
        W = WB[cb % 3]
        W3 = v3(W, 16)
        P.dma(W3, w_in_v[:, :, c0:c0 + 128], q='pool')
        stg = STG[cb % 2]
        cks = chunks if cb >= 8 else [(896, 128), (1024, 512), (1536, 512), (2048, 128)]
        for (t0, n) in cks:
            ph = phalf()
            for kc in range(16):
                P.mm(ph[:, 0:n], W3[:, kc, :], hT3[:, kc, t0:t0 + n], start=(kc == 0), stop=(kc == 15))
            evac(stg[:, 3 + t0:3 + t0 + n], ph[:, 0:n])
        tb = cks[0][0]
        P.dma(P_fm[cb, :, 3 + tb:3 + NT], stg[:, 3 + tb:3 + NT])
        tlo = 0 if cb >= 8 else 1024
        cvs = CVS[cb % 2]
        P.actf(cvs[:, tlo:2048], stg[:, tlo:2048], AF.Identity, scale=CW3[:, cb, 0:1], bias=CW3[:, cb, 4:5])
        for i in range(1, 4):
            P.stt(cvs[:, tlo:2048], stg[:, tlo + i:2048 + i], CW3[:, cb, i:i + 1], cvs[:, tlo:2048], ALU.mult, ALU.add)
        P.actf(cvs[:, tlo:2048], cvs[:, tlo:2048], AF.Silu)
        P.dma(P_cv[cb, :, tlo:2048], cvs[:, tlo:2048])

    WT = [A.bf16(16 * 512) for _ in range(2)]
    TS_ = [A.f32(512) for _ in range(2)]
    tmchunks = [(3072, 512, 0, 8), (3584, 512, 512, 8), (4112, 512, 1024, 8), (4624, 512, 1536, 8),
                (4096, 16, 2048, 0), (6672, 16, 2064, 0)]
    k = 0
    for i, (c0, ncol, pc0, t_first) in enumerate(tmchunks):
        W = WT[i % 2]
        W3 = v3(W, 16)
        P.dma(W3[:, :, 0:ncol], w_in_v[:, :, c0:c0 + ncol], q='pool')
        for t in range(t_first, NTILE):
            ph = phalf()
            for kc in range(16):
                P.mm(ph[:, 0:ncol], hT3[:, kc, t * 128:(t + 1) * 128], W3[:, kc, 0:ncol],
                     start=(kc == 0), stop=(kc == 15))
            ts_ = TS_[k % 2]
            k += 1
            evac(ts_[:, 0:ncol], ph[:, 0:ncol])
            P.dma(P_tm[t * 128:(t + 1) * 128, pc0:pc0 + ncol], ts_[:, 0:ncol])
    P.fence()

    A.top = persist_top
    MIXT = A.bf16(16 * NOUT)
    MIXT3 = v3(MIXT, 16)
    SG = A.f32(1024)
    HT = A.f32(1024)
    P.memset(SG, 0.0)
    P.memset(HT, 0.0)
    RAW = A.f32(36 * 176)
    CV = A.f32(36 * 128)
    RAWC = CV
    CV3 = v3(CV, 36)
    TM = A.f32(TMW)
    NB = 22
    bf0 = A.top
    Bf = [A.f32(1024) for _ in range(NB)]
    HST = A.at(bf0 + 17 * 1024, 4608)
    CST = HST
    MIX = A.at(bf0 + 13 * 1024, 2048)
    SMALL = A.f32(1024)
    SGS = [Bf[4], Bf[6]]
    SGO = [Bf[12], Bf[10]]
    out_dmas = []

    def sm(o, n):
        return SMALL[:, o:o + n]

    BETA = sm(0, 8)
    BETAN = sm(8, 8)
    G8 = sm(16, 8)
    TMP8 = sm(24, 8)
    DT16 = sm(32, 16)
    A16 = sm(48, 16)
    TMP16 = sm(64, 16)
    EGK = sm(80, 16)
    EG = sm(80, 8)
    EKD = sm(88, 8)
    BEG = sm(96, 8)
    RQ = sm(104, 8)
    RK = sm(112, 8)
    SSQ = sm(120, 8)
    EAA = sm(128, 32)
    EA = sm(128, 16)
    EAL = sm(144, 16)
    SSO = sm(160, 8)
    SS2 = sm(168, 2)
    EGL = sm(176, 128)
    EALB = sm(304, 256)
    RHE = sm(560, 256)
    SSK = sm(816, 8)

    def bh(ap, n=8, w=128):
        return ap.unsqueeze(2).to_broadcast([128, n, w])

    def bm(ap, n=8, w=128):
        return ap.unsqueeze(1).to_broadcast([128, n, w])

    DK_SCALE = 128 ** -0.5

    for t in range(NTILE):
        ty = 's' if t == 16 else 'p'
        nseq = 16 if ty == 's' else 1
        L = 8 if ty == 's' else 128
        emit_out = t >= 8
        ot = t - 8
        CVt = CV if t % 2 == 0 else RAW[:, 0:36 * 128]
        CV3 = v3(CVt, 36)
        if ty == 'p':
            b_lo = 0 if emit_out else 8
            P.dma(CV3[:, b_lo:36, :], P_cv[b_lo:36, :, t * 128:(t + 1) * 128].rearrange("b p t -> p b t"))
            if t == 15:
                RAW3 = v3(sm(840, 108), 36)
                P.dma(RAW3, P_fm[:, :, 3 + 2045:3 + 2048].rearrange("b p t -> p b t"))
        else:
            RAW4 = RAW.rearrange("p (b s t) -> p b s t", b=36, s=16)
            P.dma(v3(RAWC, 36), P_fm[:, :, 3 + 2048:3 + NT].rearrange("b p t -> p b t"))
            P.dma(HST[0:48, 0:3072], st_gconv)
            P.dma(HST[0:48, 3072:4608], st_sconv)
            for b0 in range(0, 36, 8):
                nb = min(8, 36 - b0)
                ps = pw()
                for b in range(nb):
                    P.tr(ps[:, b * 48:(b + 1) * 48], HST[0:48, (b0 + b) * 128:(b0 + b + 1) * 128], ID[0:48, 0:48])
                P.cp(RAW4[:, b0:b0 + nb, :, 0:3],
                     ps[:, 0:nb * 48].rearrange("p (b s t) -> p b s t", b=nb, s=16), eng='act')
            P.cp(RAW4[:, :, :, 3:11], RAWC.rearrange("p (b s t) -> p b s t", b=36, s=16), eng='pool')
        TMs = sm(950 + 32 * (t % 2), 32)
        P.dma(TMs, P_tm[t * 128:(t + 1) * 128, 2048:TMW])
        if emit_out:
            P.dma(TM[:, 0:2048], P_tm[t * 128:(t + 1) * 128, 0:2048])
        cblocks = list(range(36)) if ty == 's' else []
        NPOOL = 8
        pool_blocks = cblocks[-NPOOL:]
        dve_blocks = cblocks[:-NPOOL]
        CTMP = v3(Bf[21], 8)
        if ty == 's':
            NPOOL = 8

        def cio(b):
            if ty == 'p':
                return CV3[:, b, :], [RAW3[:, b, i:i + 128] for i in range(4)], (lambda a: a)
            return (CV3[:, b, :].rearrange("p (s t) -> p s t", s=16), [RAW4[:, b, :, i:i + 8] for i in range(4)],
                    (lambda a: a.rearrange("p (s t) -> p s t", s=16)))
        for b in cblocks:
            o_, ins, _ = cio(b)
            P.actf(o_, ins[0], AF.Identity, scale=CW3[:, b, 0:1], bias=CW3[:, b, 4:5])
        for i in range(1, 4):
            for b in dve_blocks:
                o_, ins, _ = cio(b)
                P.stt(o_, ins[i], CW3[:, b, i:i + 1], o_, ALU.mult, ALU.add)
            for j, b in enumerate(pool_blocks):
                o_, ins, vw = cio(b)
                P.ts(vw(CTMP[:, j, :]), ins[i], CW3[:, b, i:i + 1], None, ALU.mult, eng='pool')
            for j, b in enumerate(pool_blocks):
                o_, ins, vw = cio(b)
                P.tt(o_, o_, vw(CTMP[:, j, :]), ALU.add, eng='pool')
        if ty == 's':
            P.actf(CV, CV, AF.Silu)
        if t == 15 or ty == 's':
            nr = 3 if ty == 'p' else 48
            if ty == 's':
                LST = A.at(bf0 + 15 * 1024, 36 * 48)
                P.cp(LST.rearrange("p (b s t) -> p b s t", b=36, s=16), RAW4[:, :, :, 8:11], eng='pool')
                LST3 = v3(LST, 36)
            for b0 in range(0, 36, 4):
                ph = phalf()
                for b in range(4):
                    src = RAW3[:, b0 + b, 0:3] if ty == 'p' else LST3[:, b0 + b, :]
                    P.tr(ph[0:nr, b * 128:(b + 1) * 128], src, ID)
                P.cp(CST[0:nr, b0 * 128:(b0 + 4) * 128], ph[0:nr, 0:512], eng='act')
            if ty == 'p':
                out_dmas.append(P.dma(gconv_p, CST[0:3, 0:3072]))
                out_dmas.append(P.dma(sconv_p, CST[0:3, 3072:4608]))
            else:
                out_dmas.append(P.dma(gconv_s, CST[0:48, 0:3072]))
                out_dmas.append(P.dma(sconv_s, CST[0:48, 3072:4608]))
        P.actf(BETA, TMs[:, 0:8], AF.Exp, scale=-1.0)
        P.ts(BETA, BETA, 1.0, None, ALU.add)
        P.add('dve', lambda e: e.reciprocal(BETA, BETA), r=[BETA], w=[BETA])
        P.ts(BETAN, BETA, -1.0, None, ALU.mult)
        P.tt(TMP8, TMs[:, 8:16], GDB, ALU.add)
        P.actf(TMP8, TMP8, AF.Exp)
        P.actf(TMP8, TMP8, AF.Ln, bias=1.0)
        P.tt(G8, TMP8, NAG, ALU.mult)
        P.tt(TMP16, TMs[:, 16:32], SDB, ALU.add)
        P.actf(TMP16, TMP16, AF.Exp)
        P.actf(DT16, TMP16, AF.Ln, bias=1.0)
        P.tt(A16, DT16, NAS, ALU.mult)
        Q, SQ, KN, V, RH, DX, DM, QNT, KNT, QDT, NT0, QKM, N0, QKMT, R, NA, NB_, UTb, WTb, KDEC, VNb, OTb = Bf
        QN = Q

        def cgs(g):
            return slice(g * 512, (g + 1) * 512)

        def G(buf, g):
            return buf[:, g * 512:(g + 1) * 512]

        def G4(buf, g):
            return v3(buf[:, g * 512:(g + 1) * 512], 4)

        def h4(ap8, g):
            return bh(ap8[:, 4 * g:4 * g + 4], 4)

        def m4(ap):
            return bm(ap, 4)

        def lock(stages):
            for st in stages:
                for g in (0, 1):
                    st(g)

        pss = phalf()
        P.mm(pss[:, 0:8], UI[ty], G8)
        P.mm(pss[:, 8:16], MS[ty], G8)
        P.actf(EGK, pss[:, 0:16], AF.Exp)
        if ty == 'p':
            P.mm(pss[:, 16:24], ONES, G8)
        else:
            P.tt(RHE[:, 0:128].rearrange("p (s h) -> p s h", s=16),
                 G8.unsqueeze(1).to_broadcast([128, 16, 8]), RM.unsqueeze(2).to_broadcast([128, 16, 8]), ALU.mult)
            P.mm(pss[:, 16:16 + 128], ONES, RHE[:, 0:128])
        P.actf(EGL[:, 0:nseq * 8], pss[:, 16:16 + nseq * 8], AF.Exp)
        EGL3 = EGL[:, 0:nseq * 8].rearrange("p (s h) -> p s h", s=nseq)
        P.tt(BEG, BETA, EG, ALU.mult)

        def tr4(dst, srcs):
            ph = phalf()
            for hh in range(4):
                P.tr(ph[:, hh * 128:(hh + 1) * 128], srcs[hh], ID)
            P.cp(dst, ph, eng='act')

        def s_qT(g):
            if emit_out:
                tr4(G(Q, g), [CV3[:, 4 * g + hh, :] for hh in range(4)])

        def s_kT(g):
            tr4(G(KN, g), [CV3[:, 8 + 4 * g + hh, :] for hh in range(4)])

        def s_vT(g):
            tr4(G(V, g), [CV3[:, 16 + 4 * g + hh, :] for hh in range(4)])

        def norm_(X_, R_, S_, sc, g):
            P.tt(G(SQ, g), G(X_, g), G(X_, g), ALU.mult)
            P.red(S_[:, 4 * g:4 * g + 4], G4(SQ, g))
            r_ = R_[:, 4 * g:4 * g + 4]
            P.actf(r_, S_[:, 4 * g:4 * g + 4], AF.Ln, bias=EPS)
            P.actf(r_, r_, AF.Exp, scale=-0.5, bias=(LNDK if sc != 1.0 else None))
            P.tt(G4(X_, g), G4(X_, g), h4(R_, g), ALU.mult)

        def s_qn(g):
            if emit_out:
                norm_(Q, RQ, SSQ, DK_SCALE, g)

        def s_kn(g):
            norm_(KN, RK, SSK, 1.0, g)

        def s_decay(g):
            P.tt(G4(RH, g), m4(MS[ty]), h4(G8, g), ALU.mult)
            ph = phalf()
            P.mm(ph, UI[ty], G(RH, g))
            P.actf(G(DX, g), ph, AF.Exp)
            if emit_out:
                P.tt(G4(DM, g), G4(DX, g), m4(MI[ty]), ALU.mult)
            P.tt(G4(DX, g), G4(DX, g), m4(MS[ty]), ALU.mult)
            P.tt(G4(DX, g), G4(DX, g), h4(BETAN, g), ALU.mult)

        def s_qnT(g):
            if emit_out:
                tr4(G(QNT, g), [QN[:, (4 * g + hh) * 128:(4 * g + hh + 1) * 128] for hh in range(4)])

        def s_knT(g):
            tr4(G(KNT, g), [KN[:, (4 * g + hh) * 128:(4 * g + hh + 1) * 128] for hh in range(4)])

        def s_qdT(g):
            if emit_out:
                P.tt(G4(SQ, g), G4(QN, g), h4(EG, g), ALU.mult)
                tr4(G(QDT, g), [SQ[:, (4 * g + hh) * 128:(4 * g + hh + 1) * 128] for hh in range(4)])

        def mm4(lh, rh, g):
            ph = phalf()
            for hh in range(4):
                sl = slice((4 * g + hh) * 128, (4 * g + hh + 1) * 128)
                P.mm(ph[:, hh * 128:(hh + 1) * 128], lh[:, sl], rh[:, sl])
            return ph

        def s_gram(g):
            ph = mm4(KNT, KNT, g)
            P.tt(G(NT0, g), ph, G(DX, g), ALU.mult)

        def s_qk(g):
            if emit_out:
                ph = mm4(QNT, KNT, g)
                P.tt(G(QKM, g), ph, G(DM, g), ALU.mult)

        def s_n0(g):
            tr4(G(N0, g), [NT0[:, (4 * g + hh) * 128:(4 * g + hh + 1) * 128] for hh in range(4)])
            P.tt(G4(R, g), G4(N0, g), m4(ID), ALU.add)

        def s_qkT(g):
            if emit_out:
                tr4(G(QKMT, g), [QKM[:, (4 * g + hh) * 128:(4 * g + hh + 1) * 128] for hh in range(4)])

        lock([s_kT, s_vT, s_qT, s_kn, s_decay, s_qn, s_knT, s_gram, s_qnT, s_n0, s_qdT, s_qk, s_qkT])

        nsteps = 6 if ty == 'p' else 2
        Nc, NTc = N0, NT0
        pp = [(NA, NB_), (RH, DM)]
        for s_ in range(1, nsteps + 1):
            Nn, NTn = pp[s_ % 2]

            def d_sq(g, Nc=Nc, NTc=NTc, Nn=Nn, NTn=NTn, s_=s_):
                psb = mm4(Nc, NTc, g)
                P.cp(G(NTn, g), psb, eng='act')
                if s_ < nsteps:
                    psa = mm4(NTc, Nc, g)
                    P.cp(G(Nn, g), psa, eng='dve')

            def d_r(g, NTn=NTn):
                psc = mm4(NTn, R, g)
                P.tt(G(R, g), G(R, g), psc, ALU.add)

            lock([d_sq, d_r])
            Nc, NTc = Nn, NTn

        VBb = NA
        KBG = NB_
        VZ = UTb

        def s_prep(g):
            P.tt(G4(VBb, g), G4(V, g), h4(BETA, g), ALU.mult)
            P.tt(G4(KBG, g), G4(KN, g), h4(BEG, g), ALU.mult)
            P.tt(G4(KDEC, g), G4(KN, g), h4(EKD, g), ALU.mult)

        def s_u(g):
            ph = mm4(VBb, R, g)
            P.cp(G(UTb, g), ph, eng='act')

        def s_w(g):
            ph = mm4(KBG, R, g)
            P.cp(G(WTb, g), ph, eng='act')

        VNT = SQ

        def load_S(g, s, bufs):
            S_s = bufs[s % len(bufs)]
            P.dma(G4(S_s, g), st_g[s, 4 * g:4 * g + 4].rearrange("h k v -> k h v"), q='pool')
            return S_s

        def s_scanA(g):
            psA = phalf()
            psB = phalf() if emit_out else None
            for s in range(nseq):
                S_s = SG if ty == 'p' else load_S(g, s, [RH, DM, N0, NT0])
                for hh in range(4):
                    h = 4 * g + hh
                    P.mm(psA[:, hh * 128 + s * L:hh * 128 + (s + 1) * L], S_s[:, h * 128:(h + 1) * 128],
                         WTb[:, h * 128 + s * L:h * 128 + (s + 1) * L])
                if emit_out:
                    for hh in range(4):
                        h = 4 * g + hh
                        P.mm(psB[:, hh * 128 + s * L:hh * 128 + (s + 1) * L], S_s[:, h * 128:(h + 1) * 128],
                             QDT[:, h * 128 + s * L:h * 128 + (s + 1) * L])
            P.tt(G(VNT, g), G(UTb, g), psA, ALU.subtract)
            if emit_out:
                P.cp(G(OTb, g), psB, eng='act')

        def s_vn(g):
            tr4(G(VNb, g), [VNT[:, (4 * g + hh) * 128:(4 * g + hh + 1) * 128] for hh in range(4)])

        def s_c(g):
            if emit_out:
                psC = mm4(VNb, QKMT, g)
                P.tt(G(OTb, g), G(OTb, g), psC, ALU.add)

        def s_upd(g):
            for s in range(nseq):
                if ty == 'p':
                    S_s, VZs, Sn = SG, VNb, SG
                else:
                    S_s = load_S(g, s, [RH, DM, QNT, KNT])
                    VZs = [UTb, WTb][s % 2]
                    P.ts(G(VZs, g), G(VNb, g), RM[:, s:s + 1], None, ALU.mult)
                    Sn = [N0, NT0, QDT, QKM][s % 4]
                psn = mm4(KDEC, VZs, g)
                P.tt(G4(Sn, g), G4(S_s, g), bh(EGL3[:, s, 4 * g:4 * g + 4], 4), ALU.mult)
                P.tt(G(Sn, g), G(Sn, g), psn, ALU.add)
                if ty == 's':
                    out_dmas.append(P.dma(gst_s[s, 4 * g:4 * g + 4].rearrange("h k v -> k h v"), G4(Sn, g)))
            if t == 7:
                P.ts(G(SG, g), G(SG, g), FLAG, None, ALU.mult)
            if t == 15:
                out_dmas.append(P.dma(gst_p[4 * g:4 * g + 4].rearrange("h k v -> k h v"), G4(SG, g)))

        O = Q

        def s_out(g):
            if not emit_out:
                return
            ph = phalf()
            for hh in range(4):
                h = 4 * g + hh
                P.tr(ph[:, hh * 128:(hh + 1) * 128], OTb[:, h * 128:(h + 1) * 128], ID)
            P.cp(G(O, g), ph, eng='act')
            P.tt(G(SQ, g), G(O, g), G(O, g), ALU.mult)
            so = SSO[:, 4 * g:4 * g + 4]
            P.red(so, G4(SQ, g))
            P.actf(so, so, AF.Ln, scale=1.0 / 128, bias=EPS)
            P.actf(so, so, AF.Exp, scale=-0.5)
            P.tt(G4(O, g), G4(O, g), h4(SSO, g), ALU.mult)
            P.tt(G4(O, g), G4(O, g), m4(GNW), ALU.mult)
            P.tt(MIX[:, g * 512:(g + 1) * 512], G(O, g), TM[:, g * 512:(g + 1) * 512], ALU.mult)

        if emit_out:
            P.actf(TM[:, 0:2048], TM[:, 0:2048], AF.Silu)
        lock([s_prep, s_u, s_w, s_scanA, s_vn, s_c, s_upd, s_out])

        XSb, XDT, XW, DXT0, DXT1, Yb, T1a, YOT, HTS, HTN, XZ = Bf[0:11]
        BTM = Bf[11][:, 0:256]
        CBM = Bf[11][:, 256:512]
        HNAT = Bf[12]
        T1b = Bf[15]
        T1 = [T1a, T1b]
        DXT = [DXT0, DXT1]
        PSO = [PW[3][:, 0:512], PW[3][:, 512:1024]]

        def G8h(buf, g):
            return v3(buf[:, g * 512:(g + 1) * 512], 8)

        def h8(ap16, g):
            return bh(ap16[:, 8 * g:8 * g + 8], 8, 64)

        psb_ = phalf()
        for g in range(2):
            P.tr(psb_[:, g * 128:(g + 1) * 128], CV3[:, 32 + g, :], ID)
        P.cp(BTM, psb_[:, 0:256], eng='act')
        pss = phalf()
        P.mm(pss[:, 0:16], UI[ty], A16)
        P.mm(pss[:, 16:32], MS[ty], A16)
        P.actf(EAA, pss[:, 0:32], AF.Exp)
        if ty == 'p':
            P.mm(pss[:, 32:48], ONES, A16)
        else:
            P.tt(RHE.rearrange("p (s h) -> p s h", s=16),
                 A16.unsqueeze(1).to_broadcast([128, 16, 16]), RM.unsqueeze(2).to_broadcast([128, 16, 16]), ALU.mult)
            P.mm(pss[:, 32:32 + 256], ONES, RHE)
        P.actf(EALB[:, 0:nseq * 16], pss[:, 32:32 + nseq * 16], AF.Exp)
        EALB3 = EALB[:, 0:nseq * 16].rearrange("p (s h) -> p s h", s=nseq)

        def t_xs(g):
            tr4(G(XSb, g), [CV3[:, 24 + 4 * g + bb, :] for bb in range(4)])

        def t_decay(g):
            if not emit_out:
                return
            P.tt(v3(T1[g], 8), bm(UI[ty]), bh(A16[:, g * 8:(g + 1) * 8]), ALU.mult)
            for hf in range(2):
                ph = phalf()
                P.mm(ph, MS[ty], T1[g][:, hf * 512:(hf + 1) * 512])
                P.actf(DXT[g][:, hf * 512:(hf + 1) * 512], ph, AF.Exp)

        def t_cb(g):
            if not emit_out:
                return
            ph = phalf()
            P.mm(ph[:, 0:128], CV3[:, 32 + g, :], CV3[:, 34 + g, :])
            cbm = CBM[:, g * 128:(g + 1) * 128]
            P.tt(cbm, ph[:, 0:128], UI[ty], ALU.mult)
            for hf in range(2):
                d_ = v3(DXT[g][:, hf * 512:(hf + 1) * 512], 4)
                P.tt(d_, d_, bm(cbm, 4), ALU.mult)

        def t_xdt(g):
            P.tt(G8h(XDT, g), G8h(XSb, g), h8(DT16, g), ALU.mult)
            P.tt(G8h(XW, g), G8h(XDT, g), h8(EAL, g), ALU.mult)

        def t_ydiag(g):
            if not emit_out:
                return
            ph = phalf()
            for hh in range(8):
                h = 8 * g + hh
                P.mm(ph[:, hh * 64:(hh + 1) * 64], DXT[g][:, hh * 128:(hh + 1) * 128], XDT[:, h * 64:(h + 1) * 64])
            P.cp(G(Yb, g), ph, eng='act')

        def t_state(g):
            pso = PSO[g]
            for s in range(nseq):
                if ty == 'p':
                    H_s = HT
                else:
                    hin = [Bf[12], Bf[16]][s % 2]
                    P.dma(v3(G(hin, g), 4),
                          st_s[s, 8 * g:8 * g + 8].rearrange("(b h2) p n -> (h2 p) b n", h2=2), q='pool')
                    H_s = [Bf[8], Bf[19]][s % 2]
                    tr4(G(H_s, g), [hin[:, (4 * g + bb) * 128:(4 * g + bb + 1) * 128] for bb in range(4)])
                if emit_out:
                    for bb in range(4):
                        b = 4 * g + bb
                        P.mm(pso[:, bb * 128 + s * L:bb * 128 + (s + 1) * L], H_s[:, b * 128:(b + 1) * 128],
                             CV3[:, 34 + g, s * L:(s + 1) * L])
                if ty == 'p':
                    XZs, Hn = XW, HT
                else:
                    XZs, Hn = [Bf[10], Bf[21]][s % 2], [Bf[9], Bf[20]][s % 2]
                    P.ts(G(XZs, g), G(XW, g), RM[:, s:s + 1], None, ALU.mult)
                psn = phalf()
                P.mm(psn, BTM[:, g * 128:(g + 1) * 128], G(XZs, g))
                P.tt(G8h(Hn, g), G8h(H_s, g), bh(EALB3[:, s, 8 * g:8 * g + 8], 8, 64), ALU.mult)
                P.tt(G(Hn, g), G(Hn, g), psn, ALU.add)
                if ty == 's':
                    hout = [Bf[17], Bf[18]][s % 2]
                    tr4(G(hout, g), [Hn[:, (4 * g + bb) * 128:(4 * g + bb + 1) * 128] for bb in range(4)])
                    out_dmas.append(P.dma(sst_s[s, 8 * g:8 * g + 8].rearrange("(b h2) p n -> (h2 p) b n", h2=2),
                                          v3(G(hout, g), 4)))
            if emit_out:
                P.cp(G(YOT, g), pso, eng='act')
            if t == 7:
                P.ts(G(HT, g), G(HT, g), FLAG, None, ALU.mult)
            if t == 15:
                tr4(G(HNAT, g), [HT[:, (4 * g + bb) * 128:(4 * g + bb + 1) * 128] for bb in range(4)])
                out_dmas.append(P.dma(sst_p[8 * g:8 * g + 8].rearrange("(b h2) p n -> (h2 p) b n", h2=2),
                                      v3(G(HNAT, g), 4)))

        def t_y(g):
            if not emit_out:
                return
            ph = phalf()
            for bb in range(4):
                b = 4 * g + bb
                P.tr(ph[:, bb * 128:(bb + 1) * 128], YOT[:, b * 128:(b + 1) * 128], ID)
            Tg = T1[g][:, 0:512]
            T8 = v3(Tg, 8)
            P.tt(T8, v3(ph, 8), h8(EA, g), ALU.mult)
            P.tt(G(Yb, g), G(Yb, g), Tg, ALU.add)
            P.tt(T8, G8h(XSb, g), h8(SD, g), ALU.mult)
            P.tt(G(Yb, g), G(Yb, g), Tg, ALU.add)
            P.tt(G(Yb, g), G(Yb, g), TM[:, 1024 + g * 512:1024 + (g + 1) * 512], ALU.mult)
            P.tt(Tg, G(Yb, g), G(Yb, g), ALU.mult)
            s2 = SS2[:, g:g + 1]
            P.red(s2, Tg)
            P.actf(s2, s2, AF.Ln, scale=1.0 / 512, bias=EPS)
            P.actf(s2, s2, AF.Exp, scale=-0.5)
            P.ts(G(Yb, g), G(Yb, g), s2, None, ALU.mult)
            P.tt(MIX[:, 1024 + g * 512:1024 + (g + 1) * 512], G(Yb, g), SNW[:, g * 512:(g + 1) * 512], ALU.mult)

        lock([t_xs, t_decay, t_xdt, t_cb, t_ydiag, t_state, t_y])
        if emit_out:
            for kq in range(4):
                ph = phalf()
                for j in range(4):
                    kc = kq * 4 + j
                    P.tr(ph[:, j * 128:(j + 1) * 128], MIX[:, kc * 128:(kc + 1) * 128], ID)
                evac(MIXT3[:, kq * 4:kq * 4 + 4, ot * 128:(ot + 1) * 128], v3(ph, 4))
    P.fence()

    A.top = persist_top + 8 * NOUT
    WOC = [A.bf16(16 * 512) for _ in range(2)]
    r1_end = A.top
    X1 = A.f32(9 * 2048)
    X13 = v3(X1, 9)
    H2T = A.bf16(16 * NOUT)
    H2T3 = v3(H2T, 16)
    XN2 = A.f32(2048)
    SQJ2 = A.bf16(2048)
    SSc = A.f32(1)
    RSc = A.f32(1)
    w_out_v = w_out.rearrange("(kc p) c -> p kc c", p=128)
    for ot in range(9):
        P.dma(X13[:, ot, :], xin[1024 + ot * 128:1024 + (ot + 1) * 128, :])
    for dq in range(4):
        woc = v3(WOC[dq % 2], 16)
        P.dma(woc, w_out_v[:, :, dq * 512:(dq + 1) * 512], q='pool')
        for ot in range(9):
            ph = phalf()
            for kc in range(16):
                P.mm(ph, MIXT3[:, kc, ot * 128:(ot + 1) * 128], woc[:, kc, :], start=(kc == 0), stop=(kc == 15))
            xs_ = X13[:, ot, dq * 512:(dq + 1) * 512]
            P.tt(xs_, xs_, ph, ALU.add)
    for ot in range(9):
        rms_to_hT(X13[:, ot, :], FNW, H2T3[:, :, ot * 128:(ot + 1) * 128], SQJ2, XN2, SSc, RSc)
    P.fence()
    A.top = persist_top
    NFB = 4
    FFT = A.bf16(NFB * NOUT)
    FFT3 = v3(FFT, NFB)
    WG = [A.bf16(16 * 128) for _ in range(2)]
    WU = [A.bf16(16 * 128) for _ in range(2)]
    WD = [A.bf16(NFB * 2048) for _ in range(2)]
    SIL = [A.f32(512) for _ in range(2)]
    assert A.top <= r1_end, (A.top, r1_end)
    w_gate_v = w_gate.rearrange("(kc p) c -> p kc c", p=128)
    w_up_v = w_up.rearrange("(kc p) c -> p kc c", p=128)
    tchunks = [(0, 512), (512, 512), (1024, 128)]
    groups = []
    b0 = 0
    while b0 < 44:
        nb = min(NFB, 44 - b0)
        groups.append((b0, nb))
        b0 += nb
    cnt = 0
    for gi, (b0, nb) in enumerate(groups):
        wd = WD[gi % 2]
        wd3 = v3(wd, NFB)
        P.dma(wd3[:, 0:nb, :], w_down[b0 * 128:(b0 + nb) * 128, :].rearrange("(b p) d -> p b d", p=128), q='pool')
        for j in range(nb):
            fb = b0 + j
            wg = v3(WG[fb % 2], 16)
            wu = v3(WU[fb % 2], 16)
            P.dma(wg, w_gate_v[:, :, fb * 128:(fb + 1) * 128], q='pool')
            P.dma(wu, w_up_v[:, :, fb * 128:(fb + 1) * 128], q='pool')
            for (t0, n) in tchunks:
                pg = phalf()
                for kc in range(16):
                    P.mm(pg[:, 0:n], wg[:, kc, :], H2T3[:, kc, t0:t0 + n], start=(kc == 0), stop=(kc == 15))
                pu = phalf()
                for kc in range(16):
                    P.mm(pu[:, 0:n], wu[:, kc, :], H2T3[:, kc, t0:t0 + n], start=(kc == 0), stop=(kc == 15))
                sl_ = SIL[cnt % 2]
                cnt += 1
                P.actf(sl_[:, 0:n], pg[:, 0:n], AF.Silu)
                P.tt(FFT3[:, j, t0:t0 + n], sl_[:, 0:n], pu[:, 0:n], ALU.mult)
        for ot in range(9):
            for dq in range(4):
                ph = phalf()
                for j in range(nb):
                    P.mm(ph, FFT3[:, j, ot * 128:(ot + 1) * 128], wd3[:, j, dq * 512:(dq + 1) * 512],
                         start=(j == 0), stop=(j == nb - 1))
                xs_ = X13[:, ot, dq * 512:(dq + 1) * 512]
                P.tt(xs_, xs_, ph, ALU.add)
    YB = [WD[0].bitcast(F32)[:, 0:2048], WD[1].bitcast(F32)[:, 0:2048]]
    for ot in range(9):
        xt = X13[:, ot, :]
        yb = YB[ot % 2]
        P.actf(XN2, xt, AF.Square, accum=SSc)
        P.actf(RSc, SSc, AF.Sqrt, scale=1.0 / D, bias=EPS)
        P.add('dve', lambda e: e.reciprocal(RSc, RSc), r=[RSc], w=[RSc])
        P.stt(yb, xt, RSc, FINW, ALU.mult, ALU.mult)
        out_dmas.append(P.dma(y_o[ot * 128:(ot + 1) * 128, :], yb))
    stats = P.emit(final_wait_ops=out_dmas)
    es.close()
    return nc, stats


_CACHE = {}


def kernel(x_prompt, x_sample, state_gdn_conv, state_gdn, state_ssm_conv, state_ssm,
           attn_norm_w, w_in, gdn_conv_w, gdn_A_log, gdn_dt_bias, gdn_norm_w,
           ssm_conv_w, ssm_conv_b, ssm_A_log, ssm_dt_bias, ssm_D, ssm_norm_w,
           w_out, ffn_norm_w, w_gate, w_up, w_down, final_norm_w):
    f = lambda a: np.ascontiguousarray(np.asarray(a, dtype=np.float32))
    x_prompt = f(x_prompt)
    x_sample = f(x_sample)
    if 'nc' not in _CACHE:
        _CACHE['nc'] = build_program()[0]
    nc = _CACHE['nc']
    shared = dict(
        attn_norm_w=f(attn_norm_w).reshape(16, 128), w_in=f(w_in)[0], gdn_conv_w=f(gdn_conv_w)[0],
        gdn_A_log=f(gdn_A_log).reshape(1, 8), gdn_dt_bias=f(gdn_dt_bias).reshape(1, 8),
        gdn_norm_w=f(gdn_norm_w).reshape(1, 128), ssm_conv_w=f(ssm_conv_w)[0],
        ssm_conv_b=f(ssm_conv_b).reshape(1, 1536), ssm_A_log=f(ssm_A_log).reshape(1, 16),
        ssm_dt_bias=f(ssm_dt_bias).reshape(1, 16), ssm_D=f(ssm_D).reshape(1, 16),
        ssm_norm_w=f(ssm_norm_w).reshape(1, 1024), w_out=f(w_out)[0],
        ffn_norm_w=f(ffn_norm_w).reshape(16, 128), w_gate=f(w_gate)[0], w_up=f(w_up)[0],
        w_down=f(w_down)[0], final_norm_w=f(final_norm_w).reshape(1, D))
    sgc = f(state_gdn_conv)[0]
    sg = f(state_gdn)[0]
    ssc = f(state_ssm_conv)[0]
    ssm = f(state_ssm)[0]
    in_maps = []
    for c in range(8):
        b, r = c // 2, c % 2
        xin = np.zeros((NT, D), np.float32)
        if r == 1:
            xin[0:1024] = x_prompt[b, 0:1024]
        xin[1024:2048] = x_prompt[b, r * 1024:(r + 1) * 1024]
        xin[2048:] = x_sample[16 * c:16 * c + 16].reshape(128, D)
        m = dict(shared)
        m.update(xin=xin, flag=np.full((128, 1), float(r), np.float32),
                 st_gconv=np.ascontiguousarray(sgc[16 * c:16 * c + 16].reshape(48, 3072)),
                 st_g=np.ascontiguousarray(sg[16 * c:16 * c + 16]),
                 st_sconv=np.ascontiguousarray(ssc[16 * c:16 * c + 16].reshape(48, 1536)),
                 st_s=np.ascontiguousarray(ssm[16 * c:16 * c + 16]))
        in_maps.append(m)
    res = run_bass_kernel_spmd(nc, in_maps, core_ids=list(range(8)))
    R = res.results
    y_prompt = np.zeros((4, 2048, D), np.float32)
    y_sample = np.zeros((128, 8, D), np.float32)
    gconv_p = np.zeros((1, 4, 3, 3072), np.float32)
    gst_p = np.zeros((1, 4, 8, 128, 128), np.float32)
    sconv_p = np.zeros((1, 4, 3, 1536), np.float32)
    sst_p = np.zeros((1, 4, 16, 64, 128), np.float32)
    gconv_s = np.zeros((1, 128, 3, 3072), np.float32)
    gst_s = np.zeros((1, 128, 8, 128, 128), np.float32)
    sconv_s = np.zeros((1, 128, 3, 1536), np.float32)
    sst_s = np.zeros((1, 128, 16, 64, 128), np.float32)
    for c in range(8):
        b, r = c // 2, c % 2
        o = R[c]
        y_prompt[b, r * 1024:(r + 1) * 1024] = o['y'][0:1024]
        y_sample[16 * c:16 * c + 16] = o['y'][1024:].reshape(16, 8, D)
        if r == 1:
            gconv_p[0, b] = o['gconv_p']
            gst_p[0, b] = o['gst_p']
            sconv_p[0, b] = o['sconv_p']
            sst_p[0, b] = o['sst_p']
        gconv_s[0, 16 * c:16 * c + 16] = o['gconv_s'].reshape(16, 3, 3072)
        gst_s[0, 16 * c:16 * c + 16] = o['gst_s']
        sconv_s[0, 16 * c:16 * c + 16] = o['sconv_s'].reshape(16, 3, 1536)
        sst_s[0, 16 * c:16 * c + 16] = o['sst_s']
    return (y_prompt, y_sample, gconv_p, gst_p, sconv_p, sst_p, gconv_s, gst_s, sconv_s, sst_s)
```

```python
import numpy as np
import concourse.bass as bass
import concourse.mybir as mybir

F32 = mybir.dt.float32
BF16 = mybir.dt.bfloat16
ALU = mybir.AluOpType
AF = mybir.ActivationFunctionType
AX = mybir.AxisListType

_ES = {F32: 4, BF16: 2}


def _esize(dt):
    if dt in _ES:
        return _ES[dt]
    s = str(dt)
    if '32' in s:
        return 4
    if '16' in s:
        return 2
    if '64' in s:
        return 8
    return 1


def footprint(ap):
    name = ap.tensor.name
    dims = ap.ap
    es = _esize(ap.dtype)
    off = ap.offset
    space = str(ap.space)
    if 'DRAM' in space.upper() or 'HBM' in space.upper() or 'Dram' in space:
        ext = 1
        for st, cnt in dims:
            ext += (cnt - 1) * abs(st)
        return (name, True, 0, 1, off * es, (off + ext) * es)
    pst, pcnt = dims[0]
    if pst == 0:
        p0 = 0
        lo = off
        pcnt_eff = 1
    else:
        p0 = off // pst
        lo = off % pst
        pcnt_eff = pcnt
    ext = 1
    for st, cnt in dims[1:]:
        ext += (cnt - 1) * abs(st)
    lo_b, hi_b = lo * es, (lo + ext) * es
    if name.startswith('pw'):
        lo_b = (lo_b // 2048) * 2048
        hi_b = ((hi_b + 2047) // 2048) * 2048
        return (name, False, 0, 128, lo_b, hi_b)
    return (name, False, p0, p0 + pcnt_eff, lo_b, hi_b)


class Op:
    __slots__ = ('eng', 'fn', 'deps', 'dma', 'token', 'prewait', 'needed', 'idx')


class Prog:
    ENGS = ('pe', 'act', 'dve', 'pool', 'sp')
    NS = 8

    def __init__(self, nc):
        self.nc = nc
        self.ops = []
        self.acc = {}
        self.last_on_eng = {}

    def _track(self, idx, aps_r, aps_w):
        deps = set()
        for is_w, aps in ((False, aps_r), (True, aps_w)):
            for ap in aps:
                name, isd, p0, p1, lo, hi = footprint(ap)
                lst = self.acc.setdefault(name, [])
                keep = []
                for e in lst:
                    ov = not (e[2] <= p0 or e[1] >= p1 or e[4] <= lo or e[3] >= hi)
                    if ov and (is_w or e[5]):
                        if e[0] != idx:
                            deps.add(e[0])
                        if is_w and e[1] >= p0 and e[2] <= p1 and e[3] >= lo and e[4] <= hi:
                            continue
                    keep.append(e)
                keep.append([idx, p0, p1, lo, hi, is_w])
                self.acc[name] = keep
        return deps

    def add(self, eng, fn, r=(), w=(), dma=False, extra_deps=()):
        op = Op()
        op.eng = eng
        op.fn = fn
        op.dma = dma
        op.idx = len(self.ops)
        op.deps = self._track(op.idx, r, w)
        op.deps.update(extra_deps)
        op.token = None
        op.prewait = None
        op.needed = False
        self.ops.append(op)
        self.last_on_eng[eng] = op.idx
        return op.idx

    def fence(self):
        last = []
        for e in self.ENGS:
            pass
        idxs = set()
        seen_eng = set()
        for op in reversed(self.ops):
            if op.dma:
                idxs.add(op.idx)
            elif op.eng not in seen_eng:
                seen_eng.add(op.eng)
                idxs.add(op.idx)
        lf = getattr(self, '_last_fence', 0)
        idxs = {i for i in idxs if i >= lf or not self.ops[i].dma}
        for e in self.ENGS:
            self.add(e, None, extra_deps=set(idxs))
        self._last_fence = len(self.ops)

    def mm(self, out, lhsT, rhs, start=True, stop=True):
        return self.add('pe', lambda e: e.matmul(out, lhsT, rhs, start=start, stop=stop),
                        r=[lhsT, rhs], w=[out])

    def tr(self, out, in_, ident):
        return self.add('pe', lambda e: e.transpose(out, in_, ident), r=[in_, ident], w=[out])

    def actf(self, out, in_, func, bias=None, scale=None, accum=None, eng='act'):
        kw = {}
        r = [in_]
        w = [out]
        if bias is not None:
            kw['bias'] = bias
            if not isinstance(bias, (int, float)):
                r.append(bias)
        if scale is not None:
            kw['scale'] = scale
            if not isinstance(scale, (int, float)):
                r.append(scale)
        if accum is not None:
            kw['accum_out'] = accum
            w.append(accum)
        return self.add(eng, lambda e: e.activation(out, in_, func, **kw), r=r, w=w)

    def tt(self, out, a, b, op, eng='dve'):
        return self.add(eng, lambda e: e.tensor_tensor(out, a, b, op), r=[a, b], w=[out])

    def ts(self, out, a, s1, s2, op0, op1=None, eng='dve', accum=None):
        r = [a]
        if not isinstance(s1, (int, float)):
            r.append(s1)
        if s2 is not None and not isinstance(s2, (int, float)):
            r.append(s2)
        w = [out]
        kw = {}
        if accum is not None:
            kw['accum_out'] = accum
            w.append(accum)
        if op1 is None:
            if isinstance(s1, (int, float)):
                return self.add(eng, lambda e: e.tensor_scalar(out, a, s1, None, op0, **kw), r=r, w=w)
            return self.add(eng, lambda e: e.tensor_scalar(out, a, s1, 0.0, op0, ALU.add, **kw), r=r, w=w)
        return self.add(eng, lambda e: e.tensor_scalar(out, a, s1, s2, op0, op1, **kw), r=r, w=w)

    def stt(self, out, a, s, b, op0, op1, eng='dve'):
        r = [a, b]
        if not isinstance(s, (int, float)):
            r.append(s)
        return self.add(eng, lambda e: e.scalar_tensor_tensor(out, a, s, b, op0, op1), r=r, w=[out])

    def cp(self, out, in_, eng='dve'):
        if eng == 'act':
            return self.add('act', lambda e: e.copy(out, in_), r=[in_], w=[out])
        return self.add(eng, lambda e: e.tensor_copy(out, in_), r=[in_], w=[out])

    def red(self, out, in_, op=None, eng='dve'):
        op = op or ALU.add
        return self.add(eng, lambda e: e.tensor_reduce(out, in_, AX.X, op), r=[in_], w=[out])

    def memset(self, ap, val, eng='pool'):
        return self.add(eng, lambda e: e.memset(ap, val), w=[ap])

    def dma(self, out, in_, q='sp'):
        return self.add(q, lambda e: e.dma_start(out, in_), r=[in_], w=[out], dma=True)

    def emit(self, final_wait_ops=()):
        nc = self.nc
        ops = self.ops
        if final_wait_ops:
            self.add('sp', None, extra_deps=set(final_wait_ops))
        for op in ops:
            for d in op.deps:
                dop = ops[d]
                if dop.dma:
                    continue
                if dop.eng == 'pe' and op.eng == 'pe' and not op.dma:
                    continue
                dop.needed = True
        from contextlib import ExitStack
        es = ExitStack()
        sem = {e: es.enter_context(nc.semaphore('s_' + e)) for e in self.ENGS}
        dsem = {e: [es.enter_context(nc.semaphore('d_%s%d' % (e, i))) for i in range(self.NS)]
                for e in ('sp', 'act', 'pool')}
        cnt = {e: 0 for e in self.ENGS}
        dcnt = {e: 0 for e in dsem}
        for op in ops:
            if op.fn is None:
                continue
            if op.dma:
                m = dcnt[op.eng]
                s = dsem[op.eng][m % self.NS]
                op.token = (s, 16 * (m // self.NS + 1), 16)
                if m >= self.NS:
                    op.prewait = (s, 16 * (m // self.NS))
                dcnt[op.eng] = m + 1
            elif op.needed:
                cnt[op.eng] += 1
                op.token = (sem[op.eng], cnt[op.eng], 1)
        per_eng = {e: [op for op in ops if op.eng == e] for e in self.ENGS}
        stats = {e: [0, 0] for e in self.ENGS}

        def run(ename, eobj):
            known = {}
            for op in per_eng[ename]:
                waits = {}
                if op.prewait is not None:
                    s, v = op.prewait
                    waits[id(s)] = (s, v)
                for d in op.deps:
                    dop = ops[d]
                    if dop.token is None:
                        continue
                    if (not dop.dma) and dop.eng == 'pe' and ename == 'pe' and not op.dma:
                        continue
                    s, v, _ = dop.token
                    if id(s) not in waits or waits[id(s)][1] < v:
                        waits[id(s)] = (s, v)
                for k, (s, v) in waits.items():
                    if known.get(k, 0) >= v:
                        continue
                    eobj.wait_ge(s, v)
                    stats[ename][1] += 1
                    known[k] = v
                if op.fn is None:
                    continue
                ins = op.fn(eobj)
                stats[ename][0] += 1
                if op.token is not None:
                    ins.then_inc(op.token[0], op.token[2])

        with nc.Block() as block:
            @block.tensor
            def _(e):
                run('pe', e)

            @block.scalar
            def _(e):
                run('act', e)

            @block.vector
            def _(e):
                run('dve', e)

            @block.gpsimd
            def _(e):
                run('pool', e)

            @block.sync
            def _(e):
                run('sp', e)
        es.close()
        return stats

from contextlib import ExitStack
from concourse.bass_utils import run_bass_kernel_spmd

D = 2048
DIN = 6688
DFF = 5632
NT = 2176
NTILE = 17
NOUT = 1152
EPSV = 1e-6
PFW = 3 + NT
TMW = 2080


class Arena:
    def __init__(self, ar, total):
        self.ar = ar
        self.top = 0
        self.total = total

    def f32(self, n):
        o = self.top
        self.top += n
        assert self.top <= self.total, (self.top, self.total)
        return self.ar[:, o:o + n]

    def bf16(self, n):
        assert n % 2 == 0
        return self.f32(n // 2).bitcast(BF16)

    def at(self, o, n):
        return self.ar[:, o:o + n]


def v3(ap, a):
    return ap.rearrange("p (a b) -> p a b", a=a)


def build_program(debug=False):
    nc = bass.Bass("TRN2", target_bir_lowering=False)

    def din(name, shape):
        return nc.dram_tensor(name, list(shape), F32, kind="ExternalInput").ap()

    def dout(name, shape):
        return nc.dram_tensor(name, list(shape), F32, kind="ExternalOutput").ap()

    xin = din("xin", [NT, D])
    flag_d = din("flag", [128, 1])
    st_gconv = din("st_gconv", [48, 3072])
    st_g = din("st_g", [16, 8, 128, 128])
    st_sconv = din("st_sconv", [48, 1536])
    st_s = din("st_s", [16, 16, 64, 128])
    attn_norm_w = din("attn_norm_w", [16, 128])
    w_in = din("w_in", [D, DIN])
    gdn_conv_w = din("gdn_conv_w", [4, 3072])
    gdn_A_log = din("gdn_A_log", [1, 8])
    gdn_dt_bias = din("gdn_dt_bias", [1, 8])
    gdn_norm_w = din("gdn_norm_w", [1, 128])
    ssm_conv_w = din("ssm_conv_w", [4, 1536])
    ssm_conv_b = din("ssm_conv_b", [1, 1536])
    ssm_A_log = din("ssm_A_log", [1, 16])
    ssm_dt_bias = din("ssm_dt_bias", [1, 16])
    ssm_D = din("ssm_D", [1, 16])
    ssm_norm_w = din("ssm_norm_w", [1, 1024])
    w_out = din("w_out", [D, D])
    ffn_norm_w = din("ffn_norm_w", [16, 128])
    w_gate = din("w_gate", [D, DFF])
    w_up = din("w_up", [D, DFF])
    w_down = din("w_down", [DFF, D])
    final_norm_w = din("final_norm_w", [1, D])

    y_o = dout("y", [NOUT, D])
    gconv_p = dout("gconv_p", [3, 3072])
    gst_p = dout("gst_p", [8, 128, 128])
    sconv_p = dout("sconv_p", [3, 1536])
    sst_p = dout("sst_p", [16, 64, 128])
    gconv_s = dout("gconv_s", [48, 3072])
    gst_s = dout("gst_s", [16, 8, 128, 128])
    sconv_s = dout("sconv_s", [48, 1536])
    sst_s = dout("sst_s", [16, 16, 64, 128])

    P_fm = nc.dram_tensor("P_fm", [36, 128, PFW], F32).ap()
    P_tm = nc.dram_tensor("P_tm", [NT, TMW], F32).ap()
    P_cv = nc.dram_tensor("P_cv", [36, 128, 2048], F32).ap()

    es = ExitStack()
    TOT = 53200
    ar_t = es.enter_context(nc.sbuf_tensor("arena", [128, TOT], F32))
    PW = [es.enter_context(nc.psum_tensor("pw%d" % i, [128, 1024], F32)) for i in range(4)]
    A = Arena(ar_t, TOT)
    P = Prog(nc)
    pwc = [0]

    def pw():
        pwc[0] += 1
        return PW[pwc[0] % 3]
    pw_global = pw

    hb = [0]

    def phalf():
        hb[0] += 1
        k = hb[0] % 6
        return PW[k // 2][:, (k % 2) * 512:(k % 2) * 512 + 512]

    ID = A.f32(128)
    ONES = A.f32(128)
    MI = {}
    MS = {}
    UI = {}
    for ty in ('p', 's'):
        MI[ty] = A.f32(128)
        MS[ty] = A.f32(128)
        UI[ty] = A.f32(128)
    BD = A.f32(128)
    RM = A.f32(16)
    EPS = A.f32(1)
    FLAG = A.f32(1)
    ANW = A.f32(16)
    FNW = A.f32(16)
    FINW = A.f32(2048)
    GNW = A.f32(128)
    SNW = A.f32(1024)
    SD = A.f32(16)
    NAG = A.f32(8)
    GDB = A.f32(8)
    NAS = A.f32(16)
    SDB = A.f32(16)
    CW = A.f32(36 * 5)
    CW3 = v3(CW, 36)
    ZERO = A.f32(128)
    LNDK = A.f32(1)

    def asel(ap, pattern, op, fill, base, cm):
        P.add('pool', lambda e: e.affine_select(ap, ap, pattern, op, fill, base=base, channel_multiplier=cm),
              r=[ap], w=[ap])

    P.memset(ID, 0.0)
    asel(ID, [[-1, 128]], ALU.not_equal, 1.0, 0, 1)
    P.memset(ONES, 1.0)
    P.memset(ZERO, 0.0)
    P.memset(EPS, EPSV)
    P.memset(LNDK, float(np.log(128 ** -0.5)))
    P.memset(BD, 1.0)
    asel(v3(BD, 16), [[-8, 16], [0, 8]], ALU.is_ge, 0.0, 0, 1)
    asel(v3(BD, 16), [[8, 16], [0, 8]], ALU.is_ge, 0.0, 7, -1)
    P.memset(RM, 1.0)
    asel(RM, [[-8, 16]], ALU.is_ge, 0.0, 0, 1)
    asel(RM, [[8, 16]], ALU.is_ge, 0.0, 7, -1)
    for ty in ('p', 's'):
        P.memset(MI[ty], 1.0)
        asel(MI[ty], [[-1, 128]], ALU.is_ge, 0.0, 0, 1)
        P.memset(MS[ty], 1.0)
        asel(MS[ty], [[-1, 128]], ALU.is_gt, 0.0, 0, 1)
        P.memset(UI[ty], 1.0)
        asel(UI[ty], [[1, 128]], ALU.is_ge, 0.0, 0, -1)
        if ty == 's':
            for m in (MI, MS, UI):
                P.tt(m[ty], m[ty], BD, ALU.mult, eng='pool')
    P.dma(FLAG, flag_d)
    P.dma(FINW, final_norm_w.broadcast_to([128, D]))
    P.dma(GNW, gdn_norm_w.broadcast_to([128, 128]))
    P.dma(SNW, ssm_norm_w.broadcast_to([128, 1024]))
    P.dma(SD, ssm_D.broadcast_to([128, 16]))
    P.dma(NAG, gdn_A_log.broadcast_to([128, 8]))
    P.dma(GDB, gdn_dt_bias.broadcast_to([128, 8]))
    P.dma(NAS, ssm_A_log.broadcast_to([128, 16]))
    P.dma(SDB, ssm_dt_bias.broadcast_to([128, 16]))
    P.actf(NAG, NAG, AF.Exp)
    P.ts(NAG, NAG, -1.0, None, ALU.mult)
    P.actf(NAS, NAS, AF.Exp)
    P.ts(NAS, NAS, -1.0, None, ALU.mult)

    mark0 = A.top
    TMPA = A.f32(4608 + 256)
    cwst = TMPA[0:5, 0:4608]
    P.memset(TMPA[0:5, 0:4608], 0.0)
    P.dma(TMPA[0:4, 0:3072], gdn_conv_w)
    P.dma(TMPA[0:4, 3072:4608], ssm_conv_w)
    P.dma(TMPA[4:5, 3072:4608], ssm_conv_b)
    P.dma(TMPA[0:16, 4608:4736], attn_norm_w)
    P.dma(TMPA[0:16, 4736:4864], ffn_norm_w)
    ps = pw()
    for b in range(36):
        P.tr(ps[:, b * 5:b * 5 + 5], TMPA[0:5, b * 128:(b + 1) * 128], ID[0:5, 0:5])
    P.cp(CW, ps[:, 0:180], eng='act')
    ps = pw()
    P.tr(ps[:, 0:16], TMPA[0:16, 4608:4736], ID[0:16, 0:16])
    P.tr(ps[:, 16:32], TMPA[0:16, 4736:4864], ID[0:16, 0:16])
    P.cp(ANW, ps[:, 0:16], eng='act')
    P.cp(FNW, ps[:, 16:32], eng='act')
    P.fence()
    A.top = mark0
    persist_top = A.top

    def rms_to_hT(xt, nw, hT_dst, tmp_sq, xn, ss, rs):
        P.actf(tmp_sq, xt, AF.Square, accum=ss)
        P.actf(rs, ss, AF.Sqrt, scale=1.0 / D, bias=EPS)
        P.add('dve', lambda e: e.reciprocal(rs, rs), r=[rs], w=[rs])
        P.ts(xn, xt, rs, None, ALU.mult)
        for kq in range(4):
            ph = phalf()
            for j in range(4):
                kc = kq * 4 + j
                P.tr(ph[:, j * 128:(j + 1) * 128], xn[:, kc * 128:(kc + 1) * 128], ID)
            P.tt(hT_dst[:, kq * 4:kq * 4 + 4, :], v3(ph, 4),
                 nw[:, kq * 4:kq * 4 + 4].unsqueeze(2).to_broadcast([128, 4, 128]), ALU.mult)

    hT = A.bf16(16 * NT)
    hT3 = v3(hT, 16)
    XS = [A.f32(2048) for _ in range(2)]
    XN = A.f32(2048)
    SQJ = A.bf16(2048)
    SS = A.f32(1)
    RS = A.f32(1)
    for t in range(NTILE):
        xt = XS[t % 2]
        P.dma(xt, xin[t * 128:(t + 1) * 128, :])
        rms_to_hT(xt, ANW, hT3[:, :, t * 128:(t + 1) * 128], SQJ, XN, SS, RS)

    Z3 = A.f32(36 * 3)
    P.memset(Z3, 0.0)
    P.dma(P_fm.rearrange("b p t -> p b t")[:, :, 0:3], v3(Z3, 36))

    WB = [A.bf16(16 * 128) for _ in range(3)]
    STG = [A.f32(3 + NT) for _ in range(2)]
    CVS = [A.f32(2048) for _ in range(2)]
    for st_ in STG:
        P.memset(st_[:, 0:3], 0.0)
    w_in_v = w_in.rearrange("(kc p) c -> p kc c", p=128)
    chunks = [(0, 512), (512, 512), (1024, 512), (1536, 512), (2048, 128)]
    ev = [0]

    def evac(dst, src):
        ev[0] += 1
        P.cp(dst, src, eng='act' if ev[0] % 2 else 'dve')

    for cb in range(36):
        c0 = cb * 128 if cb < 24 else 5136 + (cb - 24) * 128
        W = WB[cb % 3]
        W3 = v3(W, 16)
        P.dma(W3, w_in_v[:, :, c0:c0 + 128], q='pool')
        stg = STG[cb % 2]
        cks = chunks if cb >= 8 else [(896, 128), (1024, 512), (1536, 512), (2048, 128)]
        for (t0, n) in cks:
            ph = phalf()
            for kc in range(16):
                P.mm(ph[:, 0:n], W3[:, kc, :], hT3[:, kc, t0:t0 + n], start=(kc == 0), stop=(kc == 15))
            evac(stg[:, 3 + t0:3 + t0 + n], ph[:, 0:n])
        tb = cks[0][0]
        P.dma(P_fm[cb, :, 3 + tb:3 + NT], stg[:, 3 + tb:3 + NT])
        tlo = 0 if cb >= 8 else 1024
        cvs = CVS[cb % 2]
        P.actf(cvs[:, tlo:2048], stg[:, tlo:2048], AF.Identity, scale=CW3[:, cb, 0:1], bias=CW3[:, cb, 4:5])
        for i in range(1, 4):
            P.stt(cvs[:, tlo:2048], stg[:, tlo + i:2048 + i], CW3[:, cb, i:i + 1], cvs[:, tlo:2048], ALU.mult, ALU.add)
        P.actf(cvs[:, tlo:2048], cvs[:, tlo:2048], AF.Silu)
        P.dma(P_cv[cb, :, tlo:2048], cvs[:, tlo:2048])

    WT = [A.bf16(16 * 512) for _ in range(2)]
    TS_ = [A.f32(512) for _ in range(2)]
    tmchunks = [(3072, 512, 0, 8), (3584, 512, 512, 8), (4112, 512, 1024, 8), (4624, 512, 1536, 8),
                (4096, 16, 2048, 0), (6672, 16, 2064, 0)]
    k = 0
    for i, (c0, ncol, pc0, t_first) in enumerate(tmchunks):
        W = WT[i % 2]
        W3 = v3(W, 16)
        P.dma(W3[:, :, 0:ncol], w_in_v[:, :, c0:c0 + ncol], q='pool')
        for t in range(t_first, NTILE):
            ph = phalf()
            for kc in range(16):
                P.mm(ph[:, 0:ncol], hT3[:, kc, t * 128:(t + 1) * 128], W3[:, kc, 0:ncol],
                     start=(kc == 0), stop=(kc == 15))
            ts_ = TS_[k % 2]
            k += 1
            evac(ts_[:, 0:ncol], ph[:, 0:ncol])
            P.dma(P_tm[t * 128:(t + 1) * 128, pc0:pc0 + ncol], ts_[:, 0:ncol])
    P.fence()

    A.top = persist_top
    MIXT = A.bf16(16 * NOUT)
    MIXT3 = v3(MIXT, 16)
    SG = A.f32(1024)
    HT = A.f32(1024)
    P.memset(SG, 0.0)
    P.memset(HT, 0.0)
    RAW = A.f32(36 * 176)
    CV = A.f32(36 * 128)
    RAWC = CV
    CV3 = v3(CV, 36)
    TM = A.f32(TMW)
    NB = 22
    bf0 = A.top
    Bf = [A.f32(1024) for _ in range(NB)]
    HST = A.at(bf0 + 17 * 1024, 4608)
    CST = HST
    MIX = A.at(bf0 + 13 * 1024, 2048)
    SMALL = A.f32(1024)
    SMALL2 = A.f32(160)
    SGS = [Bf[4], Bf[6]]
    SGO = [Bf[12], Bf[10]]
    out_dmas = []

    def sm(o, n):
        return SMALL[:, o:o + n]

    def gset(par):
        o = 80 * par
        names = (('BETA', 8), ('BETAN', 8), ('G8', 8), ('TMP8', 8), ('DT16', 16), ('A16', 16), ('TMP16', 16))
        d = {}
        for nm, n in names:
            d[nm] = SMALL2[:, o:o + n]
            o += n
        return d
    GS = [gset(0), gset(1)]

    def emit_gating(tt_):
        g_ = GS[tt_ % 2]
        TMs_ = sm(950 + 32 * (tt_ % 2), 32)
        P.dma(TMs_, P_tm[tt_ * 128:(tt_ + 1) * 128, 2048:TMW])
        BETA_, BETAN_, G8_, TMP8_, DT16_, A16_, TMP16_ = (g_[k] for k in ('BETA', 'BETAN', 'G8', 'TMP8', 'DT16', 'A16', 'TMP16'))
        P.actf(BETA_, TMs_[:, 0:8], AF.Exp, scale=-1.0)
        P.ts(BETA_, BETA_, 1.0, None, ALU.add)
        P.add('dve', lambda e: e.reciprocal(BETA_, BETA_), r=[BETA_], w=[BETA_])
        P.ts(BETAN_, BETA_, -1.0, None, ALU.mult)
        P.tt(TMP8_, TMs_[:, 8:16], GDB, ALU.add)
        P.actf(TMP8_, TMP8_, AF.Exp)
        P.actf(TMP8_, TMP8_, AF.Ln, bias=1.0)
        P.tt(G8_, TMP8_, NAG, ALU.mult)
        P.tt(TMP16_, TMs_[:, 16:32], SDB, ALU.add)
        P.actf(TMP16_, TMP16_, AF.Exp)
        P.actf(DT16_, TMP16_, AF.Ln, bias=1.0)
        P.tt(A16_, DT16_, NAS, ALU.mult)
    EGK = sm(80, 16)
    EG = sm(80, 8)
    EKD = sm(88, 8)
    BEG = sm(96, 8)
    RQ = sm(104, 8)
    RK = sm(112, 8)
    SSQ = sm(120, 8)
    EAA = sm(128, 32)
    EA = sm(128, 16)
    EAL = sm(144, 16)
    SSO = sm(160, 8)
    SS2 = sm(168, 2)
    EGL = sm(176, 128)
    EALB = sm(304, 256)
    RHE = sm(560, 256)
    SSK = sm(816, 8)

    def bh(ap, n=8, w=128):
        return ap.unsqueeze(2).to_broadcast([128, n, w])

    def bm(ap, n=8, w=128):
        return ap.unsqueeze(1).to_broadcast([128, n, w])

    DK_SCALE = 128 ** -0.5

    for t in range(NTILE):
        ty = 's' if t == 16 else 'p'
        nseq = 16 if ty == 's' else 1
        L = 8 if ty == 's' else 128
        emit_out = t >= 8
        ot = t - 8
        CVt = CV if t % 2 == 0 else RAW[:, 0:36 * 128]
        CV3 = v3(CVt, 36)
        if ty == 'p':
            b_lo = 0 if emit_out else 8
            P.dma(CV3[:, b_lo:36, :], P_cv[b_lo:36, :, t * 128:(t + 1) * 128].rearrange("b p t -> p b t"))
            if t == 15:
                RAW3 = v3(sm(840, 108), 36)
                P.dma(RAW3, P_fm[:, :, 3 + 2045:3 + 2048].rearrange("b p t -> p b t"))
        else:
            RAW4 = RAW.rearrange("p (b s t) -> p b s t", b=36, s=16)
            P.dma(v3(RAWC, 36), P_fm[:, :, 3 + 2048:3 + NT].rearrange("b p t -> p b t"))
            P.dma(HST[0:48, 0:3072], st_gconv)
            P.dma(HST[0:48, 3072:4608], st_sconv)
            for b0 in range(0, 36, 8):
                nb = min(8, 36 - b0)
                ps = pw()
                for b in range(nb):
                    P.tr(ps[:, b * 48:(b + 1) * 48], HST[0:48, (b0 + b) * 128:(b0 + b + 1) * 128], ID[0:48, 0:48])
                P.cp(RAW4[:, b0:b0 + nb, :, 0:3],
                     ps[:, 0:nb * 48].rearrange("p (b s t) -> p b s t", b=nb, s=16), eng='act')
            P.cp(RAW4[:, :, :, 3:11], RAWC.rearrange("p (b s t) -> p b s t", b=36, s=16), eng='pool')
        if emit_out:
            P.dma(TM[:, 0:2048], P_tm[t * 128:(t + 1) * 128, 0:2048])
        cblocks = list(range(36)) if ty == 's' else []
        NPOOL = 8
        pool_blocks = cblocks[-NPOOL:]
        dve_blocks = cblocks[:-NPOOL]
        CTMP = v3(Bf[21], 8)
        if ty == 's':
            NPOOL = 8

        def cio(b):
            if ty == 'p':
                return CV3[:, b, :], [RAW3[:, b, i:i + 128] for i in range(4)], (lambda a: a)
            return (CV3[:, b, :].rearrange("p (s t) -> p s t", s=16), [RAW4[:, b, :, i:i + 8] for i in range(4)],
                    (lambda a: a.rearrange("p (s t) -> p s t", s=16)))
        for b in cblocks:
            o_, ins, _ = cio(b)
            P.actf(o_, ins[0], AF.Identity, scale=CW3[:, b, 0:1], bias=CW3[:, b, 4:5])
        for i in range(1, 4):
            for b in dve_blocks:
                o_, ins, _ = cio(b)
                P.stt(o_, ins[i], CW3[:, b, i:i + 1], o_, ALU.mult, ALU.add)
            for j, b in enumerate(pool_blocks):
                o_, ins, vw = cio(b)
                P.ts(vw(CTMP[:, j, :]), ins[i], CW3[:, b, i:i + 1], None, ALU.mult, eng='pool')
            for j, b in enumerate(pool_blocks):
                o_, ins, vw = cio(b)
                P.tt(o_, o_, vw(CTMP[:, j, :]), ALU.add, eng='pool')
        if ty == 's':
            P.actf(CV, CV, AF.Silu)
        if t == 15 or ty == 's':
            nr = 3 if ty == 'p' else 48
            if ty == 's':
                LST = A.at(bf0 + 15 * 1024, 36 * 48)
                P.cp(LST.rearrange("p (b s t) -> p b s t", b=36, s=16), RAW4[:, :, :, 8:11], eng='pool')
                LST3 = v3(LST, 36)
            for b0 in range(0, 36, 4):
                ph = phalf()
                for b in range(4):
                    src = RAW3[:, b0 + b, 0:3] if ty == 'p' else LST3[:, b0 + b, :]
                    P.tr(ph[0:nr, b * 128:(b + 1) * 128], src, ID)
                P.cp(CST[0:nr, b0 * 128:(b0 + 4) * 128], ph[0:nr, 0:512], eng='act')
            if ty == 'p':
                out_dmas.append(P.dma(gconv_p, CST[0:3, 0:3072]))
                out_dmas.append(P.dma(sconv_p, CST[0:3, 3072:4608]))
            else:
                out_dmas.append(P.dma(gconv_s, CST[0:48, 0:3072]))
                out_dmas.append(P.dma(sconv_s, CST[0:48, 3072:4608]))
        if t == 0:
            emit_gating(0)
        BETA, BETAN, G8, DT16, A16 = (GS[t % 2][k] for k in ('BETA', 'BETAN', 'G8', 'DT16', 'A16'))
        Q, SQ, KN, V, RH, DX, DM, QNT, KNT, QDT, NT0, QKM, N0, QKMT, R, NA, NB_, UTb, WTb, KDEC, VNb, OTb = Bf
        QN = Q

        def cgs(g):
            return slice(g * 512, (g + 1) * 512)

        def G(buf, g):
            return buf[:, g * 512:(g + 1) * 512]

        def G4(buf, g):
            return v3(buf[:, g * 512:(g + 1) * 512], 4)

        def h4(ap8, g):
            return bh(ap8[:, 4 * g:4 * g + 4], 4)

        def m4(ap):
            return bm(ap, 4)

        def lock(stages):
            for st in stages:
                for g in (0, 1):
                    st(g)

        pss = phalf()
        P.mm(pss[:, 0:8], UI[ty], G8)
        P.mm(pss[:, 8:16], MS[ty], G8)
        P.actf(EGK, pss[:, 0:16], AF.Exp)
        if ty == 'p':
            P.mm(pss[:, 16:24], ONES, G8)
        else:
            P.tt(RHE[:, 0:128].rearrange("p (s h) -> p s h", s=16),
                 G8.unsqueeze(1).to_broadcast([128, 16, 8]), RM.unsqueeze(2).to_broadcast([128, 16, 8]), ALU.mult)
            P.mm(pss[:, 16:16 + 128], ONES, RHE[:, 0:128])
        P.actf(EGL[:, 0:nseq * 8], pss[:, 16:16 + nseq * 8], AF.Exp)
        EGL3 = EGL[:, 0:nseq * 8].rearrange("p (s h) -> p s h", s=nseq)
        P.tt(BEG, BETA, EG, ALU.mult)

        def tr4(dst, srcs):
            ph = phalf()
            for hh in range(4):
                P.tr(ph[:, hh * 128:(hh + 1) * 128], srcs[hh], ID)
            P.cp(dst, ph, eng='act')

        def s_qT(g):
            if emit_out:
                tr4(G(Q, g), [CV3[:, 4 * g + hh, :] for hh in range(4)])

        def s_kT(g):
            tr4(G(KN, g), [CV3[:, 8 + 4 * g + hh, :] for hh in range(4)])

        def s_vT(g):
            tr4(G(V, g), [CV3[:, 16 + 4 * g + hh, :] for hh in range(4)])

        def norm_(X_, R_, S_, sc, g):
            P.tt(G(SQ, g), G(X_, g), G(X_, g), ALU.mult)
            P.red(S_[:, 4 * g:4 * g + 4], G4(SQ, g))
            r_ = R_[:, 4 * g:4 * g + 4]
            P.actf(r_, S_[:, 4 * g:4 * g + 4], AF.Ln, bias=EPS)
            P.actf(r_, r_, AF.Exp, scale=-0.5, bias=(LNDK if sc != 1.0 else None))
            P.tt(G4(X_, g), G4(X_, g), h4(R_, g), ALU.mult)

        def s_qn(g):
            if emit_out:
                norm_(Q, RQ, SSQ, DK_SCALE, g)

        def s_kn(g):
            norm_(KN, RK, SSK, 1.0, g)

        def s_decay(g):
            P.tt(G4(RH, g), m4(MS[ty]), h4(G8, g), ALU.mult)
            ph = phalf()
            P.mm(ph, UI[ty], G(RH, g))
            P.actf(G(DX, g), ph, AF.Exp)
            if emit_out:
                P.tt(G4(DM, g), G4(DX, g), m4(MI[ty]), ALU.mult)
            P.tt(G4(DX, g), G4(DX, g), m4(MS[ty]), ALU.mult)
            P.tt(G4(DX, g), G4(DX, g), h4(BETAN, g), ALU.mult)

        def s_qnT(g):
            if emit_out:
                tr4(G(QNT, g), [QN[:, (4 * g + hh) * 128:(4 * g + hh + 1) * 128] for hh in range(4)])

        def s_knT(g):
            tr4(G(KNT, g), [KN[:, (4 * g + hh) * 128:(4 * g + hh + 1) * 128] for hh in range(4)])

        def s_qdT(g):
            if emit_out:
                P.tt(G4(SQ, g), G4(QN, g), h4(EG, g), ALU.mult)
                tr4(G(QDT, g), [SQ[:, (4 * g + hh) * 128:(4 * g + hh + 1) * 128] for hh in range(4)])

        def mm4(lh, rh, g):
            ph = phalf()
            for hh in range(4):
                sl = slice((4 * g + hh) * 128, (4 * g + hh + 1) * 128)
                P.mm(ph[:, hh * 128:(hh + 1) * 128], lh[:, sl], rh[:, sl])
            return ph

        def s_gram(g):
            ph = mm4(KNT, KNT, g)
            P.tt(G(NT0, g), ph, G(DX, g), ALU.mult)

        def s_qk(g):
            if emit_out:
                ph = mm4(QNT, KNT, g)
                P.tt(G(QKM, g), ph, G(DM, g), ALU.mult)

        def s_n0(g):
            tr4(G(N0, g), [NT0[:, (4 * g + hh) * 128:(4 * g + hh + 1) * 128] for hh in range(4)])
            P.tt(G4(R, g), G4(N0, g), m4(ID), ALU.add)

        def s_qkT(g):
            if emit_out:
                tr4(G(QKMT, g), [QKM[:, (4 * g + hh) * 128:(4 * g + hh + 1) * 128] for hh in range(4)])

        lock([s_kT, s_vT, s_qT, s_kn, s_decay, s_qn, s_knT, s_gram, s_qnT, s_n0, s_qdT, s_qk, s_qkT])
        if t + 1 < NTILE:
            emit_gating(t + 1)

        nsteps = 6 if ty == 'p' else 2
        Nc, NTc = N0, NT0
        pp = [(NA, NB_), (RH, DM)]
        for s_ in range(1, nsteps + 1):
            Nn, NTn = pp[s_ % 2]

            def d_sq(g, Nc=Nc, NTc=NTc, Nn=Nn, NTn=NTn, s_=s_):
                psb = mm4(Nc, NTc, g)
                P.cp(G(NTn, g), psb, eng='act')
                if s_ < nsteps:
                    psa = mm4(NTc, Nc, g)
                    P.cp(G(Nn, g), psa, eng='dve')

            def d_r(g, NTn=NTn):
                psc = mm4(NTn, R, g)
                P.tt(G(R, g), G(R, g), psc, ALU.add)

            lock([d_sq, d_r])
            Nc, NTc = Nn, NTn

        VBb = NA
        KBG = NB_
        VZ = UTb

        def s_prep(g):
            P.tt(G4(VBb, g), G4(V, g), h4(BETA, g), ALU.mult)
            P.tt(G4(KBG, g), G4(KN, g), h4(BEG, g), ALU.mult)
            P.tt(G4(KDEC, g), G4(KN, g), h4(EKD, g), ALU.mult)

        def s_u(g):
            ph = mm4(VBb, R, g)
            P.cp(G(UTb, g), ph, eng='act')

        def s_w(g):
            ph = mm4(KBG, R, g)
            P.cp(G(WTb, g), ph, eng='act')

        VNT = SQ

        def load_S(g, s, bufs):
            S_s = bufs[s % len(bufs)]
            P.dma(G4(S_s, g), st_g[s, 4 * g:4 * g + 4].rearrange("h k v -> k h v"), q='pool')
            return S_s

        def s_scanA(g):
            psA = phalf()
            psB = phalf() if emit_out else None
            for s in range(nseq):
                S_s = SG if ty == 'p' else load_S(g, s, [RH, DM, N0, NT0])
                for hh in range(4):
                    h = 4 * g + hh
                    P.mm(psA[:, hh * 128 + s * L:hh * 128 + (s + 1) * L], S_s[:, h * 128:(h + 1) * 128],
                         WTb[:, h * 128 + s * L:h * 128 + (s + 1) * L])
                if emit_out:
                    for hh in range(4):
                        h = 4 * g + hh
                        P.mm(psB[:, hh * 128 + s * L:hh * 128 + (s + 1) * L], S_s[:, h * 128:(h + 1) * 128],
                             QDT[:, h * 128 + s * L:h * 128 + (s + 1) * L])
            P.tt(G(VNT, g), G(UTb, g), psA, ALU.subtract)
            if emit_out:
                P.cp(G(OTb, g), psB, eng='act')

        def s_vn(g):
            tr4(G(VNb, g), [VNT[:, (4 * g + hh) * 128:(4 * g + hh + 1) * 128] for hh in range(4)])

        def s_c(g):
            if emit_out:
                psC = mm4(VNb, QKMT, g)
                P.tt(G(OTb, g), G(OTb, g), psC, ALU.add)

        def s_upd(g):
            for s in range(nseq):
                if ty == 'p':
                    S_s, VZs, Sn = SG, VNb, SG
                else:
                    S_s = load_S(g, s, [RH, DM, QNT, KNT])
                    VZs = [UTb, WTb][s % 2]
                    P.ts(G(VZs, g), G(VNb, g), RM[:, s:s + 1], None, ALU.mult)
                    Sn = [N0, NT0, QDT, QKM][s % 4]
                psn = mm4(KDEC, VZs, g)
                P.tt(G4(Sn, g), G4(S_s, g), bh(EGL3[:, s, 4 * g:4 * g + 4], 4), ALU.mult)
                P.tt(G(Sn, g), G(Sn, g), psn, ALU.add)
                if ty == 's':
                    out_dmas.append(P.dma(gst_s[s, 4 * g:4 * g + 4].rearrange("h k v -> k h v"), G4(Sn, g)))
            if t == 7:
                P.ts(G(SG, g), G(SG, g), FLAG, None, ALU.mult)
            if t == 15:
                out_dmas.append(P.dma(gst_p[4 * g:4 * g + 4].rearrange("h k v -> k h v"), G4(SG, g)))

        O = Q

        def s_out(g):
            if not emit_out:
                return
            ph = phalf()
            for hh in range(4):
                h = 4 * g + hh
                P.tr(ph[:, hh * 128:(hh + 1) * 128], OTb[:, h * 128:(h + 1) * 128], ID)
            P.cp(G(O, g), ph, eng='act')
            P.tt(G(SQ, g), G(O, g), G(O, g), ALU.mult)
            so = SSO[:, 4 * g:4 * g + 4]
            P.red(so, G4(SQ, g))
            P.actf(so, so, AF.Ln, scale=1.0 / 128, bias=EPS)
            P.actf(so, so, AF.Exp, scale=-0.5)
            P.tt(G4(O, g), G4(O, g), h4(SSO, g), ALU.mult)
            P.tt(G4(O, g), G4(O, g), m4(GNW), ALU.mult)
            P.tt(MIX[:, g * 512:(g + 1) * 512], G(O, g), TM[:, g * 512:(g + 1) * 512], ALU.mult)

        if emit_out:
            P.actf(TM[:, 0:2048], TM[:, 0:2048], AF.Silu)
        lock([s_prep, s_u, s_w, s_scanA, s_vn, s_c, s_upd, s_out])

        XSb, XDT, XW, DXT0, DXT1, Yb, T1a, YOT, HTS, HTN, XZ = Bf[0:11]
        BTM = Bf[11][:, 0:256]
        CBM = Bf[11][:, 256:512]
        HNAT = Bf[12]
        T1b = Bf[15]
        T1 = [T1a, T1b]
        DXT = [DXT0, DXT1]
        PSO = [PW[3][:, 0:512], PW[3][:, 512:1024]]

        def G8h(buf, g):
            return v3(buf[:, g * 512:(g + 1) * 512], 8)

        def h8(ap16, g):
            return bh(ap16[:, 8 * g:8 * g + 8], 8, 64)

        psb_ = phalf()
        for g in range(2):
            P.tr(psb_[:, g * 128:(g + 1) * 128], CV3[:, 32 + g, :], ID)
        P.cp(BTM, psb_[:, 0:256], eng='act')
        pss = phalf()
        P.mm(pss[:, 0:16], UI[ty], A16)
        P.mm(pss[:, 16:32], MS[ty], A16)
        P.actf(EAA, pss[:, 0:32], AF.Exp)
        if ty == 'p':
            P.mm(pss[:, 32:48], ONES, A16)
        else:
            P.tt(RHE.rearrange("p (s h) -> p s h", s=16),
                 A16.unsqueeze(1).to_broadcast([128, 16, 16]), RM.unsqueeze(2).to_broadcast([128, 16, 16]), ALU.mult)
            P.mm(pss[:, 32:32 + 256], ONES, RHE)
        P.actf(EALB[:, 0:nseq * 16], pss[:, 32:32 + nseq * 16], AF.Exp)
        EALB3 = EALB[:, 0:nseq * 16].rearrange("p (s h) -> p s h", s=nseq)

        def t_xs(g):
            tr4(G(XSb, g), [CV3[:, 24 + 4 * g + bb, :] for bb in range(4)])

        def t_decay(g):
            if not emit_out:
                return
            P.tt(v3(T1[g], 8), bm(UI[ty]), bh(A16[:, g * 8:(g + 1) * 8]), ALU.mult)
            for hf in range(2):
                ph = phalf()
                P.mm(ph, MS[ty], T1[g][:, hf * 512:(hf + 1) * 512])
                P.actf(DXT[g][:, hf * 512:(hf + 1) * 512], ph, AF.Exp)

        def t_cb(g):
            if not emit_out:
                return
            ph = phalf()
            P.mm(ph[:, 0:128], CV3[:, 32 + g, :], CV3[:, 34 + g, :])
            cbm = CBM[:, g * 128:(g + 1) * 128]
            P.tt(cbm, ph[:, 0:128], UI[ty], ALU.mult)
            for hf in range(2):
                d_ = v3(DXT[g][:, hf * 512:(hf + 1) * 512], 4)
                P.tt(d_, d_, bm(cbm, 4), ALU.mult)

        def t_xdt(g):
            P.tt(G8h(XDT, g), G8h(XSb, g), h8(DT16, g), ALU.mult)
            P.tt(G8h(XW, g), G8h(XDT, g), h8(EAL, g), ALU.mult)

        def t_ydiag(g):
            if not emit_out:
                return
            ph = phalf()
            for hh in range(8):
                h = 8 * g + hh
                P.mm(ph[:, hh * 64:(hh + 1) * 64], DXT[g][:, hh * 128:(hh + 1) * 128], XDT[:, h * 64:(h + 1) * 64])
            P.cp(G(Yb, g), ph, eng='act')

        def t_state(g):
            pso = PSO[g]
            for s in range(nseq):
                if ty == 'p':
                    H_s = HT
                else:
                    hin = [Bf[12], Bf[16]][s % 2]
                    P.dma(v3(G(hin, g), 4),
                          st_s[s, 8 * g:8 * g + 8].rearrange("(b h2) p n -> (h2 p) b n", h2=2), q='pool')
                    H_s = [Bf[8], Bf[19]][s % 2]
                    tr4(G(H_s, g), [hin[:, (4 * g + bb) * 128:(4 * g + bb + 1) * 128] for bb in range(4)])
                if emit_out:
                    for bb in range(4):
                        b = 4 * g + bb
                        P.mm(pso[:, bb * 128 + s * L:bb * 128 + (s + 1) * L], H_s[:, b * 128:(b + 1) * 128],
                             CV3[:, 34 + g, s * L:(s + 1) * L])
                if ty == 'p':
                    XZs, Hn = XW, HT
                else:
                    XZs, Hn = [Bf[10], Bf[21]][s % 2], [Bf[9], Bf[20]][s % 2]
                    P.ts(G(XZs, g), G(XW, g), RM[:, s:s + 1], None, ALU.mult)
                psn = phalf()
                P.mm(psn, BTM[:, g * 128:(g + 1) * 128], G(XZs, g))
                P.tt(G8h(Hn, g), G8h(H_s, g), bh(EALB3[:, s, 8 * g:8 * g + 8], 8, 64), ALU.mult)
                P.tt(G(Hn, g), G(Hn, g), psn, ALU.add)
                if ty == 's':
                    hout = [Bf[17], Bf[18]][s % 2]
                    tr4(G(hout, g), [Hn[:, (4 * g + bb) * 128:(4 * g + bb + 1) * 128] for bb in range(4)])
                    out_dmas.append(P.dma(sst_s[s, 8 * g:8 * g + 8].rearrange("(b h2) p n -> (h2 p) b n", h2=2),
                                          v3(G(hout, g), 4)))
            if emit_out:
                P.cp(G(YOT, g), pso, eng='act')
            if t == 7:
                P.ts(G(HT, g), G(HT, g), FLAG, None, ALU.mult)
            if t == 15:
                tr4(G(HNAT, g), [HT[:, (4 * g + bb) * 128:(4 * g + bb + 1) * 128] for bb in range(4)])
                out_dmas.append(P.dma(sst_p[8 * g:8 * g + 8].rearrange("(b h2) p n -> (h2 p) b n", h2=2),
                                      v3(G(HNAT, g), 4)))

        def t_y(g):
            if not emit_out:
                return
            ph = phalf()
            for bb in range(4):
                b = 4 * g + bb
                P.tr(ph[:, bb * 128:(bb + 1) * 128], YOT[:, b * 128:(b + 1) * 128], ID)
            Tg = T1[g][:, 0:512]
            T8 = v3(Tg, 8)
            P.tt(T8, v3(ph, 8), h8(EA, g), ALU.mult)
            P.tt(G(Yb, g), G(Yb, g), Tg, ALU.add)
            P.tt(T8, G8h(XSb, g), h8(SD, g), ALU.mult)
            P.tt(G(Yb, g), G(Yb, g), Tg, ALU.add)
            P.tt(G(Yb, g), G(Yb, g), TM[:, 1024 + g * 512:1024 + (g + 1) * 512], ALU.mult)
            P.tt(Tg, G(Yb, g), G(Yb, g), ALU.mult)
            s2 = SS2[:, g:g + 1]
            P.red(s2, Tg)
            P.actf(s2, s2, AF.Ln, scale=1.0 / 512, bias=EPS)
            P.actf(s2, s2, AF.Exp, scale=-0.5)
            P.ts(G(Yb, g), G(Yb, g), s2, None, ALU.mult)
            P.tt(MIX[:, 1024 + g * 512:1024 + (g + 1) * 512], G(Yb, g), SNW[:, g * 512:(g + 1) * 512], ALU.mult)

        lock([t_xs, t_decay, t_xdt, t_cb, t_ydiag, t_state, t_y])
        if emit_out:
            for kq in range(4):
                ph = phalf()
                for j in range(4):
                    kc = kq * 4 + j
                    P.tr(ph[:, j * 128:(j + 1) * 128], MIX[:, kc * 128:(kc + 1) * 128], ID)
                evac(MIXT3[:, kq * 4:kq * 4 + 4, ot * 128:(ot + 1) * 128], v3(ph, 4))
    P.fence()

    A.top = persist_top + 8 * NOUT
    WOC = [A.bf16(16 * 512) for _ in range(2)]
    r1_end = A.top
    X1 = A.f32(9 * 2048)
    X13 = v3(X1, 9)
    H2T = A.bf16(16 * NOUT)
    H2T3 = v3(H2T, 16)
    XN2 = A.f32(2048)
    SQJ2 = A.bf16(2048)
    SSc = A.f32(1)
    RSc = A.f32(1)
    w_out_v = w_out.rearrange("(kc p) c -> p kc c", p=128)
    for ot in range(9):
        P.dma(X13[:, ot, :], xin[1024 + ot * 128:1024 + (ot + 1) * 128, :])
    for dq in range(4):
        woc = v3(WOC[dq % 2], 16)
        P.dma(woc, w_out_v[:, :, dq * 512:(dq + 1) * 512], q='pool')
        for ot in range(9):
            ph = phalf()
            for kc in range(16):
                P.mm(ph, MIXT3[:, kc, ot * 128:(ot + 1) * 128], woc[:, kc, :], start=(kc == 0), stop=(kc == 15))
            xs_ = X13[:, ot, dq * 512:(dq + 1) * 512]
            P.tt(xs_, xs_, ph, ALU.add)
    for ot in range(9):
        rms_to_hT(X13[:, ot, :], FNW, H2T3[:, :, ot * 128:(ot + 1) * 128], SQJ2, XN2, SSc, RSc)
    P.fence()
    A.top = persist_top
    NFB = 4
    FFT = A.bf16(NFB * NOUT)
    FFT3 = v3(FFT, NFB)
    WG = [A.bf16(16 * 128) for _ in range(2)]
    WU = [A.bf16(16 * 128) for _ in range(2)]
    WD = [A.bf16(NFB * 2048) for _ in range(2)]
    SIL = [A.f32(512) for _ in range(2)]
    assert A.top <= r1_end, (A.top, r1_end)
    w_gate_v = w_gate.rearrange("(kc p) c -> p kc c", p=128)
    w_up_v = w_up.rearrange("(kc p) c -> p kc c", p=128)
    tchunks = [(0, 512), (512, 512), (1024, 128)]
    groups = []
    b0 = 0
    while b0 < 44:
        nb = min(NFB, 44 - b0)
        groups.append((b0, nb))
        b0 += nb
    cnt = 0
    for gi, (b0, nb) in enumerate(groups):
        wd = WD[gi % 2]
        wd3 = v3(wd, NFB)
        P.dma(wd3[:, 0:nb, :], w_down[b0 * 128:(b0 + nb) * 128, :].rearrange("(b p) d -> p b d", p=128), q='pool')
        for j in range(nb):
            fb = b0 + j
            wg = v3(WG[fb % 2], 16)
            wu = v3(WU[fb % 2], 16)
            P.dma(wg, w_gate_v[:, :, fb * 128:(fb + 1) * 128], q='pool')
            P.dma(wu, w_up_v[:, :, fb * 128:(fb + 1) * 128], q='pool')
            for (t0, n) in tchunks:
                pg = phalf()
                for kc in range(16):
                    P.mm(pg[:, 0:n], wg[:, kc, :], H2T3[:, kc, t0:t0 + n], start=(kc == 0), stop=(kc == 15))
                pu = phalf()
                for kc in range(16):
                    P.mm(pu[:, 0:n], wu[:, kc, :], H2T3[:, kc, t0:t0 + n], start=(kc == 0), stop=(kc == 15))
                sl_ = SIL[cnt % 2]
                cnt += 1
                P.actf(sl_[:, 0:n], pg[:, 0:n], AF.Silu)
                P.tt(FFT3[:, j, t0:t0 + n], sl_[:, 0:n], pu[:, 0:n], ALU.mult)
        for ot in range(9):
            for dq in range(4):
                ph = phalf()
                for j in range(nb):
                    P.mm(ph, FFT3[:, j, ot * 128:(ot + 1) * 128], wd3[:, j, dq * 512:(dq + 1) * 512],
                         start=(j == 0), stop=(j == nb - 1))
                xs_ = X13[:, ot, dq * 512:(dq + 1) * 512]
                P.tt(xs_, xs_, ph, ALU.add)
    YB = [WD[0].bitcast(F32)[:, 0:2048], WD[1].bitcast(F32)[:, 0:2048]]
    for ot in range(9):
        xt = X13[:, ot, :]
        yb = YB[ot % 2]
        P.actf(XN2, xt, AF.Square, accum=SSc)
        P.actf(RSc, SSc, AF.Sqrt, scale=1.0 / D, bias=EPS)
        P.add('dve', lambda e: e.reciprocal(RSc, RSc), r=[RSc], w=[RSc])
        P.stt(yb, xt, RSc, FINW, ALU.mult, ALU.mult)
        out_dmas.append(P.dma(y_o[ot * 128:(ot + 1) * 128, :], yb))
    stats = P.emit(final_wait_ops=out_dmas)
    es.close()
    return nc, stats


_CACHE = {}


def kernel(x_prompt, x_sample, state_gdn_conv, state_gdn, state_ssm_conv, state_ssm,
           attn_norm_w, w_in, gdn_conv_w, gdn_A_log, gdn_dt_bias, gdn_norm_w,
           ssm_conv_w, ssm_conv_b, ssm_A_log, ssm_dt_bias, ssm_D, ssm_norm_w,
           w_out, ffn_norm_w, w_gate, w_up, w_down, final_norm_w):
    f = lambda a: np.ascontiguousarray(np.asarray(a, dtype=np.float32))
    x_prompt = f(x_prompt)
    x_sample = f(x_sample)
    if 'nc' not in _CACHE:
        _CACHE['nc'] = build_program()[0]
    nc = _CACHE['nc']
    shared = dict(
        attn_norm_w=f(attn_norm_w).reshape(16, 128), w_in=f(w_in)[0], gdn_conv_w=f(gdn_conv_w)[0],
        gdn_A_log=f(gdn_A_log).reshape(1, 8), gdn_dt_bias=f(gdn_dt_bias).reshape(1, 8),
        gdn_norm_w=f(gdn_norm_w).reshape(1, 128), ssm_conv_w=f(ssm_conv_w)[0],
        ssm_conv_b=f(ssm_conv_b).reshape(1, 1536), ssm_A_log=f(ssm_A_log).reshape(1, 16),
        ssm_dt_bias=f(ssm_dt_bias).reshape(1, 16), ssm_D=f(ssm_D).reshape(1, 16),
        ssm_norm_w=f(ssm_norm_w).reshape(1, 1024), w_out=f(w_out)[0],
        ffn_norm_w=f(ffn_norm_w).reshape(16, 128), w_gate=f(w_gate)[0], w_up=f(w_up)[0],
        w_down=f(w_down)[0], final_norm_w=f(final_norm_w).reshape(1, D))
    sgc = f(state_gdn_conv)[0]
    sg = f(state_gdn)[0]
    ssc = f(state_ssm_conv)[0]
    ssm = f(state_ssm)[0]
    in_maps = []
    for c in range(8):
        b, r = c // 2, c % 2
        xin = np.zeros((NT, D), np.float32)
        if r == 1:
            xin[0:1024] = x_prompt[b, 0:1024]
        xin[1024:2048] = x_prompt[b, r * 1024:(r + 1) * 1024]
        xin[2048:] = x_sample[16 * c:16 * c + 16].reshape(128, D)
        m = dict(shared)
        m.update(xin=xin, flag=np.full((128, 1), float(r), np.float32),
                 st_gconv=np.ascontiguousarray(sgc[16 * c:16 * c + 16].reshape(48, 3072)),
                 st_g=np.ascontiguousarray(sg[16 * c:16 * c + 16]),
                 st_sconv=np.ascontiguousarray(ssc[16 * c:16 * c + 16].reshape(48, 1536)),
                 st_s=np.ascontiguousarray(ssm[16 * c:16 * c + 16]))
        in_maps.append(m)
    res = run_bass_kernel_spmd(nc, in_maps, core_ids=list(range(8)))
    R = res.results
    y_prompt = np.zeros((4, 2048, D), np.float32)
    y_sample = np.zeros((128, 8, D), np.float32)
    gconv_p = np.zeros((1, 4, 3, 3072), np.float32)
    gst_p = np.zeros((1, 4, 8, 128, 128), np.float32)
    sconv_p = np.zeros((1, 4, 3, 1536), np.float32)
    sst_p = np.zeros((1, 4, 16, 64, 128), np.float32)
    gconv_s = np.zeros((1, 128, 3, 3072), np.float32)
    gst_s = np.zeros((1, 128, 8, 128, 128), np.float32)
    sconv_s = np.zeros((1, 128, 3, 1536), np.float32)
    sst_s = np.zeros((1, 128, 16, 64, 128), np.float32)
    for c in range(8):
        b, r = c // 2, c % 2
        o = R[c]
        y_prompt[b, r * 1024:(r + 1) * 1024] = o['y'][0:1024]
        y_sample[16 * c:16 * c + 16] = o['y'][1024:].reshape(16, 8, D)
        if r == 1:
            gconv_p[0, b] = o['gconv_p']
            gst_p[0, b] = o['gst_p']
            sconv_p[0, b] = o['sconv_p']
            sst_p[0, b] = o['sst_p']
        gconv_s[0, 16 * c:16 * c + 16] = o['gconv_s'].reshape(16, 3, 3072)
        gst_s[0, 16 * c:16 * c + 16] = o['gst_s']
        sconv_s[0, 16 * c:16 * c + 16] = o['sconv_s'].reshape(16, 3, 1536)
        sst_s[0, 16 * c:16 * c + 16] = o['sst_s']
    return (y_prompt, y_sample, gconv_p, gst_p, sconv_p, sst_p, gconv_s, gst_s, sconv_s, sst_s)
```

```python
import numpy as np
import concourse.bass as bass
import concourse.mybir as mybir

F32 = mybir.dt.float32
BF16 = mybir.dt.bfloat16
ALU = mybir.AluOpType
AF = mybir.ActivationFunctionType
AX = mybir.AxisListType

_ES = {F32: 4, BF16: 2}


def _esize(dt):
    if dt in _ES:
        return _ES[dt]
    s = str(dt)
    if '32' in s:
        return 4
    if '16' in s:
        return 2
    if '64' in s:
        return 8
    return 1


def footprint(ap):
    name = ap.tensor.name
    dims = ap.ap
    es = _esize(ap.dtype)
    off = ap.offset
    space = str(ap.space)
    if 'DRAM' in space.upper() or 'HBM' in space.upper() or 'Dram' in space:
        ext = 1
        for st, cnt in dims:
            ext += (cnt - 1) * abs(st)
        return (name, True, 0, 1, off * es, (off + ext) * es)
    pst, pcnt = dims[0]
    if pst == 0:
        p0 = 0
        lo = off
        pcnt_eff = 1
    else:
        p0 = off // pst
        lo = off % pst
        pcnt_eff = pcnt
    ext = 1
    for st, cnt in dims[1:]:
        ext += (cnt - 1) * abs(st)
    lo_b, hi_b = lo * es, (lo + ext) * es
    if name.startswith('pw'):
        lo_b = (lo_b // 2048) * 2048
        hi_b = ((hi_b + 2047) // 2048) * 2048
        return (name, False, 0, 128, lo_b, hi_b)
    return (name, False, p0, p0 + pcnt_eff, lo_b, hi_b)


class Op:
    __slots__ = ('eng', 'fn', 'deps', 'dma', 'token', 'prewait', 'needed', 'idx')


class Prog:
    ENGS = ('pe', 'act', 'dve', 'pool', 'sp')
    NS = 8

    def __init__(self, nc):
        self.nc = nc
        self.ops = []
        self.acc = {}
        self.last_on_eng = {}

    def _track(self, idx, aps_r, aps_w):
        deps = set()
        for is_w, aps in ((False, aps_r), (True, aps_w)):
            for ap in aps:
                name, isd, p0, p1, lo, hi = footprint(ap)
                lst = self.acc.setdefault(name, [])
                keep = []
                for e in lst:
                    ov = not (e[2] <= p0 or e[1] >= p1 or e[4] <= lo or e[3] >= hi)
                    if ov and (is_w or e[5]):
                        if e[0] != idx:
                            deps.add(e[0])
                        if is_w and e[1] >= p0 and e[2] <= p1 and e[3] >= lo and e[4] <= hi:
                            continue
                    keep.append(e)
                keep.append([idx, p0, p1, lo, hi, is_w])
                self.acc[name] = keep
        return deps

    def add(self, eng, fn, r=(), w=(), dma=False, extra_deps=()):
        op = Op()
        op.eng = eng
        op.fn = fn
        op.dma = dma
        op.idx = len(self.ops)
        op.deps = self._track(op.idx, r, w)
        op.deps.update(extra_deps)
        op.token = None
        op.prewait = None
        op.needed = False
        self.ops.append(op)
        self.last_on_eng[eng] = op.idx
        return op.idx

    def fence(self):
        last = []
        for e in self.ENGS:
            pass
        idxs = set()
        seen_eng = set()
        for op in reversed(self.ops):
            if op.dma:
                idxs.add(op.idx)
            elif op.eng not in seen_eng:
                seen_eng.add(op.eng)
                idxs.add(op.idx)
        lf = getattr(self, '_last_fence', 0)
        idxs = {i for i in idxs if i >= lf or not self.ops[i].dma}
        for e in self.ENGS:
            self.add(e, None, extra_deps=set(idxs))
        self._last_fence = len(self.ops)

    def mm(self, out, lhsT, rhs, start=True, stop=True):
        return self.add('pe', lambda e: e.matmul(out, lhsT, rhs, start=start, stop=stop),
                        r=[lhsT, rhs], w=[out])

    def tr(self, out, in_, ident):
        return self.add('pe', lambda e: e.transpose(out, in_, ident), r=[in_, ident], w=[out])

    def actf(self, out, in_, func, bias=None, scale=None, accum=None, eng='act'):
        kw = {}
        r = [in_]
        w = [out]
        if bias is not None:
            kw['bias'] = bias
            if not isinstance(bias, (int, float)):
                r.append(bias)
        if scale is not None:
            kw['scale'] = scale
            if not isinstance(scale, (int, float)):
                r.append(scale)
        if accum is not None:
            kw['accum_out'] = accum
            w.append(accum)
        return self.add(eng, lambda e: e.activation(out, in_, func, **kw), r=r, w=w)

    def tt(self, out, a, b, op, eng='dve'):
        return self.add(eng, lambda e: e.tensor_tensor(out, a, b, op), r=[a, b], w=[out])

    def ts(self, out, a, s1, s2, op0, op1=None, eng='dve', accum=None):
        r = [a]
        if not isinstance(s1, (int, float)):
            r.append(s1)
        if s2 is not None and not isinstance(s2, (int, float)):
            r.append(s2)
        w = [out]
        kw = {}
        if accum is not None:
            kw['accum_out'] = accum
            w.append(accum)
        if op1 is None:
            if isinstance(s1, (int, float)):
                return self.add(eng, lambda e: e.tensor_scalar(out, a, s1, None, op0, **kw), r=r, w=w)
            return self.add(eng, lambda e: e.tensor_scalar(out, a, s1, 0.0, op0, ALU.add, **kw), r=r, w=w)
        return self.add(eng, lambda e: e.tensor_scalar(out, a, s1, s2, op0, op1, **kw), r=r, w=w)

    def stt(self, out, a, s, b, op0, op1, eng='dve'):
        r = [a, b]
        if not isinstance(s, (int, float)):
            r.append(s)
        return self.add(eng, lambda e: e.scalar_tensor_tensor(out, a, s, b, op0, op1), r=r, w=[out])

    def cp(self, out, in_, eng='dve'):
        if eng == 'act':
            return self.add('act', lambda e: e.copy(out, in_), r=[in_], w=[out])
        return self.add(eng, lambda e: e.tensor_copy(out, in_), r=[in_], w=[out])

    def red(self, out, in_, op=None, eng='dve'):
        op = op or ALU.add
        return self.add(eng, lambda e: e.tensor_reduce(out, in_, AX.X, op), r=[in_], w=[out])

    def memset(self, ap, val, eng='pool'):
        return self.add(eng, lambda e: e.memset(ap, val), w=[ap])

    def dma(self, out, in_, q='sp'):
        return self.add(q, lambda e: e.dma_start(out, in_), r=[in_], w=[out], dma=True)

    def emit(self, final_wait_ops=()):
        nc = self.nc
        ops = self.ops
        if final_wait_ops:
            self.add('sp', None, extra_deps=set(final_wait_ops))
        for op in ops:
            for d in op.deps:
                dop = ops[d]
                if dop.dma:
                    continue
                if dop.eng == 'pe' and op.eng == 'pe' and not op.dma:
                    continue
                dop.needed = True
        from contextlib import ExitStack
        es = ExitStack()
        sem = {e: es.enter_context(nc.semaphore('s_' + e)) for e in self.ENGS}
        dsem = {e: [es.enter_context(nc.semaphore('d_%s%d' % (e, i))) for i in range(self.NS)]
                for e in ('sp', 'act', 'pool')}
        cnt = {e: 0 for e in self.ENGS}
        dcnt = {e: 0 for e in dsem}
        for op in ops:
            if op.fn is None:
                continue
            if op.dma:
                m = dcnt[op.eng]
                s = dsem[op.eng][m % self.NS]
                op.token = (s, 16 * (m // self.NS + 1), 16)
                if m >= self.NS:
                    op.prewait = (s, 16 * (m // self.NS))
                dcnt[op.eng] = m + 1
            elif op.needed:
                cnt[op.eng] += 1
                op.token = (sem[op.eng], cnt[op.eng], 1)
        per_eng = {e: [op for op in ops if op.eng == e] for e in self.ENGS}
        stats = {e: [0, 0] for e in self.ENGS}

        def run(ename, eobj):
            known = {}
            for op in per_eng[ename]:
                waits = {}
                if op.prewait is not None:
                    s, v = op.prewait
                    waits[id(s)] = (s, v)
                for d in op.deps:
                    dop = ops[d]
                    if dop.token is None:
                        continue
                    if (not dop.dma) and dop.eng == 'pe' and ename == 'pe' and not op.dma:
                        continue
                    s, v, _ = dop.token
                    if id(s) not in waits or waits[id(s)][1] < v:
                        waits[id(s)] = (s, v)
                for k, (s, v) in waits.items():
                    if known.get(k, 0) >= v:
                        continue
                    eobj.wait_ge(s, v)
                    stats[ename][1] += 1
                    known[k] = v
                if op.fn is None:
                    continue
                ins = op.fn(eobj)
                stats[ename][0] += 1
                if op.token is not None:
                    ins.then_inc(op.token[0], op.token[2])

        with nc.Block() as block:
            @block.tensor
            def _(e):
                run('pe', e)

            @block.scalar
            def _(e):
                run('act', e)

            @block.vector
            def _(e):
                run('dve', e)

            @block.gpsimd
            def _(e):
                run('pool', e)

            @block.sync
            def _(e):
                run('sp', e)
        es.close()
        return stats

from contextlib import ExitStack
from concourse.bass_utils import run_bass_kernel_spmd

D = 2048
DIN = 6688
DFF = 5632
NT = 2176
NTILE = 17
NOUT = 1152
EPSV = 1e-6
PFW = 3 + NT
TMW = 2080


class Arena:
    def __init__(self, ar, total):
        self.ar = ar
        self.top = 0
        self.total = total

    def f32(self, n):
        o = self.top
        self.top += n
        assert self.top <= self.total, (self.top, self.total)
        return self.ar[:, o:o + n]

    def bf16(self, n):
        assert n % 2 == 0
        return self.f32(n // 2).bitcast(BF16)

    def at(self, o, n):
        return self.ar[:, o:o + n]


def v3(ap, a):
    return ap.rearrange("p (a b) -> p a b", a=a)


def build_program(debug=False):
    nc = bass.Bass("TRN2", target_bir_lowering=False)

    def din(name, shape):
        return nc.dram_tensor(name, list(shape), F32, kind="ExternalInput").ap()

    def dout(name, shape):
        return nc.dram_tensor(name, list(shape), F32, kind="ExternalOutput").ap()

    xin = din("xin", [NT, D])
    flag_d = din("flag", [128, 1])
    st_gconv = din("st_gconv", [48, 3072])
    st_g = din("st_g", [16, 8, 128, 128])
    st_sconv = din("st_sconv", [48, 1536])
    st_s = din("st_s", [16, 16, 64, 128])
    attn_norm_w = din("attn_norm_w", [16, 128])
    w_in = din("w_in", [D, DIN])
    gdn_conv_w = din("gdn_conv_w", [4, 3072])
    gdn_A_log = din("gdn_A_log", [1, 8])
    gdn_dt_bias = din("gdn_dt_bias", [1, 8])
    gdn_norm_w = din("gdn_norm_w", [1, 128])
    ssm_conv_w = din("ssm_conv_w", [4, 1536])
    ssm_conv_b = din("ssm_conv_b", [1, 1536])
    ssm_A_log = din("ssm_A_log", [1, 16])
    ssm_dt_bias = din("ssm_dt_bias", [1, 16])
    ssm_D = din("ssm_D", [1, 16])
    ssm_norm_w = din("ssm_norm_w", [1, 1024])
    w_out = din("w_out", [D, D])
    ffn_norm_w = din("ffn_norm_w", [16, 128])
    w_gate = din("w_gate", [D, DFF])
    w_up = din("w_up", [D, DFF])
    w_down = din("w_down", [DFF, D])
    final_norm_w = din("final_norm_w", [1, D])

    y_o = dout("y", [NOUT, D])
    gconv_p = dout("gconv_p", [3, 3072])
    gst_p = dout("gst_p", [8, 128, 128])
    sconv_p = dout("sconv_p", [3, 1536])
    sst_p = dout("sst_p", [16, 64, 128])
    gconv_s = dout("gconv_s", [48, 3072])
    gst_s = dout("gst_s", [16, 8, 128, 128])
    sconv_s = dout("sconv_s", [48, 1536])
    sst_s = dout("sst_s", [16, 16, 64, 128])

    P_fm = nc.dram_tensor("P_fm", [36, 128, PFW], F32).ap()
    P_tm = nc.dram_tensor("P_tm", [NT, TMW], F32).ap()
    P_cv = nc.dram_tensor("P_cv", [36, 128, 2048], F32).ap()

    es = ExitStack()
    TOT = 53200
    ar_t = es.enter_context(nc.sbuf_tensor("arena", [128, TOT], F32))
    PW = [es.enter_context(nc.psum_tensor("pw%d" % i, [128, 1024], F32)) for i in range(4)]
    A = Arena(ar_t, TOT)
    P = Prog(nc)
    pwc = [0]

    def pw():
        pwc[0] += 1
        return PW[pwc[0] % 3]
    pw_global = pw

    hb = [0]

    def phalf():
        hb[0] += 1
        k = hb[0] % 6
        return PW[k // 2][:, (k % 2) * 512:(k % 2) * 512 + 512]

    ID = A.f32(128)
    ONES = A.f32(128)
    MI = {}
    MS = {}
    UI = {}
    for ty in ('p', 's'):
        MI[ty] = A.f32(128)
        MS[ty] = A.f32(128)
        UI[ty] = A.f32(128)
    BD = A.f32(128)
    RM = A.f32(16)
    EPS = A.f32(1)
    FLAG = A.f32(1)
    ANW = A.f32(16)
    FNW = A.f32(16)
    FINW = A.f32(2048)
    GNW = A.f32(128)
    SNW = A.f32(1024)
    SD = A.f32(16)
    NAG = A.f32(8)
    GDB = A.f32(8)
    NAS = A.f32(16)
    SDB = A.f32(16)
    CW = A.f32(36 * 5)
    CW3 = v3(CW, 36)
    ZERO = A.f32(128)
    LNDK = A.f32(1)

    def asel(ap, pattern, op, fill, base, cm):
        P.add('pool', lambda e: e.affine_select(ap, ap, pattern, op, fill, base=base, channel_multiplier=cm),
              r=[ap], w=[ap])

    P.memset(ID, 0.0)
    asel(ID, [[-1, 128]], ALU.not_equal, 1.0, 0, 1)
    P.memset(ONES, 1.0)
    P.memset(ZERO, 0.0)
    P.memset(EPS, EPSV)
    P.memset(LNDK, float(np.log(128 ** -0.5)))
    P.memset(BD, 1.0)
    asel(v3(BD, 16), [[-8, 16], [0, 8]], ALU.is_ge, 0.0, 0, 1)
    asel(v3(BD, 16), [[8, 16], [0, 8]], ALU.is_ge, 0.0, 7, -1)
    P.memset(RM, 1.0)
    asel(RM, [[-8, 16]], ALU.is_ge, 0.0, 0, 1)
    asel(RM, [[8, 16]], ALU.is_ge, 0.0, 7, -1)
    for ty in ('p', 's'):
        P.memset(MI[ty], 1.0)
        asel(MI[ty], [[-1, 128]], ALU.is_ge, 0.0, 0, 1)
        P.memset(MS[ty], 1.0)
        asel(MS[ty], [[-1, 128]], ALU.is_gt, 0.0, 0, 1)
        P.memset(UI[ty], 1.0)
        asel(UI[ty], [[1, 128]], ALU.is_ge, 0.0, 0, -1)
        if ty == 's':
            for m in (MI, MS, UI):
                P.tt(m[ty], m[ty], BD, ALU.mult, eng='pool')
    P.dma(FLAG, flag_d)
    P.dma(FINW, final_norm_w.broadcast_to([128, D]))
    P.dma(GNW, gdn_norm_w.broadcast_to([128, 128]))
    P.dma(SNW, ssm_norm_w.broadcast_to([128, 1024]))
    P.dma(SD, ssm_D.broadcast_to([128, 16]))
    P.dma(NAG, gdn_A_log.broadcast_to([128, 8]))
    P.dma(GDB, gdn_dt_bias.broadcast_to([128, 8]))
    P.dma(NAS, ssm_A_log.broadcast_to([128, 16]))
    P.dma(SDB, ssm_dt_bias.broadcast_to([128, 16]))
    P.actf(NAG, NAG, AF.Exp)
    P.ts(NAG, NAG, -1.0, None, ALU.mult)
    P.actf(NAS, NAS, AF.Exp)
    P.ts(NAS, NAS, -1.0, None, ALU.mult)

    mark0 = A.top
    TMPA = A.f32(4608 + 256)
    cwst = TMPA[0:5, 0:4608]
    P.memset(TMPA[0:5, 0:4608], 0.0)
    P.dma(TMPA[0:4, 0:3072], gdn_conv_w)
    P.dma(TMPA[0:4, 3072:4608], ssm_conv_w)
    P.dma(TMPA[4:5, 3072:4608], ssm_conv_b)
    P.dma(TMPA[0:16, 4608:4736], attn_norm_w)
    P.dma(TMPA[0:16, 4736:4864], ffn_norm_w)
    ps = pw()
    for b in range(36):
        P.tr(ps[:, b * 5:b * 5 + 5], TMPA[0:5, b * 128:(b + 1) * 128], ID[0:5, 0:5])
    P.cp(CW, ps[:, 0:180], eng='act')
    ps = pw()
    P.tr(ps[:, 0:16], TMPA[0:16, 4608:4736], ID[0:16, 0:16])
    P.tr(ps[:, 16:32], TMPA[0:16, 4736:4864], ID[0:16, 0:16])
    P.cp(ANW, ps[:, 0:16], eng='act')
    P.cp(FNW, ps[:, 16:32], eng='act')
    P.fence()
    A.top = mark0
    persist_top = A.top

    def rms_to_hT(xt, nw, hT_dst, tmp_sq, xn, ss, rs):
        P.actf(tmp_sq, xt, AF.Square, accum=ss)
        P.actf(rs, ss, AF.Sqrt, scale=1.0 / D, bias=EPS)
        P.add('dve', lambda e: e.reciprocal(rs, rs), r=[rs], w=[rs])
        P.ts(xn, xt, rs, None, ALU.mult)
        for kq in range(4):
            ph = phalf()
            for j in range(4):
                kc = kq * 4 + j
                P.tr(ph[:, j * 128:(j + 1) * 128], xn[:, kc * 128:(kc + 1) * 128], ID)
            P.tt(hT_dst[:, kq * 4:kq * 4 + 4, :], v3(ph, 4),
                 nw[:, kq * 4:kq * 4 + 4].unsqueeze(2).to_broadcast([128, 4, 128]), ALU.mult)

    hT = A.bf16(16 * NT)
    hT3 = v3(hT, 16)
    XS = [A.f32(2048) for _ in range(2)]
    XN = A.f32(2048)
    SQJ = A.bf16(2048)
    SS = A.f32(1)
    RS = A.f32(1)
    for t in range(NTILE):
        xt = XS[t % 2]
        P.dma(xt, xin[t * 128:(t + 1) * 128, :])
        rms_to_hT(xt, ANW, hT3[:, :, t * 128:(t + 1) * 128], SQJ, XN, SS, RS)

    Z3 = A.f32(36 * 3)
    P.memset(Z3, 0.0)
    P.dma(P_fm.rearrange("b p t -> p b t")[:, :, 0:3], v3(Z3, 36))

    WB = [A.bf16(16 * 128) for _ in range(3)]
    STG = [A.f32(3 + NT) for _ in range(2)]
    CVS = [A.f32(2048) for _ in range(2)]
    for st_ in STG:
        P.memset(st_[:, 0:3], 0.0)
    w_in_v = w_in.rearrange("(kc p) c -> p kc c", p=128)
    chunks = [(0, 512), (512, 512), (1024, 512), (1536, 512), (2048, 128)]
    ev = [0]

    def evac(dst, src):
        ev[0] += 1
        P.cp(dst, src, eng='act' if ev[0] % 2 else 'dve')

    for cb in range(36):
        c0 = cb * 128 if cb < 24 else 5136 + (cb - 24) * 128
        W = WB[cb % 3]
        W3 = v3(W, 16)
        P.dma(W3, w_in_v[:, :, c0:c0 + 128], q='pool')
        stg = STG[cb % 2]
        cks = chunks if cb >= 8 else [(896, 128), (1024, 512), (1536, 512), (2048, 128)]
        for (t0, n) in cks:
            ph = phalf()
            for kc in range(16):
                P.mm(ph[:, 0:n], W3[:, kc, :], hT3[:, kc, t0:t0 + n], start=(kc == 0), stop=(kc == 15))
            evac(stg[:, 3 + t0:3 + t0 + n], ph[:, 0:n])
        tb = cks[0][0]
        P.dma(P_fm[cb, :, 3 + tb:3 + NT], stg[:, 3 + tb:3 + NT])
        tlo = 0 if cb >= 8 else 1024
        cvs = CVS[cb % 2]
        P.actf(cvs[:, tlo:2048], stg[:, tlo:2048], AF.Identity, scale=CW3[:, cb, 0:1], bias=CW3[:, cb, 4:5])
        for i in range(1, 4):
            P.stt(cvs[:, tlo:2048], stg[:, tlo + i:2048 + i], CW3[:, cb, i:i + 1], cvs[:, tlo:2048], ALU.mult, ALU.add)
        P.actf(cvs[:, tlo:2048], cvs[:, tlo:2048], AF.Silu)
        P.dma(P_cv[cb, :, tlo:2048], cvs[:, tlo:2048])

    WT = [A.bf16(16 * 512) for _ in range(2)]
    TS_ = [A.f32(512) for _ in range(2)]
    tmchunks = [(3072, 512, 0, 8), (3584, 512, 512, 8), (4112, 512, 1024, 8), (4624, 512, 1536, 8),
                (4096, 16, 2048, 0), (6672, 16, 2064, 0)]
    k = 0
    for i, (c0, ncol, pc0, t_first) in enumerate(tmchunks):
        W = WT[i % 2]
        W3 = v3(W, 16)
        P.dma(W3[:, :, 0:ncol], w_in_v[:, :, c0:c0 + ncol], q='pool')
        for t in range(t_first, NTILE):
            ph = phalf()
            for kc in range(16):
                P.mm(ph[:, 0:ncol], hT3[:, kc, t * 128:(t + 1) * 128], W3[:, kc, 0:ncol],
                     start=(kc == 0), stop=(kc == 15))
            ts_ = TS_[k % 2]
            k += 1
            evac(ts_[:, 0:ncol], ph[:, 0:ncol])
            P.dma(P_tm[t * 128:(t + 1) * 128, pc0:pc0 + ncol], ts_[:, 0:ncol])
    P.fence()

    A.top = persist_top
    MIXT = A.bf16(16 * NOUT)
    MIXT3 = v3(MIXT, 16)
    SG = A.f32(1024)
    HT = A.f32(1024)
    P.memset(SG, 0.0)
    P.memset(HT, 0.0)
    RAW = A.f32(36 * 176)
    CV = A.f32(36 * 128)
    RAWC = CV
    CV3 = v3(CV, 36)
    TM = A.f32(TMW)
    NB = 22
    bf0 = A.top
    Bf = [A.f32(1024) for _ in range(NB)]
    HST = A.at(bf0 + 17 * 1024, 4608)
    CST = HST
    MIX = A.at(bf0 + 13 * 1024, 2048)
    SMALL = A.f32(1024)
    SMALL2 = A.f32(160)
    SGS = [Bf[4], Bf[6]]
    SGO = [Bf[12], Bf[10]]
    out_dmas = []

    def sm(o, n):
        return SMALL[:, o:o + n]

    def gset(par):
        o = 80 * par
        names = (('BETA', 8), ('BETAN', 8), ('G8', 8), ('TMP8', 8), ('DT16', 16), ('A16', 16), ('TMP16', 16))
        d = {}
        for nm, n in names:
            d[nm] = SMALL2[:, o:o + n]
            o += n
        return d
    GS = [gset(0), gset(1)]

    def emit_gating(tt_):
        g_ = GS[tt_ % 2]
        TMs_ = sm(950 + 32 * (tt_ % 2), 32)
        P.dma(TMs_, P_tm[tt_ * 128:(tt_ + 1) * 128, 2048:TMW])
        BETA_, BETAN_, G8_, TMP8_, DT16_, A16_, TMP16_ = (g_[k] for k in ('BETA', 'BETAN', 'G8', 'TMP8', 'DT16', 'A16', 'TMP16'))
        P.actf(BETA_, TMs_[:, 0:8], AF.Exp, scale=-1.0)
        P.ts(BETA_, BETA_, 1.0, None, ALU.add)
        P.add('dve', lambda e: e.reciprocal(BETA_, BETA_), r=[BETA_], w=[BETA_])
        P.ts(BETAN_, BETA_, -1.0, None, ALU.mult)
        P.tt(TMP8_, TMs_[:, 8:16], GDB, ALU.add)
        P.actf(TMP8_, TMP8_, AF.Exp)
        P.actf(TMP8_, TMP8_, AF.Ln, bias=1.0)
        P.tt(G8_, TMP8_, NAG, ALU.mult)
        P.tt(TMP16_, TMs_[:, 16:32], SDB, ALU.add)
        P.actf(TMP16_, TMP16_, AF.Exp)
        P.actf(DT16_, TMP16_, AF.Ln, bias=1.0)
        P.tt(A16_, DT16_, NAS, ALU.mult)
    EGK = sm(80, 16)
    EG = sm(80, 8)
    EKD = sm(88, 8)
    BEG = sm(96, 8)
    RQ = sm(104, 8)
    RK = sm(112, 8)
    SSQ = sm(120, 8)
    EAA = sm(128, 32)
    EA = sm(128, 16)
    EAL = sm(144, 16)
    SSO = sm(160, 8)
    SS2 = sm(168, 2)
    EGL = sm(176, 128)
    EALB = sm(304, 256)
    RHE = sm(560, 256)
    SSK = sm(816, 8)

    def bh(ap, n=8, w=128):
        return ap.unsqueeze(2).to_broadcast([128, n, w])

    def bm(ap, n=8, w=128):
        return ap.unsqueeze(1).to_broadcast([128, n, w])

    DK_SCALE = 128 ** -0.5

    deferred = [None]
    for t in range(NTILE):
        ty = 's' if t == 16 else 'p'
        nseq = 16 if ty == 's' else 1
        L = 8 if ty == 's' else 128
        emit_out = t >= 8
        ot = t - 8
        CVt = CV if t % 2 == 0 else RAW[:, 0:36 * 128]
        CV3 = v3(CVt, 36)
        if ty == 'p':
            b_lo = 0 if emit_out else 8
            P.dma(CV3[:, b_lo:36, :], P_cv[b_lo:36, :, t * 128:(t + 1) * 128].rearrange("b p t -> p b t"))
            if t == 15:
                RAW3 = v3(sm(840, 108), 36)
                P.dma(RAW3, P_fm[:, :, 3 + 2045:3 + 2048].rearrange("b p t -> p b t"))
        else:
            RAW4 = RAW.rearrange("p (b s t) -> p b s t", b=36, s=16)
            P.dma(v3(RAWC, 36), P_fm[:, :, 3 + 2048:3 + NT].rearrange("b p t -> p b t"))
            P.dma(HST[0:48, 0:3072], st_gconv)
            P.dma(HST[0:48, 3072:4608], st_sconv)
            for b0 in range(0, 36, 8):
                nb = min(8, 36 - b0)
                ps = pw()
                for b in range(nb):
                    P.tr(ps[:, b * 48:(b + 1) * 48], HST[0:48, (b0 + b) * 128:(b0 + b + 1) * 128], ID[0:48, 0:48])
                P.cp(RAW4[:, b0:b0 + nb, :, 0:3],
                     ps[:, 0:nb * 48].rearrange("p (b s t) -> p b s t", b=nb, s=16), eng='act')
            P.cp(RAW4[:, :, :, 3:11], RAWC.rearrange("p (b s t) -> p b s t", b=36, s=16), eng='pool')
        if emit_out:
            P.dma(TM[:, 0:2048], P_tm[t * 128:(t + 1) * 128, 0:2048])
        cblocks = list(range(36)) if ty == 's' else []
        NPOOL = 8
        pool_blocks = cblocks[-NPOOL:]
        dve_blocks = cblocks[:-NPOOL]
        CTMP = v3(Bf[21], 8)
        if ty == 's':
            NPOOL = 8

        def cio(b):
            if ty == 'p':
                return CV3[:, b, :], [RAW3[:, b, i:i + 128] for i in range(4)], (lambda a: a)
            return (CV3[:, b, :].rearrange("p (s t) -> p s t", s=16), [RAW4[:, b, :, i:i + 8] for i in range(4)],
                    (lambda a: a.rearrange("p (s t) -> p s t", s=16)))
        for b in cblocks:
            o_, ins, _ = cio(b)
            P.actf(o_, ins[0], AF.Identity, scale=CW3[:, b, 0:1], bias=CW3[:, b, 4:5])
        for i in range(1, 4):
            for b in dve_blocks:
                o_, ins, _ = cio(b)
                P.stt(o_, ins[i], CW3[:, b, i:i + 1], o_, ALU.mult, ALU.add)
            for j, b in enumerate(pool_blocks):
                o_, ins, vw = cio(b)
                P.ts(vw(CTMP[:, j, :]), ins[i], CW3[:, b, i:i + 1], None, ALU.mult, eng='pool')
            for j, b in enumerate(pool_blocks):
                o_, ins, vw = cio(b)
                P.tt(o_, o_, vw(CTMP[:, j, :]), ALU.add, eng='pool')
        if ty == 's':
            P.actf(CV, CV, AF.Silu)
        if t == 15 or ty == 's':
            nr = 3 if ty == 'p' else 48
            if ty == 's':
                LST = A.at(bf0 + 15 * 1024, 36 * 48)
                P.cp(LST.rearrange("p (b s t) -> p b s t", b=36, s=16), RAW4[:, :, :, 8:11], eng='pool')
                LST3 = v3(LST, 36)
            for b0 in range(0, 36, 4):
                ph = phalf()
                for b in range(4):
                    src = RAW3[:, b0 + b, 0:3] if ty == 'p' else LST3[:, b0 + b, :]
                    P.tr(ph[0:nr, b * 128:(b + 1) * 128], src, ID)
                P.cp(CST[0:nr, b0 * 128:(b0 + 4) * 128], ph[0:nr, 0:512], eng='act')
            if ty == 'p':
                out_dmas.append(P.dma(gconv_p, CST[0:3, 0:3072]))
                out_dmas.append(P.dma(sconv_p, CST[0:3, 3072:4608]))
            else:
                out_dmas.append(P.dma(gconv_s, CST[0:48, 0:3072]))
                out_dmas.append(P.dma(sconv_s, CST[0:48, 3072:4608]))
        if t == 0:
            emit_gating(0)
        BETA, BETAN, G8, DT16, A16 = (GS[t % 2][k] for k in ('BETA', 'BETAN', 'G8', 'DT16', 'A16'))
        Q, SQ, KN, V, RH, DX, DM, QNT, KNT, QDT, NT0, QKM, N0, QKMT, R, NA, NB_, UTb, WTb, KDEC, VNb, OTb = Bf
        QN = Q

        def cgs(g):
            return slice(g * 512, (g + 1) * 512)

        def G(buf, g):
            return buf[:, g * 512:(g + 1) * 512]

        def G4(buf, g):
            return v3(buf[:, g * 512:(g + 1) * 512], 4)

        def h4(ap8, g):
            return bh(ap8[:, 4 * g:4 * g + 4], 4)

        def m4(ap):
            return bm(ap, 4)

        def lock(stages):
            for st in stages:
                for g in (0, 1):
                    st(g)

        pss = phalf()
        P.mm(pss[:, 0:8], UI[ty], G8)
        P.mm(pss[:, 8:16], MS[ty], G8)
        P.actf(EGK, pss[:, 0:16], AF.Exp)
        if ty == 'p':
            P.mm(pss[:, 16:24], ONES, G8)
        else:
            P.tt(RHE[:, 0:128].rearrange("p (s h) -> p s h", s=16),
                 G8.unsqueeze(1).to_broadcast([128, 16, 8]), RM.unsqueeze(2).to_broadcast([128, 16, 8]), ALU.mult)
            P.mm(pss[:, 16:16 + 128], ONES, RHE[:, 0:128])
        P.actf(EGL[:, 0:nseq * 8], pss[:, 16:16 + nseq * 8], AF.Exp)
        EGL3 = EGL[:, 0:nseq * 8].rearrange("p (s h) -> p s h", s=nseq)
        P.tt(BEG, BETA, EG, ALU.mult)

        def tr4(dst, srcs):
            ph = phalf()
            for hh in range(4):
                P.tr(ph[:, hh * 128:(hh + 1) * 128], srcs[hh], ID)
            P.cp(dst, ph, eng='act')

        def s_qT(g):
            if emit_out:
                tr4(G(Q, g), [CV3[:, 4 * g + hh, :] for hh in range(4)])

        def s_kT(g):
            tr4(G(KN, g), [CV3[:, 8 + 4 * g + hh, :] for hh in range(4)])

        def s_vT(g):
            tr4(G(V, g), [CV3[:, 16 + 4 * g + hh, :] for hh in range(4)])

        def norm_(X_, R_, S_, sc, g):
            P.tt(G(SQ, g), G(X_, g), G(X_, g), ALU.mult)
            P.red(S_[:, 4 * g:4 * g + 4], G4(SQ, g))
            r_ = R_[:, 4 * g:4 * g + 4]
            P.actf(r_, S_[:, 4 * g:4 * g + 4], AF.Ln, bias=EPS)
            P.actf(r_, r_, AF.Exp, scale=-0.5, bias=(LNDK if sc != 1.0 else None))
            P.tt(G4(X_, g), G4(X_, g), h4(R_, g), ALU.mult)

        def s_qn(g):
            if emit_out:
                norm_(Q, RQ, SSQ, DK_SCALE, g)

        def s_kn(g):
            norm_(KN, RK, SSK, 1.0, g)

        def s_decay(g):
            P.tt(G4(RH, g), m4(MS[ty]), h4(G8, g), ALU.mult)
            ph = phalf()
            P.mm(ph, UI[ty], G(RH, g))
            P.actf(G(DX, g), ph, AF.Exp)
            if emit_out:
                P.tt(G4(DM, g), G4(DX, g), m4(MI[ty]), ALU.mult)
            P.tt(G4(DX, g), G4(DX, g), m4(MS[ty]), ALU.mult)
            P.tt(G4(DX, g), G4(DX, g), h4(BETAN, g), ALU.mult)

        def s_qnT(g):
            if emit_out:
                tr4(G(QNT, g), [QN[:, (4 * g + hh) * 128:(4 * g + hh + 1) * 128] for hh in range(4)])

        def s_knT(g):
            tr4(G(KNT, g), [KN[:, (4 * g + hh) * 128:(4 * g + hh + 1) * 128] for hh in range(4)])

        def s_qdT(g):
            if emit_out:
                P.tt(G4(SQ, g), G4(QN, g), h4(EG, g), ALU.mult)
                tr4(G(QDT, g), [SQ[:, (4 * g + hh) * 128:(4 * g + hh + 1) * 128] for hh in range(4)])

        def mm4(lh, rh, g):
            ph = phalf()
            for hh in range(4):
                sl = slice((4 * g + hh) * 128, (4 * g + hh + 1) * 128)
                P.mm(ph[:, hh * 128:(hh + 1) * 128], lh[:, sl], rh[:, sl])
            return ph

        def s_gram(g):
            ph = mm4(KNT, KNT, g)
            P.tt(G(NT0, g), ph, G(DX, g), ALU.mult)

        def s_qk(g):
            if emit_out:
                ph = mm4(QNT, KNT, g)
                P.tt(G(QKM, g), ph, G(DM, g), ALU.mult)

        def s_n0(g):
            tr4(G(N0, g), [NT0[:, (4 * g + hh) * 128:(4 * g + hh + 1) * 128] for hh in range(4)])
            P.tt(G4(R, g), G4(N0, g), m4(ID), ALU.add)

        def s_qkT(g):
            if emit_out:
                tr4(G(QKMT, g), [QKM[:, (4 * g + hh) * 128:(4 * g + hh + 1) * 128] for hh in range(4)])

        lock([s_kT, s_vT, s_qT, s_kn])
        if deferred[0] is not None:
            deferred[0]()
            deferred[0] = None
        lock([s_decay, s_qn, s_knT, s_gram, s_qnT, s_n0, s_qdT, s_qk, s_qkT])
        if t + 1 < NTILE:
            emit_gating(t + 1)

        nsteps = 6 if ty == 'p' else 2
        Nc, NTc = N0, NT0
        pp = [(NA, NB_), (RH, DM)]
        for s_ in range(1, nsteps + 1):
            Nn, NTn = pp[s_ % 2]

            def d_sq(g, Nc=Nc, NTc=NTc, Nn=Nn, NTn=NTn, s_=s_):
                psb = mm4(Nc, NTc, g)
                P.cp(G(NTn, g), psb, eng='act')
                if s_ < nsteps:
                    psa = mm4(NTc, Nc, g)
                    P.cp(G(Nn, g), psa, eng='dve')

            def d_r(g, NTn=NTn):
                psc = mm4(NTn, R, g)
                P.tt(G(R, g), G(R, g), psc, ALU.add)

            lock([d_sq, d_r])
            Nc, NTc = Nn, NTn

        VBb = NA
        KBG = NB_
        VZ = UTb

        def s_prep(g):
            P.tt(G4(VBb, g), G4(V, g), h4(BETA, g), ALU.mult)
            P.tt(G4(KBG, g), G4(KN, g), h4(BEG, g), ALU.mult)
            P.tt(G4(KDEC, g), G4(KN, g), h4(EKD, g), ALU.mult)

        def s_u(g):
            ph = mm4(VBb, R, g)
            P.cp(G(UTb, g), ph, eng='act')

        def s_w(g):
            ph = mm4(KBG, R, g)
            P.cp(G(WTb, g), ph, eng='act')

        VNT = SQ

        def load_S(g, s, bufs):
            S_s = bufs[s % len(bufs)]
            P.dma(G4(S_s, g), st_g[s, 4 * g:4 * g + 4].rearrange("h k v -> k h v"), q='pool')
            return S_s

        def s_scanA(g):
            psA = phalf()
            psB = phalf() if emit_out else None
            for s in range(nseq):
                S_s = SG if ty == 'p' else load_S(g, s, [RH, DM, N0, NT0])
                for hh in range(4):
                    h = 4 * g + hh
                    P.mm(psA[:, hh * 128 + s * L:hh * 128 + (s + 1) * L], S_s[:, h * 128:(h + 1) * 128],
                         WTb[:, h * 128 + s * L:h * 128 + (s + 1) * L])
                if emit_out:
                    for hh in range(4):
                        h = 4 * g + hh
                        P.mm(psB[:, hh * 128 + s * L:hh * 128 + (s + 1) * L], S_s[:, h * 128:(h + 1) * 128],
                             QDT[:, h * 128 + s * L:h * 128 + (s + 1) * L])
            P.tt(G(VNT, g), G(UTb, g), psA, ALU.subtract)
            if emit_out:
                P.cp(G(OTb, g), psB, eng='act')

        def s_vn(g):
            tr4(G(VNb, g), [VNT[:, (4 * g + hh) * 128:(4 * g + hh + 1) * 128] for hh in range(4)])

        def s_c(g):
            if emit_out:
                psC = mm4(VNb, QKMT, g)
                P.tt(G(OTb, g), G(OTb, g), psC, ALU.add)

        def s_upd(g):
            for s in range(nseq):
                if ty == 'p':
                    S_s, VZs, Sn = SG, VNb, SG
                else:
                    S_s = load_S(g, s, [RH, DM, QNT, KNT])
                    VZs = [UTb, WTb][s % 2]
                    P.ts(G(VZs, g), G(VNb, g), RM[:, s:s + 1], None, ALU.mult)
                    Sn = [N0, NT0, QDT, QKM][s % 4]
                psn = mm4(KDEC, VZs, g)
                P.tt(G4(Sn, g), G4(S_s, g), bh(EGL3[:, s, 4 * g:4 * g + 4], 4), ALU.mult)
                P.tt(G(Sn, g), G(Sn, g), psn, ALU.add)
                if ty == 's':
                    out_dmas.append(P.dma(gst_s[s, 4 * g:4 * g + 4].rearrange("h k v -> k h v"), G4(Sn, g)))
            if t == 7:
                P.ts(G(SG, g), G(SG, g), FLAG, None, ALU.mult)
            if t == 15:
                out_dmas.append(P.dma(gst_p[4 * g:4 * g + 4].rearrange("h k v -> k h v"), G4(SG, g)))

        O = Q

        def s_out(g):
            if not emit_out:
                return
            ph = phalf()
            for hh in range(4):
                h = 4 * g + hh
                P.tr(ph[:, hh * 128:(hh + 1) * 128], OTb[:, h * 128:(h + 1) * 128], ID)
            P.cp(G(O, g), ph, eng='act')
            P.tt(G(SQ, g), G(O, g), G(O, g), ALU.mult)
            so = SSO[:, 4 * g:4 * g + 4]
            P.red(so, G4(SQ, g))
            P.actf(so, so, AF.Ln, scale=1.0 / 128, bias=EPS)
            P.actf(so, so, AF.Exp, scale=-0.5)
            P.tt(G4(O, g), G4(O, g), h4(SSO, g), ALU.mult)
            P.tt(G4(O, g), G4(O, g), m4(GNW), ALU.mult)
            P.tt(MIX[:, g * 512:(g + 1) * 512], G(O, g), TM[:, g * 512:(g + 1) * 512], ALU.mult)

        if emit_out:
            P.actf(TM[:, 0:2048], TM[:, 0:2048], AF.Silu)
        lock([s_prep, s_u, s_w, s_scanA, s_vn, s_c, s_upd, s_out])

        XSb, XDT, XW, DXT0, DXT1, Yb, T1a, YOT, HTS, HTN, XZ = Bf[0:11]
        BTM = Bf[11][:, 0:256]
        CBM = Bf[11][:, 256:512]
        HNAT = Bf[12]
        T1b = Bf[15]
        T1 = [T1a, T1b]
        DXT = [DXT0, DXT1]
        PSO = [PW[3][:, 0:512], PW[3][:, 512:1024]]

        def G8h(buf, g):
            return v3(buf[:, g * 512:(g + 1) * 512], 8)

        def h8(ap16, g):
            return bh(ap16[:, 8 * g:8 * g + 8], 8, 64)

        psb_ = phalf()
        for g in range(2):
            P.tr(psb_[:, g * 128:(g + 1) * 128], CV3[:, 32 + g, :], ID)
        P.cp(BTM, psb_[:, 0:256], eng='act')
        pss = phalf()
        P.mm(pss[:, 0:16], UI[ty], A16)
        P.mm(pss[:, 16:32], MS[ty], A16)
        P.actf(EAA, pss[:, 0:32], AF.Exp)
        if ty == 'p':
            P.mm(pss[:, 32:48], ONES, A16)
        else:
            P.tt(RHE.rearrange("p (s h) -> p s h", s=16),
                 A16.unsqueeze(1).to_broadcast([128, 16, 16]), RM.unsqueeze(2).to_broadcast([128, 16, 16]), ALU.mult)
            P.mm(pss[:, 32:32 + 256], ONES, RHE)
        P.actf(EALB[:, 0:nseq * 16], pss[:, 32:32 + nseq * 16], AF.Exp)
        EALB3 = EALB[:, 0:nseq * 16].rearrange("p (s h) -> p s h", s=nseq)

        def t_xs(g):
            tr4(G(XSb, g), [CV3[:, 24 + 4 * g + bb, :] for bb in range(4)])

        def t_decay(g):
            if not emit_out:
                return
            P.tt(v3(T1[g], 8), bm(UI[ty]), bh(A16[:, g * 8:(g + 1) * 8]), ALU.mult)
            for hf in range(2):
                ph = phalf()
                P.mm(ph, MS[ty], T1[g][:, hf * 512:(hf + 1) * 512])
                P.actf(DXT[g][:, hf * 512:(hf + 1) * 512], ph, AF.Exp)

        def t_cb(g):
            if not emit_out:
                return
            ph = phalf()
            P.mm(ph[:, 0:128], CV3[:, 32 + g, :], CV3[:, 34 + g, :])
            cbm = CBM[:, g * 128:(g + 1) * 128]
            P.tt(cbm, ph[:, 0:128], UI[ty], ALU.mult)
            for hf in range(2):
                d_ = v3(DXT[g][:, hf * 512:(hf + 1) * 512], 4)
                P.tt(d_, d_, bm(cbm, 4), ALU.mult)

        def t_xdt(g):
            P.tt(G8h(XDT, g), G8h(XSb, g), h8(DT16, g), ALU.mult)
            P.tt(G8h(XW, g), G8h(XDT, g), h8(EAL, g), ALU.mult)

        def t_ydiag(g):
            if not emit_out:
                return
            ph = phalf()
            for hh in range(8):
                h = 8 * g + hh
                P.mm(ph[:, hh * 64:(hh + 1) * 64], DXT[g][:, hh * 128:(hh + 1) * 128], XDT[:, h * 64:(h + 1) * 64])
            P.cp(G(Yb, g), ph, eng='act')

        def t_state(g):
            pso = PSO[g]
            for s in range(nseq):
                if ty == 'p':
                    H_s = HT
                else:
                    hin = [Bf[12], Bf[16]][s % 2]
                    P.dma(v3(G(hin, g), 4),
                          st_s[s, 8 * g:8 * g + 8].rearrange("(b h2) p n -> (h2 p) b n", h2=2), q='pool')
                    H_s = [Bf[8], Bf[19]][s % 2]
                    tr4(G(H_s, g), [hin[:, (4 * g + bb) * 128:(4 * g + bb + 1) * 128] for bb in range(4)])
                if emit_out:
                    for bb in range(4):
                        b = 4 * g + bb
                        P.mm(pso[:, bb * 128 + s * L:bb * 128 + (s + 1) * L], H_s[:, b * 128:(b + 1) * 128],
                             CV3[:, 34 + g, s * L:(s + 1) * L])
                if ty == 'p':
                    XZs, Hn = XW, HT
                else:
                    XZs, Hn = [Bf[10], Bf[21]][s % 2], [Bf[9], Bf[20]][s % 2]
                    P.ts(G(XZs, g), G(XW, g), RM[:, s:s + 1], None, ALU.mult)
                psn = phalf()
                P.mm(psn, BTM[:, g * 128:(g + 1) * 128], G(XZs, g))
                P.tt(G8h(Hn, g), G8h(H_s, g), bh(EALB3[:, s, 8 * g:8 * g + 8], 8, 64), ALU.mult)
                P.tt(G(Hn, g), G(Hn, g), psn, ALU.add)
                if ty == 's':
                    hout = [Bf[17], Bf[18]][s % 2]
                    tr4(G(hout, g), [Hn[:, (4 * g + bb) * 128:(4 * g + bb + 1) * 128] for bb in range(4)])
                    out_dmas.append(P.dma(sst_s[s, 8 * g:8 * g + 8].rearrange("(b h2) p n -> (h2 p) b n", h2=2),
                                          v3(G(hout, g), 4)))
            if emit_out:
                P.cp(G(YOT, g), pso, eng='act')
            if t == 7:
                P.ts(G(HT, g), G(HT, g), FLAG, None, ALU.mult)
            if t == 15:
                tr4(G(HNAT, g), [HT[:, (4 * g + bb) * 128:(4 * g + bb + 1) * 128] for bb in range(4)])
                out_dmas.append(P.dma(sst_p[8 * g:8 * g + 8].rearrange("(b h2) p n -> (h2 p) b n", h2=2),
                                      v3(G(HNAT, g), 4)))

        def t_y(g):
            if not emit_out:
                return
            ph = phalf()
            for bb in range(4):
                b = 4 * g + bb
                P.tr(ph[:, bb * 128:(bb + 1) * 128], YOT[:, b * 128:(b + 1) * 128], ID)
            Tg = T1[g][:, 0:512]
            T8 = v3(Tg, 8)
            P.tt(T8, v3(ph, 8), h8(EA, g), ALU.mult)
            P.tt(G(Yb, g), G(Yb, g), Tg, ALU.add)
            P.tt(T8, G8h(XSb, g), h8(SD, g), ALU.mult)
            P.tt(G(Yb, g), G(Yb, g), Tg, ALU.add)
            P.tt(G(Yb, g), G(Yb, g), TM[:, 1024 + g * 512:1024 + (g + 1) * 512], ALU.mult)
            P.tt(Tg, G(Yb, g), G(Yb, g), ALU.mult)
            s2 = SS2[:, g:g + 1]
            P.red(s2, Tg)
            P.actf(s2, s2, AF.Ln, scale=1.0 / 512, bias=EPS)
            P.actf(s2, s2, AF.Exp, scale=-0.5)
            P.ts(G(Yb, g), G(Yb, g), s2, None, ALU.mult)
            P.tt(MIX[:, 1024 + g * 512:1024 + (g + 1) * 512], G(Yb, g), SNW[:, g * 512:(g + 1) * 512], ALU.mult)

        lock([t_xs, t_decay, t_xdt, t_cb, t_ydiag, t_state, t_y])
        if emit_out:
            def mixT(ot=ot):
                for kq in range(4):
                    ph = phalf()
                    for j in range(4):
                        kc = kq * 4 + j
                        P.tr(ph[:, j * 128:(j + 1) * 128], MIX[:, kc * 128:(kc + 1) * 128], ID)
                    evac(MIXT3[:, kq * 4:kq * 4 + 4, ot * 128:(ot + 1) * 128], v3(ph, 4))
            if t + 1 < NTILE:
                deferred[0] = mixT
            else:
                mixT()
    P.fence()

    A.top = persist_top + 8 * NOUT
    WOC = [A.bf16(16 * 512) for _ in range(2)]
    r1_end = A.top
    X1 = A.f32(9 * 2048)
    X13 = v3(X1, 9)
    H2T = A.bf16(16 * NOUT)
    H2T3 = v3(H2T, 16)
    XN2 = A.f32(2048)
    SQJ2 = A.bf16(2048)
    SSc = A.f32(1)
    RSc = A.f32(1)
    w_out_v = w_out.rearrange("(kc p) c -> p kc c", p=128)
    for ot in range(9):
        P.dma(X13[:, ot, :], xin[1024 + ot * 128:1024 + (ot + 1) * 128, :])
    for dq in range(4):
        woc = v3(WOC[dq % 2], 16)
        P.dma(woc, w_out_v[:, :, dq * 512:(dq + 1) * 512], q='pool')
        for ot in range(9):
            ph = phalf()
            for kc in range(16):
                P.mm(ph, MIXT3[:, kc, ot * 128:(ot + 1) * 128], woc[:, kc, :], start=(kc == 0), stop=(kc == 15))
            xs_ = X13[:, ot, dq * 512:(dq + 1) * 512]
            P.tt(xs_, xs_, ph, ALU.add)
    for ot in range(9):
        rms_to_hT(X13[:, ot, :], FNW, H2T3[:, :, ot * 128:(ot + 1) * 128], SQJ2, XN2, SSc, RSc)
    P.fence()
    A.top = persist_top
    NFB = 4
    FFT = A.bf16(NFB * NOUT)
    FFT3 = v3(FFT, NFB)
    WG = [A.bf16(16 * 128) for _ in range(2)]
    WU = [A.bf16(16 * 128) for _ in range(2)]
    WD = [A.bf16(NFB * 2048) for _ in range(2)]
    SIL = [A.f32(512) for _ in range(2)]
    assert A.top <= r1_end, (A.top, r1_end)
    w_gate_v = w_gate.rearrange("(kc p) c -> p kc c", p=128)
    w_up_v = w_up.rearrange("(kc p) c -> p kc c", p=128)
    tchunks = [(0, 512), (512, 512), (1024, 128)]
    groups = []
    b0 = 0
    while b0 < 44:
        nb = min(NFB, 44 - b0)
        groups.append((b0, nb))
        b0 += nb
    cnt = 0
    for gi, (b0, nb) in enumerate(groups):
        wd = WD[gi % 2]
        wd3 = v3(wd, NFB)
        P.dma(wd3[:, 0:nb, :], w_down[b0 * 128:(b0 + nb) * 128, :].rearrange("(b p) d -> p b d", p=128), q='pool')
        for j in range(nb):
            fb = b0 + j
            wg = v3(WG[fb % 2], 16)
            wu = v3(WU[fb % 2], 16)
            P.dma(wg, w_gate_v[:, :, fb * 128:(fb + 1) * 128], q='pool')
            P.dma(wu, w_up_v[:, :, fb * 128:(fb + 1) * 128], q='pool')
            for (t0, n) in tchunks:
                pg = phalf()
                for kc in range(16):
                    P.mm(pg[:, 0:n], wg[:, kc, :], H2T3[:, kc, t0:t0 + n], start=(kc == 0), stop=(kc == 15))
                pu = phalf()
                for kc in range(16):
                    P.mm(pu[:, 0:n], wu[:, kc, :], H2T3[:, kc, t0:t0 + n], start=(kc == 0), stop=(kc == 15))
                sl_ = SIL[cnt % 2]
                cnt += 1
                P.actf(sl_[:, 0:n], pg[:, 0:n], AF.Silu)
                P.tt(FFT3[:, j, t0:t0 + n], sl_[:, 0:n], pu[:, 0:n], ALU.mult)
        for ot in range(9):
            for dq in range(4):
                ph = phalf()
                for j in range(nb):
                    P.mm(ph, FFT3[:, j, ot * 128:(ot + 1) * 128], wd3[:, j, dq * 512:(dq + 1) * 512],
                         start=(j == 0), stop=(j == nb - 1))
                xs_ = X13[:, ot, dq * 512:(dq + 1) * 512]
                P.tt(xs_, xs_, ph, ALU.add)
    YB = [WD[0].bitcast(F32)[:, 0:2048], WD[1].bitcast(F32)[:, 0:2048]]
    for ot in range(9):
        xt = X13[:, ot, :]
        yb = YB[ot % 2]
        P.actf(XN2, xt, AF.Square, accum=SSc)
        P.actf(RSc, SSc, AF.Sqrt, scale=1.0 / D, bias=EPS)
        P.add('dve', lambda e: e.reciprocal(RSc, RSc), r=[RSc], w=[RSc])
        P.stt(yb, xt, RSc, FINW, ALU.mult, ALU.mult)
        out_dmas.append(P.dma(y_o[ot * 128:(ot + 1) * 128, :], yb))
    stats = P.emit(final_wait_ops=out_dmas)
    es.close()
    return nc, stats


_CACHE = {}


def kernel(x_prompt, x_sample, state_gdn_conv, state_gdn, state_ssm_conv, state_ssm,
           attn_norm_w, w_in, gdn_conv_w, gdn_A_log, gdn_dt_bias, gdn_norm_w,
           ssm_conv_w, ssm_conv_b, ssm_A_log, ssm_dt_bias, ssm_D, ssm_norm_w,
           w_out, ffn_norm_w, w_gate, w_up, w_down, final_norm_w):
    f = lambda a: np.ascontiguousarray(np.asarray(a, dtype=np.float32))
    x_prompt = f(x_prompt)
    x_sample = f(x_sample)
    if 'nc' not in _CACHE:
        _CACHE['nc'] = build_program()[0]
    nc = _CACHE['nc']
    shared = dict(
        attn_norm_w=f(attn_norm_w).reshape(16, 128), w_in=f(w_in)[0], gdn_conv_w=f(gdn_conv_w)[0],
        gdn_A_log=f(gdn_A_log).reshape(1, 8), gdn_dt_bias=f(gdn_dt_bias).reshape(1, 8),
        gdn_norm_w=f(gdn_norm_w).reshape(1, 128), ssm_conv_w=f(ssm_conv_w)[0],
        ssm_conv_b=f(ssm_conv_b).reshape(1, 1536), ssm_A_log=f(ssm_A_log).reshape(1, 16),
        ssm_dt_bias=f(ssm_dt_bias).reshape(1, 16), ssm_D=f(ssm_D).reshape(1, 16),
        ssm_norm_w=f(ssm_norm_w).reshape(1, 1024), w_out=f(w_out)[0],
        ffn_norm_w=f(ffn_norm_w).reshape(16, 128), w_gate=f(w_gate)[0], w_up=f(w_up)[0],
        w_down=f(w_down)[0], final_norm_w=f(final_norm_w).reshape(1, D))
    sgc = f(state_gdn_conv)[0]
    sg = f(state_gdn)[0]
    ssc = f(state_ssm_conv)[0]
    ssm = f(state_ssm)[0]
    in_maps = []
    for c in range(8):
        b, r = c // 2, c % 2
        xin = np.zeros((NT, D), np.float32)
        if r == 1:
            xin[0:1024] = x_prompt[b, 0:1024]
        xin[1024:2048] = x_prompt[b, r * 1024:(r + 1) * 1024]
        xin[2048:] = x_sample[16 * c:16 * c + 16].reshape(128, D)
        m = dict(shared)
        m.update(xin=xin, flag=np.full((128, 1), float(r), np.float32),
                 st_gconv=np.ascontiguousarray(sgc[16 * c:16 * c + 16].reshape(48, 3072)),
                 st_g=np.ascontiguousarray(sg[16 * c:16 * c + 16]),
                 st_sconv=np.ascontiguousarray(ssc[16 * c:16 * c + 16].reshape(48, 1536)),
                 st_s=np.ascontiguousarray(ssm[16 * c:16 * c + 16]))
        in_maps.append(m)
    res = run_bass_kernel_spmd(nc, in_maps, core_ids=list(range(8)))
    R = res.results
    y_prompt = np.zeros((4, 2048, D), np.float32)
    y_sample = np.zeros((128, 8, D), np.float32)
    gconv_p = np.zeros((1, 4, 3, 3072), np.float32)
    gst_p = np.zeros((1, 4, 8, 128, 128), np.float32)
    sconv_p = np.zeros((1, 4, 3, 1536), np.float32)
    sst_p = np.zeros((1, 4, 16, 64, 128), np.float32)
    gconv_s = np.zeros((1, 128, 3, 3072), np.float32)
    gst_s = np.zeros((1, 128, 8, 128, 128), np.float32)
    sconv_s = np.zeros((1, 128, 3, 1536), np.float32)
    sst_s = np.zeros((1, 128, 16, 64, 128), np.float32)
    for c in range(8):
        b, r = c // 2, c % 2
        o = R[c]
        y_prompt[b, r * 1024:(r + 1) * 1024] = o['y'][0:1024]
        y_sample[16 * c:16 * c + 16] = o['y'][1024:].reshape(16, 8, D)
        if r == 1:
            gconv_p[0, b] = o['gconv_p']
            gst_p[0, b] = o['gst_p']
            sconv_p[0, b] = o['sconv_p']
            sst_p[0, b] = o['sst_p']
        gconv_s[0, 16 * c:16 * c + 16] = o['gconv_s'].reshape(16, 3, 3072)
        gst_s[0, 16 * c:16 * c + 16] = o['gst_s']
        sconv_s[0, 16 * c:16 * c + 16] = o['sconv_s'].reshape(16, 3, 1536)
        sst_s[0, 16 * c:16 * c + 16] = o['sst_s']
    return (y_prompt, y_sample, gconv_p, gst_p, sconv_p, sst_p, gconv_s, gst_s, sconv_s, sst_s)
```

```python
import numpy as np
import concourse.bass as bass
import concourse.mybir as mybir

F32 = mybir.dt.float32
BF16 = mybir.dt.bfloat16
ALU = mybir.AluOpType
AF = mybir.ActivationFunctionType
AX = mybir.AxisListType

_ES = {F32: 4, BF16: 2}


def _esize(dt):
    if dt in _ES:
        return _ES[dt]
    s = str(dt)
    if '32' in s:
        return 4
    if '16' in s:
        return 2
    if '64' in s:
        return 8
    return 1


def footprint(ap):
    name = ap.tensor.name
    dims = ap.ap
    es = _esize(ap.dtype)
    off = ap.offset
    space = str(ap.space)
    if 'DRAM' in space.upper() or 'HBM' in space.upper() or 'Dram' in space:
        ext = 1
        for st, cnt in dims:
            ext += (cnt - 1) * abs(st)
        return (name, True, 0, 1, off * es, (off + ext) * es)
    pst, pcnt = dims[0]
    if pst == 0:
        p0 = 0
        lo = off
        pcnt_eff = 1
    else:
        p0 = off // pst
        lo = off % pst
        pcnt_eff = pcnt
    ext = 1
    for st, cnt in dims[1:]:
        ext += (cnt - 1) * abs(st)
    lo_b, hi_b = lo * es, (lo + ext) * es
    if name.startswith('pw'):
        lo_b = (lo_b // 2048) * 2048
        hi_b = ((hi_b + 2047) // 2048) * 2048
        return (name, False, 0, 128, lo_b, hi_b)
    return (name, False, p0, p0 + pcnt_eff, lo_b, hi_b)


class Op:
    __slots__ = ('eng', 'fn', 'deps', 'dma', 'token', 'prewait', 'needed', 'idx')


class Prog:
    ENGS = ('pe', 'act', 'dve', 'pool', 'sp')
    NS = 8

    def __init__(self, nc):
        self.nc = nc
        self.ops = []
        self.acc = {}
        self.last_on_eng = {}

    def _track(self, idx, aps_r, aps_w):
        deps = set()
        for is_w, aps in ((False, aps_r), (True, aps_w)):
            for ap in aps:
                name, isd, p0, p1, lo, hi = footprint(ap)
                lst = self.acc.setdefault(name, [])
                keep = []
                for e in lst:
                    ov = not (e[2] <= p0 or e[1] >= p1 or e[4] <= lo or e[3] >= hi)
                    if ov and (is_w or e[5]):
                        if e[0] != idx:
                            deps.add(e[0])
                        if is_w and e[1] >= p0 and e[2] <= p1 and e[3] >= lo and e[4] <= hi:
                            continue
                    keep.append(e)
                keep.append([idx, p0, p1, lo, hi, is_w])
                self.acc[name] = keep
        return deps

    def add(self, eng, fn, r=(), w=(), dma=False, extra_deps=()):
        op = Op()
        op.eng = eng
        op.fn = fn
        op.dma = dma
        op.idx = len(self.ops)
        op.deps = self._track(op.idx, r, w)
        op.deps.update(extra_deps)
        op.token = None
        op.prewait = None
        op.needed = False
        self.ops.append(op)
        self.last_on_eng[eng] = op.idx
        return op.idx

    def fence(self):
        last = []
        for e in self.ENGS:
            pass
        idxs = set()
        seen_eng = set()
        for op in reversed(self.ops):
            if op.dma:
                idxs.add(op.idx)
            elif op.eng not in seen_eng:
                seen_eng.add(op.eng)
                idxs.add(op.idx)
        lf = getattr(self, '_last_fence', 0)
        idxs = {i for i in idxs if i >= lf or not self.ops[i].dma}
        for e in self.ENGS:
            self.add(e, None, extra_deps=set(idxs))
        self._last_fence = len(self.ops)

    def mm(self, out, lhsT, rhs, start=True, stop=True):
        return self.add('pe', lambda e: e.matmul(out, lhsT, rhs, start=start, stop=stop),
                        r=[lhsT, rhs], w=[out])

    def tr(self, out, in_, ident):
        return self.add('pe', lambda e: e.transpose(out, in_, ident), r=[in_, ident], w=[out])

    def actf(self, out, in_, func, bias=None, scale=None, accum=None, eng='act'):
        kw = {}
        r = [in_]
        w = [out]
        if bias is not None:
            kw['bias'] = bias
            if not isinstance(bias, (int, float)):
                r.append(bias)
        if scale is not None:
            kw['scale'] = scale
            if not isinstance(scale, (int, float)):
                r.append(scale)
        if accum is not None:
            kw['accum_out'] = accum
            w.append(accum)
        return self.add(eng, lambda e: e.activation(out, in_, func, **kw), r=r, w=w)

    def tt(self, out, a, b, op, eng='dve'):
        return self.add(eng, lambda e: e.tensor_tensor(out, a, b, op), r=[a, b], w=[out])

    def ts(self, out, a, s1, s2, op0, op1=None, eng='dve', accum=None):
        r = [a]
        if not isinstance(s1, (int, float)):
            r.append(s1)
        if s2 is not None and not isinstance(s2, (int, float)):
            r.append(s2)
        w = [out]
        kw = {}
        if accum is not None:
            kw['accum_out'] = accum
            w.append(accum)
        if op1 is None:
            if isinstance(s1, (int, float)):
                return self.add(eng, lambda e: e.tensor_scalar(out, a, s1, None, op0, **kw), r=r, w=w)
            return self.add(eng, lambda e: e.tensor_scalar(out, a, s1, 0.0, op0, ALU.add, **kw), r=r, w=w)
        return self.add(eng, lambda e: e.tensor_scalar(out, a, s1, s2, op0, op1, **kw), r=r, w=w)

    def stt(self, out, a, s, b, op0, op1, eng='dve'):
        r = [a, b]
        if not isinstance(s, (int, float)):
            r.append(s)
        return self.add(eng, lambda e: e.scalar_tensor_tensor(out, a, s, b, op0, op1), r=r, w=[out])

    def cp(self, out, in_, eng='dve'):
        if eng == 'act':
            return self.add('act', lambda e: e.copy(out, in_), r=[in_], w=[out])
        return self.add(eng, lambda e: e.tensor_copy(out, in_), r=[in_], w=[out])

    def red(self, out, in_, op=None, eng='dve'):
        op = op or ALU.add
        return self.add(eng, lambda e: e.tensor_reduce(out, in_, AX.X, op), r=[in_], w=[out])

    def memset(self, ap, val, eng='pool'):
        return self.add(eng, lambda e: e.memset(ap, val), w=[ap])

    def dma(self, out, in_, q='sp'):
        return self.add(q, lambda e: e.dma_start(out, in_), r=[in_], w=[out], dma=True)

    def emit(self, final_wait_ops=()):
        nc = self.nc
        ops = self.ops
        if final_wait_ops:
            self.add('sp', None, extra_deps=set(final_wait_ops))
        for op in ops:
            for d in op.deps:
                dop = ops[d]
                if dop.dma:
                    continue
                if dop.eng == 'pe' and op.eng == 'pe' and not op.dma:
                    continue
                dop.needed = True
        from contextlib import ExitStack
        es = ExitStack()
        sem = {e: es.enter_context(nc.semaphore('s_' + e)) for e in self.ENGS}
        dsem = {e: [es.enter_context(nc.semaphore('d_%s%d' % (e, i))) for i in range(self.NS)]
                for e in ('sp', 'act', 'pool')}
        cnt = {e: 0 for e in self.ENGS}
        dcnt = {e: 0 for e in dsem}
        for op in ops:
            if op.fn is None:
                continue
            if op.dma:
                m = dcnt[op.eng]
                s = dsem[op.eng][m % self.NS]
                op.token = (s, 16 * (m // self.NS + 1), 16)
                if m >= self.NS:
                    op.prewait = (s, 16 * (m // self.NS))
                dcnt[op.eng] = m + 1
            elif op.needed:
                cnt[op.eng] += 1
                op.token = (sem[op.eng], cnt[op.eng], 1)
        per_eng = {e: [op for op in ops if op.eng == e] for e in self.ENGS}
        stats = {e: [0, 0] for e in self.ENGS}

        def run(ename, eobj):
            known = {}
            for op in per_eng[ename]:
                waits = {}
                if op.prewait is not None:
                    s, v = op.prewait
                    waits[id(s)] = (s, v)
                for d in op.deps:
                    dop = ops[d]
                    if dop.token is None:
                        continue
                    if (not dop.dma) and dop.eng == 'pe' and ename == 'pe' and not op.dma:
                        continue
                    s, v, _ = dop.token
                    if id(s) not in waits or waits[id(s)][1] < v:
                        waits[id(s)] = (s, v)
                for k, (s, v) in waits.items():
                    if known.get(k, 0) >= v:
                        continue
                    eobj.wait_ge(s, v)
                    stats[ename][1] += 1
                    known[k] = v
                if op.fn is None:
                    continue
                ins = op.fn(eobj)
                stats[ename][0] += 1
                if op.token is not None:
                    ins.then_inc(op.token[0], op.token[2])

        with nc.Block() as block:
            @block.tensor
            def _(e):
                run('pe', e)

            @block.scalar
            def _(e):
                run('act', e)

            @block.vector
            def _(e):
                run('dve', e)

            @block.gpsimd
            def _(e):
                run('pool', e)

            @block.sync
            def _(e):
                run('sp', e)
        es.close()
        return stats

from contextlib import ExitStack
from concourse.bass_utils import run_bass_kernel_spmd

D = 2048
DIN = 6688
DFF = 5632
NT = 2176
NTILE = 17
NOUT = 1152
EPSV = 1e-6
PFW = 3 + NT
TMW = 2080


class Arena:
    def __init__(self, ar, total):
        self.ar = ar
        self.top = 0
        self.total = total

    def f32(self, n):
        o = self.top
        self.top += n
        assert self.top <= self.total, (self.top, self.total)
        return self.ar[:, o:o + n]

    def bf16(self, n):
        assert n % 2 == 0
        return self.f32(n // 2).bitcast(BF16)

    def at(self, o, n):
        return self.ar[:, o:o + n]


def v3(ap, a):
    return ap.rearrange("p (a b) -> p a b", a=a)


def build_program(debug=False):
    nc = bass.Bass("TRN2", target_bir_lowering=False)

    def din(name, shape):
        return nc.dram_tensor(name, list(shape), F32, kind="ExternalInput").ap()

    def dout(name, shape):
        return nc.dram_tensor(name, list(shape), F32, kind="ExternalOutput").ap()

    xin = din("xin", [NT, D])
    flag_d = din("flag", [128, 1])
    st_gconv = din("st_gconv", [48, 3072])
    st_g = din("st_g", [16, 8, 128, 128])
    st_sconv = din("st_sconv", [48, 1536])
    st_s = din("st_s", [16, 16, 64, 128])
    attn_norm_w = din("attn_norm_w", [16, 128])
    w_in = din("w_in", [D, DIN])
    gdn_conv_w = din("gdn_conv_w", [4, 3072])
    gdn_A_log = din("gdn_A_log", [1, 8])
    gdn_dt_bias = din("gdn_dt_bias", [1, 8])
    gdn_norm_w = din("gdn_norm_w", [1, 128])
    ssm_conv_w = din("ssm_conv_w", [4, 1536])
    ssm_conv_b = din("ssm_conv_b", [1, 1536])
    ssm_A_log = din("ssm_A_log", [1, 16])
    ssm_dt_bias = din("ssm_dt_bias", [1, 16])
    ssm_D = din("ssm_D", [1, 16])
    ssm_norm_w = din("ssm_norm_w", [1, 1024])
    w_out = din("w_out", [D, D])
    ffn_norm_w = din("ffn_norm_w", [16, 128])
    w_gate = din("w_gate", [D, DFF])
    w_up = din("w_up", [D, DFF])
    w_down = din("w_down", [DFF, D])
    final_norm_w = din("final_norm_w", [1, D])

    y_o = dout("y", [NOUT, D])
    gconv_p = dout("gconv_p", [3, 3072])
    gst_p = dout("gst_p", [8, 128, 128])
    sconv_p = dout("sconv_p", [3, 1536])
    sst_p = dout("sst_p", [16, 64, 128])
    gconv_s = dout("gconv_s", [48, 3072])
    gst_s = dout("gst_s", [16, 8, 128, 128])
    sconv_s = dout("sconv_s", [48, 1536])
    sst_s = dout("sst_s", [16, 16, 64, 128])

    P_fm = nc.dram_tensor("P_fm", [36, 128, PFW], F32).ap()
    P_tm = nc.dram_tensor("P_tm", [NT, TMW], F32).ap()
    P_cv = nc.dram_tensor("P_cv", [36, 128, 2048], F32).ap()

    es = ExitStack()
    TOT = 53200
    ar_t = es.enter_context(nc.sbuf_tensor("arena", [128, TOT], F32))
    PW = [es.enter_context(nc.psum_tensor("pw%d" % i, [128, 1024], F32)) for i in range(4)]
    A = Arena(ar_t, TOT)
    P = Prog(nc)
    pwc = [0]

    def pw():
        pwc[0] += 1
        return PW[pwc[0] % 3]
    pw_global = pw

    hb = [0]
    nbanks = [8]

    def phalf():
        hb[0] += 1
        k = hb[0] % nbanks[0]
        return PW[k // 2][:, (k % 2) * 512:(k % 2) * 512 + 512]

    ID = A.f32(128)
    ONES = A.f32(128)
    MI = {}
    MS = {}
    UI = {}
    for ty in ('p', 's'):
        MI[ty] = A.f32(128)
        MS[ty] = A.f32(128)
        UI[ty] = A.f32(128)
    BD = A.f32(128)
    RM = A.f32(16)
    EPS = A.f32(1)
    FLAG = A.f32(1)
    ANW = A.f32(16)
    FNW = A.f32(16)
    FINW = A.f32(2048)
    GNW = A.f32(128)
    SNW = A.f32(1024)
    SD = A.f32(16)
    NAG = A.f32(8)
    GDB = A.f32(8)
    NAS = A.f32(16)
    SDB = A.f32(16)
    CW = A.f32(36 * 5)
    CW3 = v3(CW, 36)
    ZERO = A.f32(128)
    LNDK = A.f32(1)

    def asel(ap, pattern, op, fill, base, cm):
        P.add('pool', lambda e: e.affine_select(ap, ap, pattern, op, fill, base=base, channel_multiplier=cm),
              r=[ap], w=[ap])

    P.memset(ID, 0.0)
    asel(ID, [[-1, 128]], ALU.not_equal, 1.0, 0, 1)
    P.memset(ONES, 1.0)
    P.memset(ZERO, 0.0)
    P.memset(EPS, EPSV)
    P.memset(LNDK, float(np.log(128 ** -0.5)))
    P.memset(BD, 1.0)
    asel(v3(BD, 16), [[-8, 16], [0, 8]], ALU.is_ge, 0.0, 0, 1)
    asel(v3(BD, 16), [[8, 16], [0, 8]], ALU.is_ge, 0.0, 7, -1)
    P.memset(RM, 1.0)
    asel(RM, [[-8, 16]], ALU.is_ge, 0.0, 0, 1)
    asel(RM, [[8, 16]], ALU.is_ge, 0.0, 7, -1)
    for ty in ('p', 's'):
        P.memset(MI[ty], 1.0)
        asel(MI[ty], [[-1, 128]], ALU.is_ge, 0.0, 0, 1)
        P.memset(MS[ty], 1.0)
        asel(MS[ty], [[-1, 128]], ALU.is_gt, 0.0, 0, 1)
        P.memset(UI[ty], 1.0)
        asel(UI[ty], [[1, 128]], ALU.is_ge, 0.0, 0, -1)
        if ty == 's':
            for m in (MI, MS, UI):
                P.tt(m[ty], m[ty], BD, ALU.mult, eng='pool')
    P.dma(FLAG, flag_d)
    P.dma(FINW, final_norm_w.broadcast_to([128, D]))
    P.dma(GNW, gdn_norm_w.broadcast_to([128, 128]))
    P.dma(SNW, ssm_norm_w.broadcast_to([128, 1024]))
    P.dma(SD, ssm_D.broadcast_to([128, 16]))
    P.dma(NAG, gdn_A_log.broadcast_to([128, 8]))
    P.dma(GDB, gdn_dt_bias.broadcast_to([128, 8]))
    P.dma(NAS, ssm_A_log.broadcast_to([128, 16]))
    P.dma(SDB, ssm_dt_bias.broadcast_to([128, 16]))
    P.actf(NAG, NAG, AF.Exp)
    P.ts(NAG, NAG, -1.0, None, ALU.mult)
    P.actf(NAS, NAS, AF.Exp)
    P.ts(NAS, NAS, -1.0, None, ALU.mult)

    mark0 = A.top
    TMPA = A.f32(4608 + 256)
    cwst = TMPA[0:5, 0:4608]
    P.memset(TMPA[0:5, 0:4608], 0.0)
    P.dma(TMPA[0:4, 0:3072], gdn_conv_w)
    P.dma(TMPA[0:4, 3072:4608], ssm_conv_w)
    P.dma(TMPA[4:5, 3072:4608], ssm_conv_b)
    P.dma(TMPA[0:16, 4608:4736], attn_norm_w)
    P.dma(TMPA[0:16, 4736:4864], ffn_norm_w)
    ps = pw()
    for b in range(36):
        P.tr(ps[:, b * 5:b * 5 + 5], TMPA[0:5, b * 128:(b + 1) * 128], ID[0:5, 0:5])
    P.cp(CW, ps[:, 0:180], eng='act')
    ps = pw()
    P.tr(ps[:, 0:16], TMPA[0:16, 4608:4736], ID[0:16, 0:16])
    P.tr(ps[:, 16:32], TMPA[0:16, 4736:4864], ID[0:16, 0:16])
    P.cp(ANW, ps[:, 0:16], eng='act')
    P.cp(FNW, ps[:, 16:32], eng='act')
    P.fence()
    A.top = mark0
    persist_top = A.top

    def rms_to_hT(xt, nw, hT_dst, tmp_sq, xn, ss, rs):
        P.actf(tmp_sq, xt, AF.Square, accum=ss)
        P.actf(rs, ss, AF.Sqrt, scale=1.0 / D, bias=EPS)
        P.add('dve', lambda e: e.reciprocal(rs, rs), r=[rs], w=[rs])
        P.ts(xn, xt, rs, None, ALU.mult)
        for kq in range(4):
            ph = phalf()
            for j in range(4):
                kc = kq * 4 + j
                P.tr(ph[:, j * 128:(j + 1) * 128], xn[:, kc * 128:(kc + 1) * 128], ID)
            P.tt(hT_dst[:, kq * 4:kq * 4 + 4, :], v3(ph, 4),
                 nw[:, kq * 4:kq * 4 + 4].unsqueeze(2).to_broadcast([128, 4, 128]), ALU.mult)

    hT = A.bf16(16 * NT)
    hT3 = v3(hT, 16)
    XS = [A.f32(2048) for _ in range(2)]
    XN = A.f32(2048)
    SQJ = A.bf16(2048)
    SS = A.f32(1)
    RS = A.f32(1)
    for t in range(NTILE):
        xt = XS[t % 2]
        P.dma(xt, xin[t * 128:(t + 1) * 128, :])
        rms_to_hT(xt, ANW, hT3[:, :, t * 128:(t + 1) * 128], SQJ, XN, SS, RS)

    Z3 = A.f32(36 * 3)
    P.memset(Z3, 0.0)
    P.dma(P_fm.rearrange("b p t -> p b t")[:, :, 0:3], v3(Z3, 36))

    WB = [A.bf16(16 * 128) for _ in range(3)]
    STG = [A.f32(3 + NT) for _ in range(2)]
    CVS = [A.f32(2048) for _ in range(2)]
    for st_ in STG:
        P.memset(st_[:, 0:3], 0.0)
    w_in_v = w_in.rearrange("(kc p) c -> p kc c", p=128)
    chunks = [(0, 512), (512, 512), (1024, 512), (1536, 512), (2048, 128)]
    ev = [0]

    def evac(dst, src):
        ev[0] += 1
        P.cp(dst, src, eng='act' if ev[0] % 2 else 'dve')

    for cb in range(36):
        c0 = cb * 128 if cb < 24 else 5136 + (cb - 24) * 128
        W = WB[cb % 3]
        W3 = v3(W, 16)
        P.dma(W3, w_in_v[:, :, c0:c0 + 128], q='pool')
        stg = STG[cb % 2]
        cks = chunks if cb >= 8 else [(896, 128), (1024, 512), (1536, 512), (2048, 128)]
        for (t0, n) in cks:
            ph = phalf()
            for kc in range(16):
                P.mm(ph[:, 0:n], W3[:, kc, :], hT3[:, kc, t0:t0 + n], start=(kc == 0), stop=(kc == 15))
            evac(stg[:, 3 + t0:3 + t0 + n], ph[:, 0:n])
        tb = cks[0][0]
        P.dma(P_fm[cb, :, 3 + tb:3 + NT], stg[:, 3 + tb:3 + NT])
        tlo = 0 if cb >= 8 else 1024
        cvs = CVS[cb % 2]
        P.actf(cvs[:, tlo:2048], stg[:, tlo:2048], AF.Identity, scale=CW3[:, cb, 0:1], bias=CW3[:, cb, 4:5])
        for i in range(1, 4):
            P.stt(cvs[:, tlo:2048], stg[:, tlo + i:2048 + i], CW3[:, cb, i:i + 1], cvs[:, tlo:2048], ALU.mult, ALU.add)
        P.actf(cvs[:, tlo:2048], cvs[:, tlo:2048], AF.Silu)
        P.dma(P_cv[cb, :, tlo:2048], cvs[:, tlo:2048])

    WT = [A.bf16(16 * 512) for _ in range(2)]
    TS_ = [A.f32(512) for _ in range(2)]
    tmchunks = [(3072, 512, 0, 8), (3584, 512, 512, 8), (4112, 512, 1024, 8), (4624, 512, 1536, 8),
                (4096, 16, 2048, 0), (6672, 16, 2064, 0)]
    k = 0
    for i, (c0, ncol, pc0, t_first) in enumerate(tmchunks):
        W = WT[i % 2]
        W3 = v3(W, 16)
        P.dma(W3[:, :, 0:ncol], w_in_v[:, :, c0:c0 + ncol], q='pool')
        for t in range(t_first, NTILE):
            ph = phalf()
            for kc in range(16):
                P.mm(ph[:, 0:ncol], hT3[:, kc, t * 128:(t + 1) * 128], W3[:, kc, 0:ncol],
                     start=(kc == 0), stop=(kc == 15))
            ts_ = TS_[k % 2]
            k += 1
            evac(ts_[:, 0:ncol], ph[:, 0:ncol])
            P.dma(P_tm[t * 128:(t + 1) * 128, pc0:pc0 + ncol], ts_[:, 0:ncol])

    A.top = persist_top
    MIXT = A.bf16(16 * NOUT)
    MIXT3 = v3(MIXT, 16)
    SG = A.f32(1024)
    HT = A.f32(1024)
    P.memset(SG, 0.0)
    P.memset(HT, 0.0)
    RAW = A.f32(36 * 176)
    CV = A.f32(36 * 128)
    RAWC = CV
    CV3 = v3(CV, 36)
    TM = A.f32(TMW)
    NB = 22
    bf0 = A.top
    Bf = [A.f32(1024) for _ in range(NB)]
    HST = A.at(bf0 + 17 * 1024, 4608)
    CST = HST
    MIX = A.at(bf0 + 13 * 1024, 2048)
    SMALL = A.f32(1024)
    SMALL2 = A.f32(160)
    SGS = [Bf[4], Bf[6]]
    SGO = [Bf[12], Bf[10]]
    out_dmas = []

    def sm(o, n):
        return SMALL[:, o:o + n]

    def gset(par):
        o = 80 * par
        names = (('BETA', 8), ('BETAN', 8), ('G8', 8), ('TMP8', 8), ('DT16', 16), ('A16', 16), ('TMP16', 16))
        d = {}
        for nm, n in names:
            d[nm] = SMALL2[:, o:o + n]
            o += n
        return d
    GS = [gset(0), gset(1)]

    def emit_gating(tt_):
        g_ = GS[tt_ % 2]
        TMs_ = sm(950 + 32 * (tt_ % 2), 32)
        P.dma(TMs_, P_tm[tt_ * 128:(tt_ + 1) * 128, 2048:TMW])
        BETA_, BETAN_, G8_, TMP8_, DT16_, A16_, TMP16_ = (g_[k] for k in ('BETA', 'BETAN', 'G8', 'TMP8', 'DT16', 'A16', 'TMP16'))
        P.actf(BETA_, TMs_[:, 0:8], AF.Exp, scale=-1.0)
        P.ts(BETA_, BETA_, 1.0, None, ALU.add)
        P.add('dve', lambda e: e.reciprocal(BETA_, BETA_), r=[BETA_], w=[BETA_])
        P.ts(BETAN_, BETA_, -1.0, None, ALU.mult)
        P.tt(TMP8_, TMs_[:, 8:16], GDB, ALU.add)
        P.actf(TMP8_, TMP8_, AF.Exp)
        P.actf(TMP8_, TMP8_, AF.Ln, bias=1.0)
        P.tt(G8_, TMP8_, NAG, ALU.mult)
        P.tt(TMP16_, TMs_[:, 16:32], SDB, ALU.add)
        P.actf(TMP16_, TMP16_, AF.Exp)
        P.actf(DT16_, TMP16_, AF.Ln, bias=1.0)
        P.tt(A16_, DT16_, NAS, ALU.mult)
    EGK = sm(80, 16)
    EG = sm(80, 8)
    EKD = sm(88, 8)
    BEG = sm(96, 8)
    RQ = sm(104, 8)
    RK = sm(112, 8)
    SSQ = sm(120, 8)
    EAA = sm(128, 32)
    EA = sm(128, 16)
    EAL = sm(144, 16)
    SSO = sm(160, 8)
    SS2 = sm(168, 2)
    EGL = sm(176, 128)
    EALB = sm(304, 256)
    RHE = sm(560, 256)
    SSK = sm(816, 8)

    def bh(ap, n=8, w=128):
        return ap.unsqueeze(2).to_broadcast([128, n, w])

    def bm(ap, n=8, w=128):
        return ap.unsqueeze(1).to_broadcast([128, n, w])

    DK_SCALE = 128 ** -0.5

    deferred = [None]
    for t in range(NTILE):
        ty = 's' if t == 16 else 'p'
        nseq = 16 if ty == 's' else 1
        L = 8 if ty == 's' else 128
        emit_out = t >= 8
        ot = t - 8
        CVt = CV if t % 2 == 0 else RAW[:, 0:36 * 128]
        CV3 = v3(CVt, 36)
        if ty == 'p':
            b_lo = 0 if emit_out else 8
            P.dma(CV3[:, b_lo:36, :], P_cv[b_lo:36, :, t * 128:(t + 1) * 128].rearrange("b p t -> p b t"))
            if t == 15:
                RAW3 = v3(sm(840, 108), 36)
                P.dma(RAW3, P_fm[:, :, 3 + 2045:3 + 2048].rearrange("b p t -> p b t"))
        else:
            RAW4 = RAW.rearrange("p (b s t) -> p b s t", b=36, s=16)
            P.dma(v3(RAWC, 36), P_fm[:, :, 3 + 2048:3 + NT].rearrange("b p t -> p b t"))
            P.dma(HST[0:48, 0:3072], st_gconv)
            P.dma(HST[0:48, 3072:4608], st_sconv)
            for b0 in range(0, 36, 8):
                nb = min(8, 36 - b0)
                ps = pw()
                for b in range(nb):
                    P.tr(ps[:, b * 48:(b + 1) * 48], HST[0:48, (b0 + b) * 128:(b0 + b + 1) * 128], ID[0:48, 0:48])
                P.cp(RAW4[:, b0:b0 + nb, :, 0:3],
                     ps[:, 0:nb * 48].rearrange("p (b s t) -> p b s t", b=nb, s=16), eng='act')
            P.cp(RAW4[:, :, :, 3:11], RAWC.rearrange("p (b s t) -> p b s t", b=36, s=16), eng='pool')
        if emit_out:
            P.dma(TM[:, 0:2048], P_tm[t * 128:(t + 1) * 128, 0:2048])
        cblocks = list(range(36)) if ty == 's' else []
        NPOOL = 8
        pool_blocks = cblocks[-NPOOL:]
        dve_blocks = cblocks[:-NPOOL]
        CTMP = v3(Bf[21], 8)
        if ty == 's':
            NPOOL = 8

        def cio(b):
            if ty == 'p':
                return CV3[:, b, :], [RAW3[:, b, i:i + 128] for i in range(4)], (lambda a: a)
            return (CV3[:, b, :].rearrange("p (s t) -> p s t", s=16), [RAW4[:, b, :, i:i + 8] for i in range(4)],
                    (lambda a: a.rearrange("p (s t) -> p s t", s=16)))
        for b in cblocks:
            o_, ins, _ = cio(b)
            P.actf(o_, ins[0], AF.Identity, scale=CW3[:, b, 0:1], bias=CW3[:, b, 4:5])
        for i in range(1, 4):
            for b in dve_blocks:
                o_, ins, _ = cio(b)
                P.stt(o_, ins[i], CW3[:, b, i:i + 1], o_, ALU.mult, ALU.add)
            for j, b in enumerate(pool_blocks):
                o_, ins, vw = cio(b)
                P.ts(vw(CTMP[:, j, :]), ins[i], CW3[:, b, i:i + 1], None, ALU.mult, eng='pool')
            for j, b in enumerate(pool_blocks):
                o_, ins, vw = cio(b)
                P.tt(o_, o_, vw(CTMP[:, j, :]), ALU.add, eng='pool')
        if ty == 's':
            P.actf(CV, CV, AF.Silu)
        if t == 15 or ty == 's':
            nr = 3 if ty == 'p' else 48
            if ty == 's':
                LST = A.at(bf0 + 15 * 1024, 36 * 48)
                P.cp(LST.rearrange("p (b s t) -> p b s t", b=36, s=16), RAW4[:, :, :, 8:11], eng='pool')
                LST3 = v3(LST, 36)
            for b0 in range(0, 36, 4):
                ph = phalf()
                for b in range(4):
                    src = RAW3[:, b0 + b, 0:3] if ty == 'p' else LST3[:, b0 + b, :]
                    P.tr(ph[0:nr, b * 128:(b + 1) * 128], src, ID)
                P.cp(CST[0:nr, b0 * 128:(b0 + 4) * 128], ph[0:nr, 0:512], eng='act')
            if ty == 'p':
                out_dmas.append(P.dma(gconv_p, CST[0:3, 0:3072]))
                out_dmas.append(P.dma(sconv_p, CST[0:3, 3072:4608]))
            else:
                out_dmas.append(P.dma(gconv_s, CST[0:48, 0:3072]))
                out_dmas.append(P.dma(sconv_s, CST[0:48, 3072:4608]))
        if t == 0:
            emit_gating(0)
        BETA, BETAN, G8, DT16, A16 = (GS[t % 2][k] for k in ('BETA', 'BETAN', 'G8', 'DT16', 'A16'))
        Q, SQ, KN, V, RH, DX, DM, QNT, KNT, QDT, NT0, QKM, N0, QKMT, R, NA, NB_, UTb, WTb, KDEC, VNb, OTb = Bf
        QN = Q

        def cgs(g):
            return slice(g * 512, (g + 1) * 512)

        def G(buf, g):
            return buf[:, g * 512:(g + 1) * 512]

        def G4(buf, g):
            return v3(buf[:, g * 512:(g + 1) * 512], 4)

        def h4(ap8, g):
            return bh(ap8[:, 4 * g:4 * g + 4], 4)

        def m4(ap):
            return bm(ap, 4)

        def lock(stages):
            for st in stages:
                for g in (0, 1):
                    st(g)

        pss = phalf()
        P.mm(pss[:, 0:8], UI[ty], G8)
        P.mm(pss[:, 8:16], MS[ty], G8)
        P.actf(EGK, pss[:, 0:16], AF.Exp)
        if ty == 'p':
            P.mm(pss[:, 16:24], ONES, G8)
        else:
            P.tt(RHE[:, 0:128].rearrange("p (s h) -> p s h", s=16),
                 G8.unsqueeze(1).to_broadcast([128, 16, 8]), RM.unsqueeze(2).to_broadcast([128, 16, 8]), ALU.mult)
            P.mm(pss[:, 16:16 + 128], ONES, RHE[:, 0:128])
        P.actf(EGL[:, 0:nseq * 8], pss[:, 16:16 + nseq * 8], AF.Exp)
        EGL3 = EGL[:, 0:nseq * 8].rearrange("p (s h) -> p s h", s=nseq)
        P.tt(BEG, BETA, EG, ALU.mult)

        def tr4(dst, srcs):
            ph = phalf()
            for hh in range(4):
                P.tr(ph[:, hh * 128:(hh + 1) * 128], srcs[hh], ID)
            P.cp(dst, ph, eng='act')

        def s_qT(g):
            if emit_out:
                tr4(G(Q, g), [CV3[:, 4 * g + hh, :] for hh in range(4)])

        def s_kT(g):
            tr4(G(KN, g), [CV3[:, 8 + 4 * g + hh, :] for hh in range(4)])

        def s_vT(g):
            tr4(G(V, g), [CV3[:, 16 + 4 * g + hh, :] for hh in range(4)])

        def norm_(X_, R_, S_, sc, g):
            P.tt(G(SQ, g), G(X_, g), G(X_, g), ALU.mult)
            P.red(S_[:, 4 * g:4 * g + 4], G4(SQ, g))
            r_ = R_[:, 4 * g:4 * g + 4]
            P.actf(r_, S_[:, 4 * g:4 * g + 4], AF.Ln, bias=EPS)
            P.actf(r_, r_, AF.Exp, scale=-0.5, bias=(LNDK if sc != 1.0 else None))
            P.tt(G4(X_, g), G4(X_, g), h4(R_, g), ALU.mult)

        def s_qn(g):
            if emit_out:
                norm_(Q, RQ, SSQ, DK_SCALE, g)

        def s_kn(g):
            norm_(KN, RK, SSK, 1.0, g)

        def s_decay(g):
            P.tt(G4(RH, g), m4(MS[ty]), h4(G8, g), ALU.mult)
            ph = phalf()
            P.mm(ph, UI[ty], G(RH, g))
            P.actf(G(DX, g), ph, AF.Exp)
            if emit_out:
                P.tt(G4(DM, g), G4(DX, g), m4(MI[ty]), ALU.mult)
            P.tt(G4(DX, g), G4(DX, g), m4(MS[ty]), ALU.mult)
            P.tt(G4(DX, g), G4(DX, g), h4(BETAN, g), ALU.mult)

        def s_qnT(g):
            if emit_out:
                tr4(G(QNT, g), [QN[:, (4 * g + hh) * 128:(4 * g + hh + 1) * 128] for hh in range(4)])

        def s_knT(g):
            tr4(G(KNT, g), [KN[:, (4 * g + hh) * 128:(4 * g + hh + 1) * 128] for hh in range(4)])

        def s_qdT(g):
            if emit_out:
                P.tt(G4(SQ, g), G4(QN, g), h4(EG, g), ALU.mult)
                tr4(G(QDT, g), [SQ[:, (4 * g + hh) * 128:(4 * g + hh + 1) * 128] for hh in range(4)])

        def mm4(lh, rh, g):
            ph = phalf()
            for hh in range(4):
                sl = slice((4 * g + hh) * 128, (4 * g + hh + 1) * 128)
                P.mm(ph[:, hh * 128:(hh + 1) * 128], lh[:, sl], rh[:, sl])
            return ph

        def s_gram(g):
            ph = mm4(KNT, KNT, g)
            P.tt(G(NT0, g), ph, G(DX, g), ALU.mult)

        def s_qk(g):
            if emit_out:
                ph = mm4(QNT, KNT, g)
                P.tt(G(QKM, g), ph, G(DM, g), ALU.mult)

        def s_n0(g):
            tr4(G(N0, g), [NT0[:, (4 * g + hh) * 128:(4 * g + hh + 1) * 128] for hh in range(4)])
            P.tt(G4(R, g), G4(N0, g), m4(ID), ALU.add)

        def s_qkT(g):
            if emit_out:
                tr4(G(QKMT, g), [QKM[:, (4 * g + hh) * 128:(4 * g + hh + 1) * 128] for hh in range(4)])

        lock([s_kT, s_vT, s_qT, s_kn])
        if deferred[0] is not None:
            deferred[0]()
            deferred[0] = None
        lock([s_decay, s_qn, s_knT, s_gram, s_qnT, s_n0, s_qdT, s_qk, s_qkT])
        if t + 1 < NTILE:
            emit_gating(t + 1)

        nsteps = 6 if ty == 'p' else 2
        Nc, NTc = N0, NT0
        pp = [(NA, NB_), (RH, DM)]
        for s_ in range(1, nsteps + 1):
            Nn, NTn = pp[s_ % 2]

            def d_sq(g, Nc=Nc, NTc=NTc, Nn=Nn, NTn=NTn, s_=s_):
                psb = mm4(Nc, NTc, g)
                P.cp(G(NTn, g), psb, eng='act')
                if s_ < nsteps:
                    psa = mm4(NTc, Nc, g)
                    P.cp(G(Nn, g), psa, eng='dve')

            def d_r(g, NTn=NTn):
                psc = mm4(NTn, R, g)
                P.tt(G(R, g), G(R, g), psc, ALU.add)

            lock([d_sq, d_r])
            Nc, NTc = Nn, NTn

        VBb = NA
        KBG = NB_
        VZ = UTb

        def s_prep(g):
            P.tt(G4(VBb, g), G4(V, g), h4(BETA, g), ALU.mult)
            P.tt(G4(KBG, g), G4(KN, g), h4(BEG, g), ALU.mult)
            P.tt(G4(KDEC, g), G4(KN, g), h4(EKD, g), ALU.mult)

        def s_u(g):
            ph = mm4(VBb, R, g)
            P.cp(G(UTb, g), ph, eng='act')

        def s_w(g):
            ph = mm4(KBG, R, g)
            P.cp(G(WTb, g), ph, eng='act')

        VNT = SQ

        def load_S(g, s, bufs):
            S_s = bufs[s % len(bufs)]
            P.dma(G4(S_s, g), st_g[s, 4 * g:4 * g + 4].rearrange("h k v -> k h v"), q='pool')
            return S_s

        def s_scanA(g):
            psA = phalf()
            psB = phalf() if emit_out else None
            for s in range(nseq):
                S_s = SG if ty == 'p' else load_S(g, s, [RH, DM, N0, NT0])
                for hh in range(4):
                    h = 4 * g + hh
                    P.mm(psA[:, hh * 128 + s * L:hh * 128 + (s + 1) * L], S_s[:, h * 128:(h + 1) * 128],
                         WTb[:, h * 128 + s * L:h * 128 + (s + 1) * L])
                if emit_out:
                    for hh in range(4):
                        h = 4 * g + hh
                        P.mm(psB[:, hh * 128 + s * L:hh * 128 + (s + 1) * L], S_s[:, h * 128:(h + 1) * 128],
                             QDT[:, h * 128 + s * L:h * 128 + (s + 1) * L])
            P.tt(G(VNT, g), G(UTb, g), psA, ALU.subtract)
            if emit_out:
                P.cp(G(OTb, g), psB, eng='act')

        def s_vn(g):
            tr4(G(VNb, g), [VNT[:, (4 * g + hh) * 128:(4 * g + hh + 1) * 128] for hh in range(4)])

        def s_c(g):
            if emit_out:
                psC = mm4(VNb, QKMT, g)
                P.tt(G(OTb, g), G(OTb, g), psC, ALU.add)

        def s_upd(g):
            for s in range(nseq):
                if ty == 'p':
                    S_s, VZs, Sn = SG, VNb, SG
                else:
                    S_s = load_S(g, s, [RH, DM, QNT, KNT])
                    VZs = [UTb, WTb][s % 2]
                    P.ts(G(VZs, g), G(VNb, g), RM[:, s:s + 1], None, ALU.mult)
                    Sn = [N0, NT0, QDT, QKM][s % 4]
                psn = mm4(KDEC, VZs, g)
                P.tt(G4(Sn, g), G4(S_s, g), bh(EGL3[:, s, 4 * g:4 * g + 4], 4), ALU.mult)
                P.tt(G(Sn, g), G(Sn, g), psn, ALU.add)
                if ty == 's':
                    out_dmas.append(P.dma(gst_s[s, 4 * g:4 * g + 4].rearrange("h k v -> k h v"), G4(Sn, g)))
            if t == 7:
                P.ts(G(SG, g), G(SG, g), FLAG, None, ALU.mult)
            if t == 15:
                out_dmas.append(P.dma(gst_p[4 * g:4 * g + 4].rearrange("h k v -> k h v"), G4(SG, g)))

        O = Q

        def s_out(g):
            if not emit_out:
                return
            ph = phalf()
            for hh in range(4):
                h = 4 * g + hh
                P.tr(ph[:, hh * 128:(hh + 1) * 128], OTb[:, h * 128:(h + 1) * 128], ID)
            P.cp(G(O, g), ph, eng='act')
            P.tt(G(SQ, g), G(O, g), G(O, g), ALU.mult)
            so = SSO[:, 4 * g:4 * g + 4]
            P.red(so, G4(SQ, g))
            P.actf(so, so, AF.Ln, scale=1.0 / 128, bias=EPS)
            P.actf(so, so, AF.Exp, scale=-0.5)
            P.tt(G4(O, g), G4(O, g), h4(SSO, g), ALU.mult)
            P.tt(G4(O, g), G4(O, g), m4(GNW), ALU.mult)
            P.tt(MIX[:, g * 512:(g + 1) * 512], G(O, g), TM[:, g * 512:(g + 1) * 512], ALU.mult)

        if emit_out:
            P.actf(TM[:, 0:2048], TM[:, 0:2048], AF.Silu)
        lock([s_prep, s_u, s_w, s_scanA, s_vn, s_c, s_upd, s_out])

        XSb, XDT, XW, DXT0, DXT1, Yb, T1a, YOT, HTS, HTN, XZ = Bf[0:11]
        BTM = Bf[11][:, 0:256]
        CBM = Bf[11][:, 256:512]
        HNAT = Bf[12]
        T1b = Bf[15]
        T1 = [T1a, T1b]
        DXT = [DXT0, DXT1]
        if ty == 's':
            nbanks[0] = 6
            PSO = [PW[3][:, 0:512], PW[3][:, 512:1024]]
        else:
            PSO = [None, None]

        def G8h(buf, g):
            return v3(buf[:, g * 512:(g + 1) * 512], 8)

        def h8(ap16, g):
            return bh(ap16[:, 8 * g:8 * g + 8], 8, 64)

        psb_ = phalf()
        for g in range(2):
            P.tr(psb_[:, g * 128:(g + 1) * 128], CV3[:, 32 + g, :], ID)
        P.cp(BTM, psb_[:, 0:256], eng='act')
        pss = phalf()
        P.mm(pss[:, 0:16], UI[ty], A16)
        P.mm(pss[:, 16:32], MS[ty], A16)
        P.actf(EAA, pss[:, 0:32], AF.Exp)
        if ty == 'p':
            P.mm(pss[:, 32:48], ONES, A16)
        else:
            P.tt(RHE.rearrange("p (s h) -> p s h", s=16),
                 A16.unsqueeze(1).to_broadcast([128, 16, 16]), RM.unsqueeze(2).to_broadcast([128, 16, 16]), ALU.mult)
            P.mm(pss[:, 32:32 + 256], ONES, RHE)
        P.actf(EALB[:, 0:nseq * 16], pss[:, 32:32 + nseq * 16], AF.Exp)
        EALB3 = EALB[:, 0:nseq * 16].rearrange("p (s h) -> p s h", s=nseq)

        def t_xs(g):
            tr4(G(XSb, g), [CV3[:, 24 + 4 * g + bb, :] for bb in range(4)])

        def t_decay(g):
            if not emit_out:
                return
            P.tt(v3(T1[g], 8), bm(UI[ty]), bh(A16[:, g * 8:(g + 1) * 8]), ALU.mult)
            for hf in range(2):
                ph = phalf()
                P.mm(ph, MS[ty], T1[g][:, hf * 512:(hf + 1) * 512])
                P.actf(DXT[g][:, hf * 512:(hf + 1) * 512], ph, AF.Exp)

        def t_cb(g):
            if not emit_out:
                return
            ph = phalf()
            P.mm(ph[:, 0:128], CV3[:, 32 + g, :], CV3[:, 34 + g, :])
            cbm = CBM[:, g * 128:(g + 1) * 128]
            P.tt(cbm, ph[:, 0:128], UI[ty], ALU.mult)
            for hf in range(2):
                d_ = v3(DXT[g][:, hf * 512:(hf + 1) * 512], 4)
                P.tt(d_, d_, bm(cbm, 4), ALU.mult)

        def t_xdt(g):
            P.tt(G8h(XDT, g), G8h(XSb, g), h8(DT16, g), ALU.mult)
            P.tt(G8h(XW, g), G8h(XDT, g), h8(EAL, g), ALU.mult)

        def t_ydiag(g):
            if not emit_out:
                return
            ph = phalf()
            for hh in range(8):
                h = 8 * g + hh
                P.mm(ph[:, hh * 64:(hh + 1) * 64], DXT[g][:, hh * 128:(hh + 1) * 128], XDT[:, h * 64:(h + 1) * 64])
            P.cp(G(Yb, g), ph, eng='act')

        def t_state(g):
            pso = PSO[g] if ty == 's' else (phalf() if emit_out else None)
            for s in range(nseq):
                if ty == 'p':
                    H_s = HT
                else:
                    hin = [Bf[12], Bf[16]][s % 2]
                    P.dma(v3(G(hin, g), 4),
                          st_s[s, 8 * g:8 * g + 8].rearrange("(b h2) p n -> (h2 p) b n", h2=2), q='pool')
                    H_s = [Bf[8], Bf[19]][s % 2]
                    tr4(G(H_s, g), [hin[:, (4 * g + bb) * 128:(4 * g + bb + 1) * 128] for bb in range(4)])
                if emit_out:
                    for bb in range(4):
                        b = 4 * g + bb
                        P.mm(pso[:, bb * 128 + s * L:bb * 128 + (s + 1) * L], H_s[:, b * 128:(b + 1) * 128],
                             CV3[:, 34 + g, s * L:(s + 1) * L])
                if ty == 'p':
                    XZs, Hn = XW, HT
                else:
                    XZs, Hn = [Bf[10], Bf[21]][s % 2], [Bf[9], Bf[20]][s % 2]
                    P.ts(G(XZs, g), G(XW, g), RM[:, s:s + 1], None, ALU.mult)
                psn = phalf()
                P.mm(psn, BTM[:, g * 128:(g + 1) * 128], G(XZs, g))
                P.tt(G8h(Hn, g), G8h(H_s, g), bh(EALB3[:, s, 8 * g:8 * g + 8], 8, 64), ALU.mult)
                P.tt(G(Hn, g), G(Hn, g), psn, ALU.add)
                if ty == 's':
                    hout = [Bf[17], Bf[18]][s % 2]
                    tr4(G(hout, g), [Hn[:, (4 * g + bb) * 128:(4 * g + bb + 1) * 128] for bb in range(4)])
                    out_dmas.append(P.dma(sst_s[s, 8 * g:8 * g + 8].rearrange("(b h2) p n -> (h2 p) b n", h2=2),
                                          v3(G(hout, g), 4)))
            if emit_out:
                P.cp(G(YOT, g), pso, eng='act')
            if t == 7:
                P.ts(G(HT, g), G(HT, g), FLAG, None, ALU.mult)
            if t == 15:
                tr4(G(HNAT, g), [HT[:, (4 * g + bb) * 128:(4 * g + bb + 1) * 128] for bb in range(4)])
                out_dmas.append(P.dma(sst_p[8 * g:8 * g + 8].rearrange("(b h2) p n -> (h2 p) b n", h2=2),
                                      v3(G(HNAT, g), 4)))

        def t_y(g):
            if not emit_out:
                return
            ph = phalf()
            for bb in range(4):
                b = 4 * g + bb
                P.tr(ph[:, bb * 128:(bb + 1) * 128], YOT[:, b * 128:(b + 1) * 128], ID)
            Tg = T1[g][:, 0:512]
            T8 = v3(Tg, 8)
            P.tt(T8, v3(ph, 8), h8(EA, g), ALU.mult)
            P.tt(G(Yb, g), G(Yb, g), Tg, ALU.add)
            P.tt(T8, G8h(XSb, g), h8(SD, g), ALU.mult)
            P.tt(G(Yb, g), G(Yb, g), Tg, ALU.add)
            P.tt(G(Yb, g), G(Yb, g), TM[:, 1024 + g * 512:1024 + (g + 1) * 512], ALU.mult)
            P.tt(Tg, G(Yb, g), G(Yb, g), ALU.mult)
            s2 = SS2[:, g:g + 1]
            P.red(s2, Tg)
            P.actf(s2, s2, AF.Ln, scale=1.0 / 512, bias=EPS)
            P.actf(s2, s2, AF.Exp, scale=-0.5)
            P.ts(G(Yb, g), G(Yb, g), s2, None, ALU.mult)
            P.tt(MIX[:, 1024 + g * 512:1024 + (g + 1) * 512], G(Yb, g), SNW[:, g * 512:(g + 1) * 512], ALU.mult)

        lock([t_xs, t_decay, t_xdt, t_cb, t_ydiag, t_state, t_y])
        if emit_out:
            def mixT(ot=ot):
                for kq in range(4):
                    ph = phalf()
                    for j in range(4):
                        kc = kq * 4 + j
                        P.tr(ph[:, j * 128:(j + 1) * 128], MIX[:, kc * 128:(kc + 1) * 128], ID)
                    evac(MIXT3[:, kq * 4:kq * 4 + 4, ot * 128:(ot + 1) * 128], v3(ph, 4))
            if t + 1 < NTILE:
                deferred[0] = mixT
            else:
                mixT()

    nbanks[0] = 8
    A.top = persist_top + 8 * NOUT
    WOC = [A.bf16(16 * 512) for _ in range(2)]
    r1_end = A.top
    X1 = A.f32(9 * 2048)
    X13 = v3(X1, 9)
    H2T = A.bf16(16 * NOUT)
    H2T3 = v3(H2T, 16)
    XN2 = A.f32(2048)
    SQJ2 = A.bf16(2048)
    SSc = A.f32(1)
    RSc = A.f32(1)
    w_out_v = w_out.rearrange("(kc p) c -> p kc c", p=128)
    for ot in range(9):
        P.dma(X13[:, ot, :], xin[1024 + ot * 128:1024 + (ot + 1) * 128, :])
    for dq in range(4):
        woc = v3(WOC[dq % 2], 16)
        P.dma(woc, w_out_v[:, :, dq * 512:(dq + 1) * 512], q='pool')
        for ot in range(9):
            ph = phalf()
            for kc in range(16):
                P.mm(ph, MIXT3[:, kc, ot * 128:(ot + 1) * 128], woc[:, kc, :], start=(kc == 0), stop=(kc == 15))
            xs_ = X13[:, ot, dq * 512:(dq + 1) * 512]
            P.tt(xs_, xs_, ph, ALU.add)
    for ot in range(9):
        rms_to_hT(X13[:, ot, :], FNW, H2T3[:, :, ot * 128:(ot + 1) * 128], SQJ2, XN2, SSc, RSc)
    A.top = persist_top
    NFB = 4
    FFT = A.bf16(NFB * NOUT)
    FFT3 = v3(FFT, NFB)
    WG = [A.bf16(16 * 128) for _ in range(2)]
    WU = [A.bf16(16 * 128) for _ in range(2)]
    WD = [A.bf16(NFB * 2048) for _ in range(2)]
    SIL = [A.f32(512) for _ in range(2)]
    assert A.top <= r1_end, (A.top, r1_end)
    w_gate_v = w_gate.rearrange("(kc p) c -> p kc c", p=128)
    w_up_v = w_up.rearrange("(kc p) c -> p kc c", p=128)
    tchunks = [(0, 512), (512, 512), (1024, 128)]
    groups = []
    b0 = 0
    while b0 < 44:
        nb = min(NFB, 44 - b0)
        groups.append((b0, nb))
        b0 += nb
    cnt = 0
    for gi, (b0, nb) in enumerate(groups):
        wd = WD[gi % 2]
        wd3 = v3(wd, NFB)
        P.dma(wd3[:, 0:nb, :], w_down[b0 * 128:(b0 + nb) * 128, :].rearrange("(b p) d -> p b d", p=128), q='pool')
        for j in range(nb):
            fb = b0 + j
            wg = v3(WG[fb % 2], 16)
            wu = v3(WU[fb % 2], 16)
            P.dma(wg, w_gate_v[:, :, fb * 128:(fb + 1) * 128], q='pool')
            P.dma(wu, w_up_v[:, :, fb * 128:(fb + 1) * 128], q='pool')
            for (t0, n) in tchunks:
                pg = phalf()
                for kc in range(16):
                    P.mm(pg[:, 0:n], wg[:, kc, :], H2T3[:, kc, t0:t0 + n], start=(kc == 0), stop=(kc == 15))
                pu = phalf()
                for kc in range(16):
                    P.mm(pu[:, 0:n], wu[:, kc, :], H2T3[:, kc, t0:t0 + n], start=(kc == 0), stop=(kc == 15))
                sl_ = SIL[cnt % 2]
                cnt += 1
                P.actf(sl_[:, 0:n], pg[:, 0:n], AF.Silu)
                P.tt(FFT3[:, j, t0:t0 + n], sl_[:, 0:n], pu[:, 0:n], ALU.mult)
        for ot in range(9):
            for dq in range(4):
                ph = phalf()
                for j in range(nb):
                    P.mm(ph, FFT3[:, j, ot * 128:(ot + 1) * 128], wd3[:, j, dq * 512:(dq + 1) * 512],
                         start=(j == 0), stop=(j == nb - 1))
                xs_ = X13[:, ot, dq * 512:(dq + 1) * 512]
                P.tt(xs_, xs_, ph, ALU.add)
    YB = [WD[0].bitcast(F32)[:, 0:2048], WD[1].bitcast(F32)[:, 0:2048]]
    for ot in range(9):
        xt = X13[:, ot, :]
        yb = YB[ot % 2]
        P.actf(XN2, xt, AF.Square, accum=SSc)
        P.actf(RSc, SSc, AF.Sqrt, scale=1.0 / D, bias=EPS)
        P.add('dve', lambda e: e.reciprocal(RSc, RSc), r=[RSc], w=[RSc])
        P.stt(yb, xt, RSc, FINW, ALU.mult, ALU.mult)
        out_dmas.append(P.dma(y_o[ot * 128:(ot + 1) * 128, :], yb))
    stats = P.emit(final_wait_ops=out_dmas)
    es.close()
    return nc, stats


_CACHE = {}


def kernel(x_prompt, x_sample, state_gdn_conv, state_gdn, state_ssm_conv, state_ssm,
           attn_norm_w, w_in, gdn_conv_w, gdn_A_log, gdn_dt_bias, gdn_norm_w,
           ssm_conv_w, ssm_conv_b, ssm_A_log, ssm_dt_bias, ssm_D, ssm_norm_w,
           w_out, ffn_norm_w, w_gate, w_up, w_down, final_norm_w):
    f = lambda a: np.ascontiguousarray(np.asarray(a, dtype=np.float32))
    x_prompt = f(x_prompt)
    x_sample = f(x_sample)
    if 'nc' not in _CACHE:
        _CACHE['nc'] = build_program()[0]
    nc = _CACHE['nc']
    shared = dict(
        attn_norm_w=f(attn_norm_w).reshape(16, 128), w_in=f(w_in)[0], gdn_conv_w=f(gdn_conv_w)[0],
        gdn_A_log=f(gdn_A_log).reshape(1, 8), gdn_dt_bias=f(gdn_dt_bias).reshape(1, 8),
        gdn_norm_w=f(gdn_norm_w).reshape(1, 128), ssm_conv_w=f(ssm_conv_w)[0],
        ssm_conv_b=f(ssm_conv_b).reshape(1, 1536), ssm_A_log=f(ssm_A_log).reshape(1, 16),
        ssm_dt_bias=f(ssm_dt_bias).reshape(1, 16), ssm_D=f(ssm_D).reshape(1, 16),
        ssm_norm_w=f(ssm_norm_w).reshape(1, 1024), w_out=f(w_out)[0],
        ffn_norm_w=f(ffn_norm_w).reshape(16, 128), w_gate=f(w_gate)[0], w_up=f(w_up)[0],
        w_down=f(w_down)[0], final_norm_w=f(final_norm_w).reshape(1, D))
    sgc = f(state_gdn_conv)[0]
    sg = f(state_gdn)[0]
    ssc = f(state_ssm_conv)[0]
    ssm = f(state_ssm)[0]
    in_maps = []
    for c in range(8):
        b, r = c // 2, c % 2
        xin = np.zeros((NT, D), np.float32)
        if r == 1:
            xin[0:1024] = x_prompt[b, 0:1024]
        xin[1024:2048] = x_prompt[b, r * 1024:(r + 1) * 1024]
        xin[2048:] = x_sample[16 * c:16 * c + 16].reshape(128, D)
        m = dict(shared)
        m.update(xin=xin, flag=np.full((128, 1), float(r), np.float32),
                 st_gconv=np.ascontiguousarray(sgc[16 * c:16 * c + 16].reshape(48, 3072)),
                 st_g=np.ascontiguousarray(sg[16 * c:16 * c + 16]),
                 st_sconv=np.ascontiguousarray(ssc[16 * c:16 * c + 16].reshape(48, 1536)),
                 st_s=np.ascontiguousarray(ssm[16 * c:16 * c + 16]))
        in_maps.append(m)
    res = run_bass_kernel_spmd(nc, in_maps, core_ids=list(range(8)))
    R = res.results
    y_prompt = np.zeros((4, 2048, D), np.float32)
    y_sample = np.zeros((128, 8, D), np.float32)
    gconv_p = np.zeros((1, 4, 3, 3072), np.float32)
    gst_p = np.zeros((1, 4, 8, 128, 128), np.float32)
    sconv_p = np.zeros((1, 4, 3, 1536), np.float32)
    sst_p = np.zeros((1, 4, 16, 64, 128), np.float32)
    gconv_s = np.zeros((1, 128, 3, 3072), np.float32)
    gst_s = np.zeros((1, 128, 8, 128, 128), np.float32)
    sconv_s = np.zeros((1, 128, 3, 1536), np.float32)
    sst_s = np.zeros((1, 128, 16, 64, 128), np.float32)
    for c in range(8):
        b, r = c // 2, c % 2
        o = R[c]
        y_prompt[b, r * 1024:(r + 1) * 1024] = o['y'][0:1024]
        y_sample[16 * c:16 * c + 16] = o['y'][1024:].reshape(16, 8, D)
        if r == 1:
            gconv_p[0, b] = o['gconv_p']
            gst_p[0, b] = o['gst_p']
            sconv_p[0, b] = o['sconv_p']
            sst_p[0, b] = o['sst_p']
        gconv_s[0, 16 * c:16 * c + 16] = o['gconv_s'].reshape(16, 3, 3072)
        gst_s[0, 16 * c:16 * c + 16] = o['gst_s']
        sconv_s[0, 16 * c:16 * c + 16] = o['sconv_s'].reshape(16, 3, 1536)
        sst_s[0, 16 * c:16 * c + 16] = o['sst_s']
    return (y_prompt, y_sample, gconv_p, gst_p, sconv_p, sst_p, gconv_s, gst_s, sconv_s, sst_s)
```
